# Optimizing a Trainium2 kernel written in Bass

```python
import math
import jax, jax.numpy as jnp
from jax import lax
import numpy as np

D_MODEL = 1024
BATCH = 8
SEQ = 2048
DEPTH = 2
DEC_BATCH = 128
DEC_SEQ = 8
PAST_LEN = 16384
PAGE_SIZE = 128

GROUP_W = D_MODEL // 4
MIX_W = 4 * GROUP_W
RET_HEADS = 4
RET_DK = GROUP_W // RET_HEADS
RET_DV = GROUP_W // RET_HEADS
RET_CHUNK = 64
ROPE_BASE = 10000.0
HG_HEADS = 4
HG_DK = GROUP_W // HG_HEADS
HG_DV = GROUP_W // HG_HEADS
HG_CHUNK = 16
HG_MIN_FORGET = 1e-30
SSD_HEADS = 4
SSD_HEADDIM = GROUP_W // SSD_HEADS
SSD_GROUPS = 2
SSD_STATE = 128
SSD_CONV = 4
SSD_CHUNK = 128
SSD_CONV_DIM = GROUP_W + 2 * SSD_GROUPS * SSD_STATE
S5_CH = 16
S5_GROUPS = GROUP_W // S5_CH
S5_STATE = 64
D_FF = 4 * D_MODEL
IN_COLS = 8 * GROUP_W + GROUP_W + SSD_CONV_DIM + SSD_HEADS + GROUP_W
EPS = 1e-6

kernel_name = 'hybrid_retention_hgrn2_ssd_s5_step'


def rmsnorm(x, w):
    xf = x.astype(jnp.float32)
    y = xf * lax.rsqrt(jnp.mean(xf * xf, axis=-1, keepdims=True) + EPS)
    return (y * w.astype(jnp.float32)).astype(x.dtype)


def _chunk_len(L, chunk):
    return L if L <= chunk else math.gcd(L, chunk)


def chunked_scalar_decay(q, k, v, log_a, s0, chunk):
    f32 = jnp.float32
    B, L, H, K = q.shape
    V = v.shape[-1]
    c = _chunk_len(L, chunk)
    n = L // c
    qc = q.astype(f32).reshape(B, n, c, H, K)
    kc = k.astype(f32).reshape(B, n, c, H, K)
    vc = v.astype(f32).reshape(B, n, c, H, V)
    b = jnp.cumsum(log_a.astype(f32).reshape(B, n, c, H), axis=2).transpose(0, 1, 3, 2)
    causal = jnp.tril(jnp.ones((c, c), dtype=bool))
    diff = b[..., :, None] - b[..., None, :]
    decay = jnp.where(causal, jnp.exp(jnp.where(causal, diff, 0.0)), 0.0)
    scores = jnp.einsum('bnthk,bnshk->bnhts', qc, kc) * decay
    o_intra = jnp.einsum('bnhts,bnshv->bnthv', scores, vc)
    b_last = b[..., -1]
    ds = jnp.einsum('bnhs,bnshk,bnshv->bnhkv', jnp.exp(b_last[..., None] - b), kc, vc)
    a_chunk = jnp.exp(b_last)

    def step(s, inp):
        a, d = inp
        return a[..., None, None] * s + d, s

    s_fin, s_prev = lax.scan(step, s0.astype(f32), (jnp.moveaxis(a_chunk, 1, 0), jnp.moveaxis(ds, 1, 0)))
    s_prev = jnp.moveaxis(s_prev, 0, 1)
    q_dec = qc * jnp.exp(b).transpose(0, 1, 3, 2)[..., None]
    o_cross = jnp.einsum('bnthk,bnhkv->bnthv', q_dec, s_prev)
    return (o_intra + o_cross).reshape(B, L, H, V), s_fin


def chunked_vector_decay(q, k, v, log_f, s0, chunk):
    f32 = jnp.float32
    B, L, H, K = q.shape
    V = v.shape[-1]
    c = _chunk_len(L, chunk)
    n = L // c
    qc = q.astype(f32).reshape(B, n, c, H, K)
    kc = k.astype(f32).reshape(B, n, c, H, K)
    vc = v.astype(f32).reshape(B, n, c, H, V)
    b = jnp.cumsum(log_f.astype(f32).reshape(B, n, c, H, K), axis=2)
    causal = jnp.tril(jnp.ones((c, c), dtype=bool))[None, None, :, :, None, None]
    diff = b[:, :, :, None] - b[:, :, None]
    decay = jnp.where(causal, jnp.exp(jnp.where(causal, diff, 0.0)), 0.0)
    scores = jnp.einsum('bnthk,bnshk,bntshk->bnhts', qc, kc, decay)
    o_intra = jnp.einsum('bnhts,bnshv->bnthv', scores, vc)
    b_last = b[:, :, -1]
    ds = jnp.einsum('bnshk,bnshv->bnhkv', kc * jnp.exp(b_last[:, :, None] - b), vc)
    a_chunk = jnp.exp(b_last)

    def step(s, inp):
        a, d = inp
        return a[..., None] * s + d, s

    s_fin, s_prev = lax.scan(step, s0.astype(f32), (jnp.moveaxis(a_chunk, 1, 0), jnp.moveaxis(ds, 1, 0)))
    s_prev = jnp.moveaxis(s_prev, 0, 1)
    o_cross = jnp.einsum('bnthk,bnhkv->bnthv', qc * jnp.exp(b), s_prev)
    return (o_intra + o_cross).reshape(B, L, H, V), s_fin


def rotate_every_two(x, pos0):
    L, d = x.shape[1], x.shape[-1]
    angle = 1.0 / (ROPE_BASE ** jnp.linspace(0.0, 1.0, d // 2, dtype=jnp.float32))
    pos = (pos0 + jnp.arange(L)).astype(jnp.float32)
    theta = pos[:, None] * angle[None, :]
    cos = jnp.cos(theta)[None, :, None, :]
    sin = jnp.sin(theta)[None, :, None, :]
    xf = x.astype(jnp.float32).reshape(x.shape[:-1] + (d // 2, 2))
    x1, x2 = xf[..., 0], xf[..., 1]
    return jnp.stack([x1 * cos - x2 * sin, x1 * sin + x2 * cos], axis=-1).reshape(x.shape)


def retention(q, k, v, g, s0, pos0):
    B, L, _ = q.shape
    qh = rotate_every_two(q.reshape(B, L, RET_HEADS, RET_DK), pos0)
    kh = rotate_every_two(k.reshape(B, L, RET_HEADS, RET_DK), pos0) * (RET_DK ** -0.5)
    vh = v.reshape(B, L, RET_HEADS, RET_DV)
    log_gamma = jnp.log(1.0 - jnp.exp2(-5.0 - jnp.arange(RET_HEADS, dtype=jnp.float32)))
    log_a = jnp.broadcast_to(log_gamma, (B, L, RET_HEADS))
    o, s = chunked_scalar_decay(qh, kh, vh, log_a, s0, RET_CHUNK)
    mu = jnp.mean(o, axis=-1, keepdims=True)
    var = jnp.mean(jnp.square(o - mu), axis=-1, keepdims=True)
    o = (o - mu) * lax.rsqrt(var + EPS)
    return o.reshape(B, L, GROUP_W) * jax.nn.silu(g.astype(jnp.float32)), s


def hgrn2(q, f, i, g, s0, lower):
    B, L, _ = q.shape
    ff = f.astype(jnp.float32)
    forget = lower + (1.0 - lower) * jax.nn.sigmoid(ff)
    log_forget = jnp.log(jnp.maximum(forget, HG_MIN_FORGET))
    key = (1.0 - lower) * jax.nn.sigmoid(-ff)
    qh = jax.nn.silu(q.astype(jnp.float32)).reshape(B, L, HG_HEADS, HG_DK)
    o, s = chunked_vector_decay(qh, key.reshape(B, L, HG_HEADS, HG_DK),
                                i.reshape(B, L, HG_HEADS, HG_DV),
                                log_forget.reshape(B, L, HG_HEADS, HG_DK), s0, HG_CHUNK)
    o = o * lax.rsqrt(jnp.mean(o * o, axis=-1, keepdims=True) + EPS)
    return o.reshape(B, L, GROUP_W) * jax.nn.silu(g.astype(jnp.float32)), s


def ssd_mixer(z, xbc, dt, conv_buf, s0, conv_w, conv_b, dt_bias, a_log, d_skip, norm_w):
    B, L, _ = xbc.shape
    f32 = jnp.float32
    xpad = jnp.concatenate([conv_buf.astype(xbc.dtype), xbc], axis=1)
    new_buf = xpad[:, -(SSD_CONV - 1):]
    conv = lax.conv_general_dilated(xpad, conv_w[:, None, :].astype(xbc.dtype), window_strides=(1,),
                                    padding='VALID', dimension_numbers=('NWC', 'WIO', 'NWC'),
                                    feature_group_count=SSD_CONV_DIM)
    xbc = jax.nn.silu(conv.astype(f32) + conv_b.astype(f32))
    xs, bm, cm = jnp.split(xbc, [GROUP_W, GROUP_W + SSD_GROUPS * SSD_STATE], axis=-1)
    xs = xs.reshape(B, L, SSD_HEADS, SSD_HEADDIM)
    rep = SSD_HEADS // SSD_GROUPS
    bm = jnp.repeat(bm.reshape(B, L, SSD_GROUPS, SSD_STATE), rep, axis=2)
    cm = jnp.repeat(cm.reshape(B, L, SSD_GROUPS, SSD_STATE), rep, axis=2)
    delta = jax.nn.softplus(dt.astype(f32) + dt_bias.astype(f32))
    log_a = delta * (-jnp.exp(a_log.astype(f32)))
    y, s = chunked_scalar_decay(cm, bm * delta[..., None], xs, log_a, s0, SSD_CHUNK)
    y = y + d_skip.astype(f32)[:, None] * xs
    y = (y.reshape(B, L, GROUP_W) * jax.nn.silu(z.astype(f32))).reshape(B, L, SSD_GROUPS, -1)
    y = y * lax.rsqrt(jnp.mean(y * y, axis=-1, keepdims=True) + EPS)
    return y.reshape(B, L, GROUP_W) * norm_w.astype(f32), s, new_buf


def s5_mixer(u, s0_re, s0_im, a_re, a_im, log_dt, b_re, b_im, c_re, c_im, d_skip, glu_w, glu_b):
    B, L, _ = u.shape
    f32 = jnp.float32
    uf = u.astype(f32).reshape(B, L, S5_GROUPS, S5_CH)
    lam = lax.complex(a_re.astype(f32), a_im.astype(f32))
    step = jnp.exp(log_dt.astype(f32))[:, None]
    a_bar = jnp.exp(lam * step)
    b_bar = ((a_bar - 1.0) / lam)[..., None] * lax.complex(b_re.astype(f32), b_im.astype(f32))
    bu = jnp.einsum('gph,blgh->blgp', b_bar, uf)
    s0 = lax.complex(s0_re.astype(f32), s0_im.astype(f32))
    bu = bu.at[:, 0].add(a_bar * s0)
    a_seq = jnp.broadcast_to(a_bar, bu.shape)

    def combine(e1, e2):
        a1, x1 = e1
        a2, x2 = e2
        return a2 * a1, a2 * x1 + x2

    _, states = lax.associative_scan(combine, (a_seq, bu), axis=1)
    cmat = lax.complex(c_re.astype(f32), c_im.astype(f32))
    y = jnp.einsum('ghp,blgp->blgh', cmat, states).real + d_skip.astype(f32).reshape(S5_GROUPS, S5_CH) * uf
    y = jax.nn.gelu(y.reshape(B, L, GROUP_W))
    y = y * jax.nn.sigmoid(y @ glu_w.astype(f32) + glu_b.astype(f32))
    s_fin = states[:, -1]
    return y, jnp.real(s_fin), jnp.imag(s_fin)


def run_trunk(x, pos0, st_ret, st_hg, st_ssd, st_conv, st_s5_re, st_s5_im,
              norm_mix, w_in, w_out, hg_lb_logits, ssd_conv_w, ssd_conv_b, ssd_dt_bias,
              ssd_a_log, ssd_d, ssd_norm, s5_a_re, s5_a_im, s5_log_dt, s5_b_re, s5_b_im,
              s5_c_re, s5_c_im, s5_d, s5_glu_w, s5_glu_b, norm_mlp, w_up, w_down, norm_final):
    sizes = [GROUP_W] * 8 + [GROUP_W, SSD_CONV_DIM, SSD_HEADS, GROUP_W]
    cuts = [int(c) for c in np.cumsum(sizes)[:-1]]
    sm = jax.nn.softmax(hg_lb_logits.astype(jnp.float32), axis=0)
    lower = jnp.cumsum(sm, axis=0) - sm[0]
    new = [[] for _ in range(6)]
    for l in range(DEPTH):
        h = rmsnorm(x, norm_mix[l])
        (rq, rk, rv, rg, hq, hf, hi, hgate, sz, sxbc, sdt, su) = jnp.split(h @ w_in[l], cuts, axis=-1)
        ya, s_a = retention(rq, rk, rv, rg, st_ret[l], pos0)
        yb, s_b = hgrn2(hq, hf, hi, hgate, st_hg[l], lower[l])
        yc, s_c, c_c = ssd_mixer(sz, sxbc, sdt, st_conv[l], st_ssd[l], ssd_conv_w[l], ssd_conv_b[l],
                                 ssd_dt_bias[l], ssd_a_log[l], ssd_d[l], ssd_norm[l])
        yd, s_re, s_im = s5_mixer(su, st_s5_re[l], st_s5_im[l], s5_a_re[l], s5_a_im[l], s5_log_dt[l],
                                  s5_b_re[l], s5_b_im[l], s5_c_re[l], s5_c_im[l], s5_d[l],
                                  s5_glu_w[l], s5_glu_b[l])
        mix = jnp.concatenate([ya, yb, yc, yd], axis=-1).astype(x.dtype)
        x = x + mix @ w_out[l]
        h = rmsnorm(x, norm_mlp[l])
        x = x + jnp.square(jax.nn.relu(h @ w_up[l])) @ w_down[l]
        for lst, s in zip(new, (s_a, s_b, s_c, c_c, s_re, s_im)):
            lst.append(s)
    y = rmsnorm(x, norm_final)
    return (y, jnp.stack(new[0]), jnp.stack(new[1]), jnp.stack(new[2]),
            jnp.stack(new[3]), jnp.stack(new[4]), jnp.stack(new[5]))


def setup_inputs(seed: int = 0) -> dict:
    key = jax.random.key(seed)
    ks = iter(jax.random.split(key, 48))
    f32 = jnp.float32

    def nrm(shape, scale):
        return scale * jax.random.normal(next(ks), shape, f32)

    def unif(shape, lo, hi):
        return jax.random.uniform(next(ks), shape, f32, lo, hi)

    ssd_dt = jnp.exp(unif((DEPTH, SSD_HEADS), math.log(1e-3), math.log(1e-1)))
    return {
        'x_prompt': nrm((BATCH, SEQ, D_MODEL), 1.0),
        'x_sample': nrm((DEC_BATCH, DEC_SEQ, D_MODEL), 1.0),
        'state_ret': nrm((DEPTH, DEC_BATCH, RET_HEADS, RET_DK, RET_DV), 1.0),
        'state_hgrn': nrm((DEPTH, DEC_BATCH, HG_HEADS, HG_DK, HG_DV), 0.5),
        'state_ssd': nrm((DEPTH, DEC_BATCH, SSD_HEADS, SSD_STATE, SSD_HEADDIM), 0.5),
        'state_ssd_conv': nrm((DEPTH, DEC_BATCH, SSD_CONV - 1, SSD_CONV_DIM), 1.0),
        'state_s5_re': nrm((DEPTH, DEC_BATCH, S5_GROUPS, S5_STATE), 0.5),
        'state_s5_im': nrm((DEPTH, DEC_BATCH, S5_GROUPS, S5_STATE), 0.5),
        'norm_mix': 1.0 + nrm((DEPTH, D_MODEL), 0.02),
        'w_in': nrm((DEPTH, D_MODEL, IN_COLS), D_MODEL ** -0.5),
        'w_out': nrm((DEPTH, MIX_W, D_MODEL), MIX_W ** -0.5),
        'hg_lb_logits': nrm((DEPTH, GROUP_W), 1.0),
        'ssd_conv_w': nrm((DEPTH, SSD_CONV, SSD_CONV_DIM), SSD_CONV ** -0.5),
        'ssd_conv_b': nrm((DEPTH, SSD_CONV_DIM), 0.02),
        'ssd_dt_bias': ssd_dt + jnp.log(-jnp.expm1(-ssd_dt)),
        'ssd_a_log': jnp.log(unif((DEPTH, SSD_HEADS), 1.0, 16.0)),
        'ssd_d': 1.0 + nrm((DEPTH, SSD_HEADS), 0.1),
        'ssd_norm': 1.0 + nrm((DEPTH, GROUP_W), 0.02),
        's5_a_re': -0.5 + nrm((DEPTH, S5_GROUPS, S5_STATE), 0.01),
        's5_a_im': math.pi * jnp.arange(S5_STATE, dtype=f32) + nrm((DEPTH, S5_GROUPS, S5_STATE), 0.01),
        's5_log_dt': unif((DEPTH, S5_GROUPS), math.log(1e-3), math.log(1e-1)),
        's5_b_re': nrm((DEPTH, S5_GROUPS, S5_STATE, S5_CH), (2 * S5_CH) ** -0.5),
        's5_b_im': nrm((DEPTH, S5_GROUPS, S5_STATE, S5_CH), (2 * S5_CH) ** -0.5),
        's5_c_re': nrm((DEPTH, S5_GROUPS, S5_CH, S5_STATE), S5_STATE ** -0.5),
        's5_c_im': nrm((DEPTH, S5_GROUPS, S5_CH, S5_STATE), S5_STATE ** -0.5),
        's5_d': nrm((DEPTH, GROUP_W), 1.0),
        's5_glu_w': nrm((DEPTH, GROUP_W, GROUP_W), GROUP_W ** -0.5),
        's5_glu_b': nrm((DEPTH, GROUP_W), 0.02),
        'norm_mlp': 1.0 + nrm((DEPTH, D_MODEL), 0.02),
        'w_up': nrm((DEPTH, D_MODEL, D_FF), D_MODEL ** -0.5),
        'w_down': nrm((DEPTH, D_FF, D_MODEL), D_FF ** -0.5),
        'norm_final': 1.0 + nrm((D_MODEL,), 0.02),
    }


def reference(x_prompt, x_sample, state_ret, state_hgrn, state_ssd, state_ssd_conv, state_s5_re, state_s5_im,
              norm_mix, w_in, w_out, hg_lb_logits, ssd_conv_w, ssd_conv_b, ssd_dt_bias, ssd_a_log, ssd_d,
              ssd_norm, s5_a_re, s5_a_im, s5_log_dt, s5_b_re, s5_b_im, s5_c_re, s5_c_im, s5_d, s5_glu_w,
              s5_glu_b, norm_mlp, w_up, w_down, norm_final):
    nb = x_prompt.shape[0]

    def fresh(st):
        return jnp.zeros((DEPTH, nb) + st.shape[2:], st.dtype)

    (y_prompt, p_ret, p_hgrn, p_ssd, p_conv, p_s5_re, p_s5_im) = run_trunk(
        x_prompt, 0, fresh(state_ret), fresh(state_hgrn), fresh(state_ssd), fresh(state_ssd_conv),
        fresh(state_s5_re), fresh(state_s5_im),
        norm_mix, w_in, w_out, hg_lb_logits, ssd_conv_w, ssd_conv_b, ssd_dt_bias, ssd_a_log, ssd_d,
        ssd_norm, s5_a_re, s5_a_im, s5_log_dt, s5_b_re, s5_b_im, s5_c_re, s5_c_im, s5_d, s5_glu_w,
        s5_glu_b, norm_mlp, w_up, w_down, norm_final)
    (y_sample, s_ret, s_hgrn, s_ssd, s_conv, s_s5_re, s_s5_im) = run_trunk(
        x_sample, PAST_LEN, state_ret, state_hgrn, state_ssd, state_ssd_conv, state_s5_re, state_s5_im,
        norm_mix, w_in, w_out, hg_lb_logits, ssd_conv_w, ssd_conv_b, ssd_dt_bias, ssd_a_log, ssd_d,
        ssd_norm, s5_a_re, s5_a_im, s5_log_dt, s5_b_re, s5_b_im, s5_c_re, s5_c_im, s5_d, s5_glu_w,
        s5_glu_b, norm_mlp, w_up, w_down, norm_final)
    return (y_prompt, y_sample, p_ret, p_hgrn, p_ssd, p_conv, p_s5_re, p_s5_im,
            s_ret, s_hgrn, s_ssd, s_conv, s_s5_re, s_s5_im)
```

```python
import contextlib
import math
import numpy as np
import concourse.bass as bass
import concourse.mybir as mybir
from concourse.bass_utils import run_bass_kernel_spmd

F32 = mybir.dt.float32
BF16 = mybir.dt.bfloat16
I32 = mybir.dt.int32
AF = mybir.ActivationFunctionType
ALU = mybir.AluOpType
AX = mybir.AxisListType

ENGS = ("pe", "act", "dve", "pool", "sp")
MAXV = 30000

D = 1024
NCORE = 8
SEQ = 2048
NPT = 16
NT = 17
SB = 16
SL = 8
PAST = 16384
INC = 3332
DFF = 4096
EPS = 1e-6
SGS = [list(range(0, 4)), list(range(4, 8)), list(range(8, 12)), list(range(12, 16)), [16]]
NEG = -30000.0
FCH = 512
NFC = DFF // FCH


class Op:
    __slots__ = ("eng", "fn", "reads", "writes", "dma", "dkey", "deps", "sig", "idx")

    def __init__(self, eng, fn, reads, writes, dma, dkey):
        self.eng, self.fn, self.reads, self.writes = eng, fn, tuple(reads), tuple(writes)
        self.dma, self.dkey = dma, dkey
        self.deps = []
        self.sig = None
        self.idx = -1


class Sched:
    def __init__(self, nc):
        self.nc = nc
        self.ops = []
        self.lastw = {}
        self.readers = {}
        self.alias = {}

    def _expand(self, keys):
        out = []
        for k in keys:
            for kk in self.alias.get(k, (k,)) if isinstance(k, str) else (k,):
                for k3 in (self.alias.get(kk, (kk,)) if isinstance(kk, str) else (kk,)):
                    if k3 not in out:
                        out.append(k3)
        return out

    def op(self, eng, fn, reads=(), writes=(), dma=False, dkey=None):
        reads, writes = self._expand(reads), self._expand(writes)
        o = Op(eng, fn, reads, writes, dma, dkey)
        o.idx = len(self.ops)
        deps = set()
        for r in o.reads:
            w = self.lastw.get(r)
            if w is not None:
                deps.add(w)
        for r in o.writes:
            w = self.lastw.get(r)
            if w is not None:
                deps.add(w)
            for rd in self.readers.get(r, ()):
                deps.add(rd)
        deps.discard(o.idx)
        o.deps = sorted(deps)
        for r in o.writes:
            self.lastw[r] = o.idx
            self.readers[r] = []
        for r in o.reads:
            if r not in o.writes:
                self.readers.setdefault(r, []).append(o.idx)
        self.ops.append(o)
        return o

    def pe(self, fn, reads=(), writes=()):
        return self.op("pe", fn, reads, writes)

    def act(self, fn, reads=(), writes=()):
        return self.op("act", fn, reads, writes)

    def dve(self, fn, reads=(), writes=()):
        return self.op("dve", fn, reads, writes)

    def pool(self, fn, reads=(), writes=()):
        return self.op("pool", fn, reads, writes)

    def dma(self, eng, fn, reads=(), writes=(), dkey=None):
        if dkey is None:
            dkey = writes[0] if len(writes) else ("st", reads[0])
        return self.op(eng, fn, reads, writes, dma=True, dkey=dkey)

    def emit(self, final_eng="sp"):
        nc = self.nc
        ops = self.ops
        needed = [False] * len(ops)
        for o in ops:
            for d in o.deps:
                p = ops[d]
                if p.eng == "pe" and o.eng == "pe" and not p.dma and not o.dma:
                    continue
                needed[d] = True
        for o in ops:
            if o.dma:
                needed[o.idx] = True
        cnt = {e: 0 for e in ENGS}
        dcnt = {}
        for o in ops:
            if not needed[o.idx]:
                continue
            if o.dma:
                dcnt[o.dkey] = dcnt.get(o.dkey, 0) + 16
                o.sig = ("d", o.dkey, dcnt[o.dkey])
            else:
                cnt[o.eng] += 1
                o.sig = ("e", o.eng, cnt[o.eng])
        stack = contextlib.ExitStack()
        esems = {}
        for e in ENGS:
            n = (cnt[e] + MAXV - 1) // MAXV
            esems[e] = [stack.enter_context(nc.semaphore(f"s_{e}{i}")) for i in range(max(n, 1))]
        dsems = {}
        for i, k in enumerate(dcnt):
            dsems[k] = stack.enter_context(nc.semaphore(f"d{i}"))
        self.n_sems = sum(len(v) for v in esems.values()) + len(dsems)

        def semval(sig):
            if sig[0] == "d":
                return dsems[sig[1]], sig[2]
            k = sig[2] - 1
            return esems[sig[1]][k // MAXV], (k % MAXV) + 1

        by_eng = {e: [o for o in ops if o.eng == e] for e in ENGS}
        final = {}
        for o in ops:
            if o.dma:
                final[o.dkey] = o.sig[2]

        def run(e, eh):
            waited = {}
            for o in by_eng[e]:
                need = {}
                for d in o.deps:
                    p = ops[d]
                    if p.sig is None:
                        continue
                    if p.eng == "pe" and e == "pe" and not p.dma and not o.dma:
                        continue
                    s, v = semval(p.sig)
                    key = (p.sig[0], p.sig[1], (p.sig[2] - 1) // MAXV if p.sig[0] == "e" else 0)
                    if key not in need or need[key][1] < v:
                        need[key] = (s, v)
                for key, (s, v) in need.items():
                    if waited.get(key, 0) >= v:
                        continue
                    waited[key] = v
                    eh.wait_ge(s, v)
                ins = o.fn(eh)
                if o.sig is not None:
                    s, v = semval(o.sig)
                    ins.then_inc(s, 16 if o.dma else 1)
            if e == final_eng:
                for k, v in final.items():
                    eh.wait_ge(dsems[k], v)

        with stack:
            with nc.Block() as block:
                @block.tensor
                def _(eh):
                    run("pe", eh)

                @block.scalar
                def _(eh):
                    run("act", eh)

                @block.vector
                def _(eh):
                    run("dve", eh)

                @block.gpsimd
                def _(eh):
                    run("pool", eh)

                @block.sync
                def _(eh):
                    run("sp", eh)


def _const_tables():
    c = {}
    c["ident"] = np.eye(128, dtype=np.float32)
    s = np.arange(128)[:, None]
    t = np.arange(128)[None, :]
    cp = (s <= t).astype(np.float32)
    cs = ((s <= t) & (s // SL == t // SL)).astype(np.float32)
    c["causal"] = np.stack([cp, cs], 1)
    c["negmask"] = np.stack([(1 - cp) * NEG, (1 - cs) * NEG], 1).astype(np.float32)
    ssame = (s // SL == t // SL).astype(np.float32)
    c["sameseq"] = np.stack([np.ones((128, 128), np.float32), ssame], 1)
    gam = 1.0 - np.exp2(-5.0 - np.arange(4, dtype=np.float64))
    lg = np.log(gam)
    kscale = 64.0 ** -0.5
    retd = np.zeros((128, 2, 4, 128), np.float64)
    for h in range(4):
        retd[:, 0, h, :] = np.exp(lg[h] * np.clip(t - s, 0, None)) * cp * kscale
        retd[:, 1, h, :] = np.exp(lg[h] * np.clip(t - s, 0, None)) * cs * kscale
    c["retd"] = retd.astype(np.float32)
    qs = np.zeros((128, 2, 2, 128), np.float64)
    ks = np.zeros((128, 2, 4), np.float64)
    ra = np.zeros((128, 2, 2), np.float64)
    tt = np.arange(128)
    for h in range(4):
        blk, hl = h // 2, h % 2
        qs[64 * hl:64 * hl + 64, 0, blk, :] = np.exp(lg[h] * (tt + 1))[None, :]
        qs[64 * hl:64 * hl + 64, 1, blk, :] = np.exp(lg[h] * (tt % SL + 1))[None, :]
        ks[:, 0, h] = np.exp(lg[h] * (127 - tt)) * kscale
        ks[:, 1, h] = np.exp(lg[h] * (SL - 1 - tt % SL)) * kscale
        ra[64 * hl:64 * hl + 64, 0, blk] = np.exp(lg[h] * 128)
        ra[64 * hl:64 * hl + 64, 1, blk] = np.exp(lg[h] * SL)
    c["ret_qs"] = qs.astype(np.float32)
    c["ret_ks"] = ks.astype(np.float32)
    c["ret_a"] = ra.astype(np.float32)
    ang = (1.0 / (10000.0 ** np.linspace(0.0, 1.0, 32, dtype=np.float32))).astype(np.float32)
    pos = np.zeros((128, NT), np.float32)
    for i in range(NPT):
        pos[:, i] = i * 128 + tt
    pos[:, NPT] = PAST + tt % SL
    theta = (pos[:, :, None] * ang[None, None, :]).astype(np.float32)
    c["cos"] = np.cos(theta).astype(np.float32)
    c["sin"] = np.sin(theta).astype(np.float32)
    b = np.arange(SB)
    c["colmask"] = np.broadcast_to((tt[None, :] // SL == b[:, None]).astype(np.float32)[None], (128, SB, 128)).copy()
    c["rowmask"] = (tt[:, None] // SL == b[None, :]).astype(np.float32)
    rst = np.ones((128, 2, 128), np.float32)
    rst[:, 1, :] = (tt % SL != 0).astype(np.float32)[None, :]
    rst[:, 0, 0] = 0.0
    c["rst"] = rst
    tau = np.zeros((128, 2, 128), np.float32)
    tau[:, 0, :] = tt[None, :]
    tau[:, 1, :] = (tt % SL)[None, :]
    c["tau"] = tau
    hm = np.zeros((128, 2), np.float32)
    hm[:64, 0] = 1.0
    hm[64:, 1] = 1.0
    c["hmask"] = hm
    return c


CONST_SHAPES = {
    "ident": [128, 128], "causal": [128, 2, 128], "negmask": [128, 2, 128], "sameseq": [128, 2, 128],
    "retd": [128, 2, 4, 128], "ret_qs": [128, 2, 2, 128], "ret_ks": [128, 2, 4], "ret_a": [128, 2, 2],
    "cos": [128, NT, 32], "sin": [128, NT, 32], "colmask": [128, SB, 128], "rowmask": [128, SB],
    "rst": [128, 2, 128], "tau": [128, 2, 128], "hmask": [128, 2],
}

IN_SHAPES = {
    "xp": [SEQ, D], "xs": [128, D],
    "st_ret": [2, SB, 4, 64, 64], "st_hg": [2, SB, 4, 64, 64], "st_ssd": [2, SB, 4, 128, 64],
    "st_conv": [2, SB * 3, 768], "st_s5re": [2, SB, 1024], "st_s5im": [2, SB, 1024],
    "nm_fm": [2, 128, 8], "nmlp_fm": [2, 128, 8], "norm_final": [D],
    "w_in": [2, D, INC], "w_out": [2, D, D], "w_up": [2, D, DFF], "w_down": [2, DFF, D],
    "hg_lg": [128, 2, 2], "conv_w": [2, 128, 6, 4], "conv_b": [2, 128, 6],
    "dt_bias": [2, 4], "a_log": [2, 4], "ssd_d": [2, 4], "ssd_norm": [2, 256],
    "s5_are": [2, 128, 8], "s5_aim": [2, 128, 8], "s5_ldt": [2, 128, 8],
    "s5_bre": [2, 128, 8, 128], "s5_bim": [2, 128, 8, 128],
    "s5_cre": [2, 128, 8, 32], "s5_cim": [2, 128, 8, 32], "s5_dfm": [2, 128, 2],
    "glu_w": [2, 256, 256], "glu_b": [2, 256],
}
OUT_SHAPES = {
    "yp": [SEQ, D], "ys": [128, D],
    "p_ret": [2, 4, 64, 64], "p_hg": [2, 4, 64, 64], "p_ssd": [2, 4, 128, 64], "p_conv": [2, 3, 768],
    "p_s5re": [2, 8, 128], "p_s5im": [2, 8, 128],
    "s_ret": [2, SB, 4, 64, 64], "s_hg": [2, SB, 4, 64, 64], "s_ssd": [2, SB, 4, 128, 64],
    "s_conv": [2, SB * 3, 768], "s_s5re": [2, SB, 1024], "s_s5im": [2, SB, 1024],
}


def build_program(n_layers=2, sgs=None, mixers=("ret", "hg", "ssd", "s5"), do_mlp=True):
    sgs = SGS if sgs is None else sgs
    LASTP = max([i for g in sgs for i in g if i < NPT] + [-1])
    nc = bass.Bass("TRN2", target_bir_lowering=False)
    S = Sched(nc)
    st = contextlib.ExitStack()
    din = {k: nc.dram_tensor(k, v, F32, kind="ExternalInput").ap() for k, v in IN_SHAPES.items()}
    dcon = {k: nc.dram_tensor("c_" + k, v, F32, kind="ExternalInput").ap() for k, v in CONST_SHAPES.items()}
    dout = {k: nc.dram_tensor(k, v, F32, kind="ExternalOutput").ap() for k, v in OUT_SHAPES.items()}

    def sb(name, shape, dt=F32):
        return st.enter_context(nc.sbuf_tensor(name, shape, dt))

    with st:
        PS = st.enter_context(nc.psum_tensor("PS", [128, 8, 512], F32))

        def bk(k):
            return "b%d" % k

        NSG = max(len(g) for g in sgs)
        X = sb("X", [128, NSG, D])
        H2T = sb("H2T", [128, NSG, 8, 128], BF16)
        WIN = sb("WIN", [128, 8, INC], BF16)
        WOUT = sb("WOUT", [128, 8, D], BF16)
        GB = sb("GB", [128, 8, 1024])
        GBF = GB[:].rearrange("p a b -> p (a b)")

        def galias(name, k, c0, c1, shape):
            ap = GB[:, k, c0 * 128:c1 * 128]
            if len(shape) == 3:
                ap = ap.rearrange("p (a b) -> p a b", b=shape[2])
            elif len(shape) == 4:
                ap = ap.rearrange("p (a b c) -> p a b c", b=shape[2], c=shape[3])
            S.alias[name] = [("G", k, q) for q in range(c0 // 2, (c1 + 1) // 2)]
            return ap

        WUP, WDN = [], []
        for r in range(2):
            WUP.append(GB[:, 4 * r:4 * r + 2, :].bitcast(BF16).rearrange("p a (k f) -> p (a k) f", f=FCH))
            WDN.append(GB[:, 4 * r + 2:4 * r + 4, :].bitcast(BF16).rearrange("p a (f n) -> p (a f) n", n=D))
            S.alias["WUP%d" % r] = [("G", 4 * r + a, q) for a in range(2) for q in range(4)]
            S.alias["WDN%d" % r] = [("G", 4 * r + 2 + a, q) for a in range(2) for q in range(4)]
        C = {k: sb("C_" + k, v, BF16 if k == "colmask" else F32) for k, v in CONST_SHAPES.items()}
        NM = sb("NM", [128, 2, 8]); NMLP = sb("NMLP", [128, 2, 8])
        HGLG = sb("HGLG", [128, 2, 2]); LOW = sb("LOW", [128, 2, 2]); OML = sb("OML", [128, 2, 2])
        CW = sb("CW", [128, 2, 6, 4]); CB = sb("CB", [128, 2, 6])
        DTB = sb("DTB", [128, 2, 4]); NEGA = sb("NEGA", [128, 2, 4]); SSDD = sb("SSDD", [128, 2, 4])
        SNORM = sb("SNORM", [128, 256]); GLUB = sb("GLUB", [128, 256]); GLUW = sb("GLUW", [128, 2, 256])
        S5P = sb("S5P", [128, 2, 3, 8])
        S5C = sb("S5C", [128, 8, 32]); S5CI = sb("S5CI", [128, 8, 32])
        S5D = sb("S5D", [128, 2, 2]); DIAGD = sb("DIAGD", [128, 2, 128])
        COSR = sb("COSR", [128, 8, 128]); SINR = sb("SINR", [128, 8, 128])
        ABAR = sb("ABAR", [128, 2, 8]); COEF = sb("COEF", [128, 2, 8]); RR = sb("RR", [128, 8]); PHI = sb("PHI", [128, 8])
        BCF = sb("BCF", [128, 2, 8, 128])
        SRET = sb("SRET", [128, 2, 2, 128]); SHG = sb("SHG", [128, 2, 2, 128]); SSSD = sb("SSSD", [128, 2, 4, 64])
        ZS5 = sb("ZS5", [128, 2, 2, 8]); XPH = sb("XPH", [128, 2, 6, 3])
        HTS = [sb("HT0", [128, 8, 128], BF16), sb("HT1", [128, 8, 128], BF16)]
        XN = sb("XN", [128, D])
        STA = sb("STA", [128, 16]); STB = sb("STB", [128, 16]); STC = sb("STC", [128, 16])
        TM = sb("TM", [128, 1796]); FM = sb("FM", [128, 12, 128])
        MIX = sb("MIX", [128, D]); MIXT = sb("MIXT", [128, 8, 128], BF16)
        JUNK = MIXT[:].rearrange("p a b -> p (a b)")
        S.alias["JUNK"] = ["MIXT"]
        SGR = sb("SGR", [128, 256]); SGH = sb("SGH", [128, 256]); SSZ = sb("SSZ", [128, 256])
        HREF = sb("HREF", [128, 2, 4]); EBL = sb("EBL", [128, 2, SB])
        DEL = sb("DEL", [128, 4]); LOGA = sb("LOGA", [128, 4]); LND = sb("LND", [128, 4]); BCOL = sb("BCOL", [128, 4])
        BLAST = sb("BLAST", [128, 4]); KSC = sb("KSC", [128, 4])
        RELU = [TM[:, 0:256], TM[:, 256:512]]
        S.alias["TM"] = [("TM", q) for q in range(4)]
        S.alias["RELU0"] = [("TM", 0)]; S.alias["RELU1"] = [("TM", 1)]
        S.alias["ACTT0"] = [("TM", 2)]; S.alias["ACTT1"] = [("TM", 3)]
        ACTT2 = [TM[:, 512:1024].bitcast(BF16).rearrange("p (f n) -> p f n", n=256),
                 TM[:, 1024:1536].bitcast(BF16).rearrange("p (f n) -> p f n", n=256)]
        QKT = galias("QKT", 0, 0, 4, [128, 4, 128]); PT = galias("PT", 0, 4, 8, [128, 4, 128])
        QKR = galias("QKR", 1, 0, 4, [128, 512]); RT1 = galias("RT1", 1, 4, 6, [128, 8, 32]); RT2 = galias("RT2", 1, 6, 8, [128, 8, 32])
        QD = galias("QD", 3, 0, 2, [128, 2, 128]); KD = galias("KD", 3, 2, 4, [128, 256]); QM = galias("QM", 3, 4, 8, [128, 4, 128])
        def galias16(name, k, c0, c1, shape):
            ap = GB[:, k, c0 * 128:c1 * 128].bitcast(BF16)
            if len(shape) == 3:
                ap = ap.rearrange("p (a b) -> p a b", b=shape[2])
            elif len(shape) == 4:
                ap = ap.rearrange("p (a b c) -> p a b c", b=shape[2], c=shape[3])
            S.alias[name] = [("G", k, q) for q in range(c0 // 2, (c1 + 1) // 2)]
            return ap

        QH = galias16("QH", 3, 4, 6, [128, 2, 2, 128])
        QKTB = galias16("QKTB", 0, 0, 2, [128, 4, 128])
        HKVB = galias16("HKVB", 0, 0, 4, [128, 4, 2, 128])
        XRB = galias16("XRB", 2, 0, 4, [128, 8, 128]); XIB = galias16("XIB", 3, 0, 4, [128, 8, 128])
        VB = sb("VB", [128, 512], BF16)
        PTB = galias16("PTB", 0, 4, 6, [128, 4, 128]); PT2B = galias16("PT2B", 5, 0, 2, [128, 4, 128]); XSB = galias16("XSB", 5, 2, 3, [128, 256])
        S5CB = sb("S5CB", [128, 8, 32], BF16); S5CIB = sb("S5CIB", [128, 8, 32], BF16)
        KDM = galias("KDM", 7, 0, 8, [128, 4, 256]); SST2 = galias("SST2", 7, 0, 8, [128, 4, 4, 64])
        CVT = galias("CVT", 7, 0, 6, [128, 768])
        SST = galias("SST", 5, 0, 8, [128, 4, 2, 128])
        HE = galias("HE", 1, 0, 2, [128, 2, 128]); HSIG = galias("HSIG", 1, 2, 4, [128, 2, 128])
        HKEY = galias("HKEY", 1, 4, 6, [128, 2, 128]); HB = galias("HB", 1, 6, 8, [128, 2, 128])
        HQ = galias("HQ", 2, 0, 2, [128, 2, 128]); HD = galias("HD", 2, 2, 4, [128, 2, 128])
        HEX = galias("HEX", 2, 4, 6, [128, 2, 128]); HQA = galias("HQA", 2, 6, 8, [128, 2, 128])
        HQD = galias("HQD", 3, 0, 2, [128, 2, 128]); HKT = galias("HKT", 3, 2, 4, [128, 256])
        LB = galias("LB", 0, 0, 4, [128, 4, 128]); D1 = galias("D1", 0, 4, 8, [128, 4, 128])
        DEC = galias("DEC", 1, 0, 4, [128, 4, 128]); EBT = galias("EBT", 1, 4, 8, [128, 4, 128])
        CMD = galias("CMD", 2, 0, 4, [128, 4, 128]); BD = galias("BD", 2, 4, 8, [128, 4, 128])
        BDM = galias("BDM", 3, 0, 4, [128, 4, 128])
        PT2 = galias("PT2", 5, 0, 4, [128, 4, 128]); XS = galias("XS", 5, 4, 6, [128, 256]); BMT = galias("BMT", 5, 6, 8, [128, 2, 128])
        CACC = galias("CACC", 4, 0, 6, [128, 6, 128]); XC = galias("XC", 6, 0, 6, [128, 6, 128])
        CTMPP = galias("CTMPP", 7, 0, 6, [128, 6, 128]); CTMPS = galias("CTMPS", 5, 0, 6, [128, 6, 128])
        XPW = GB[:, 5, 0:786].rearrange("p (c n) -> p c n", n=131)
        S.alias["XPW"] = [("G", 5, q) for q in range(4)]
        XPS = GBF[:, 0:1056].rearrange("p (c b j) -> p c b j", b=SB, j=11)
        S.alias["XPS"] = [("G", 0, q) for q in range(4)] + [("G", 1, 0)]
        HKV = galias("HKV", 4, 0, 8, [128, 4, 2, 128]); HKV4 = galias("HKV4", 7, 6, 8, [128, 2, 128])
        OB = galias("OB", 4, 0, 2, [128, 256]); OB2 = galias("OB2", 4, 2, 4, [128, 256])
        PR = galias("PR", 0, 0, 8, [128, 8, 128]); PI = galias("PI", 1, 0, 8, [128, 8, 128])
        XR = galias("XR", 2, 0, 8, [128, 8, 128]); XI = galias("XI", 3, 0, 8, [128, 8, 128])
        T1 = galias("T1", 4, 0, 8, [128, 8, 128]); T2 = galias("T2", 5, 0, 8, [128, 8, 128])
        WR = galias("WR", 6, 0, 8, [128, 8, 128]); WI = galias("WI", 7, 0, 8, [128, 8, 128])
        TI = GB[:, 7, :].bitcast(I32).rearrange("p (c n) -> p c n", n=128)
        S.alias["TI"] = [("G", 7, q) for q in range(4)]
        YG = galias("YG", 4, 0, 2, [128, 256]); YU = galias("YU", 4, 2, 4, [128, 256])
        YE = galias("YE", 4, 4, 6, [128, 256]); YGT = galias("YGT", 4, 6, 8, [128, 2, 128])
        ZT = galias("ZT", 5, 0, 8, [128, 1024])
        INJ = galias("INJ", 0, 0, 2, [128, 2, 8, SB]); INJ2 = galias("INJ2", 0, 2, 4, [128, 2, 8, SB])
        ZL = galias("ZL", 6, 0, 2, [128, 2, 8, SB]); Z0 = galias("Z0", 1, 2, 4, [128, 2, 8, SB])
        YOUT = galias("YOUT", 0, 0, 8, [128, 1024]); WF = galias("WF", 1, 0, 8, [128, 1024])

        def dma_in(eng, dst, src, key):
            S.dma(eng, lambda e: e.dma_start(out=dst, in_=src), writes=[key])

        def dma_out(eng, dst, src, key):
            S.dma(eng, lambda e: e.dma_start(out=dst, in_=src), reads=[key], dkey=("st", key))

        def MM(out, lhsT, rhs, start=True, stop=True):
            return lambda e: e.matmul(out, lhsT=lhsT, rhs=rhs, start=start, stop=stop, skip_group_check=True)

        def TR(out, in_, n=128):
            return lambda e: e.transpose(out, in_, C["ident"][0:n, 0:n])

        def ACT(out, in_, func, **kw):
            return lambda e: e.activation(out=out, in_=in_, func=func, **kw)

        def TT(out, in0, in1, op):
            return lambda e: e.tensor_tensor(out=out, in0=in0, in1=in1, op=op)

        def TS(out, in0, s1, s2, op0, op1=None):
            if op1 is None:
                return lambda e: e.tensor_scalar(out=out, in0=in0, scalar1=s1, scalar2=None, op0=op0)
            return lambda e: e.tensor_scalar(out=out, in0=in0, scalar1=s1, scalar2=s2, op0=op0, op1=op1)

        def STT(out, in0, scalar, in1, op0, op1):
            return lambda e: e.scalar_tensor_tensor(out=out, in0=in0, scalar=scalar, in1=in1, op0=op0, op1=op1)

        def CP(out, in_):
            return lambda e: e.tensor_copy(out=out, in_=in_)

        def bc(ap, shape):
            return ap.to_broadcast(shape)

        MUL, ADD, SUB = ALU.mult, ALU.add, ALU.subtract

        for k in CONST_SHAPES:
            dma_in("pool" if k == "colmask" else "sp", C[k][:], dcon[k], "C_" + k)
        dma_in("sp", NM[:], din["nm_fm"].rearrange("l p c -> p l c"), "NM")
        dma_in("sp", NMLP[:], din["nmlp_fm"].rearrange("l p c -> p l c"), "NMLP")
        dma_in("sp", HGLG[:], din["hg_lg"], "HGLG")
        dma_in("sp", CW[:], din["conv_w"].rearrange("l p b j -> p l b j"), "CW")
        dma_in("sp", CB[:], din["conv_b"].rearrange("l p b -> p l b"), "CB")
        for l in range(2):
            dma_in("sp", DTB[:, l, :], din["dt_bias"][l].partition_broadcast(128), "DTB")
            dma_in("sp", NEGA[:, l, :], din["a_log"][l].partition_broadcast(128), "NEGA")
            dma_in("sp", SSDD[:, l, :], din["ssd_d"][l].partition_broadcast(128), "SSDD")
            dma_in("sp", S5P[:, l, 0, :], din["s5_are"][l], "S5P")
            dma_in("sp", S5P[:, l, 1, :], din["s5_aim"][l], "S5P")
            dma_in("sp", S5P[:, l, 2, :], din["s5_ldt"][l], "S5P")
            dma_in("sp", S5D[:, l, :], din["s5_dfm"][l], "S5D")
        S.act(ACT(NEGA[:], NEGA[:], AF.Exp), reads=["NEGA"], writes=["NEGA"])
        S.dve(TS(NEGA[:], NEGA[:], -1.0, None, MUL), reads=["NEGA"], writes=["NEGA"])
        S.act(ACT(HGLG[:], HGLG[:], AF.Exp), reads=["HGLG"], writes=["HGLG"])
        S.dve(TT(LOW[:, 1, :], HGLG[:, :, 0], HGLG[:, :, 1], ADD), reads=["HGLG"], writes=["LOW"])
        S.dve(lambda e: e.reciprocal(out=LOW[:, 1, :], in_=LOW[:, 1, :]), reads=["LOW"], writes=["LOW"])
        S.dve(TT(LOW[:, 1, :], LOW[:, 1, :], HGLG[:, :, 1], MUL), reads=["LOW", "HGLG"], writes=["LOW"])
        S.pool(lambda e: e.memset(LOW[:, 0, :], 0.0), writes=["LOW"])
        S.dve(TS(OML[:], LOW[:], -1.0, 1.0, MUL, ADD), reads=["LOW"], writes=["OML"])
        S.pool(lambda e: e.memset(HREF[:], 0.0), writes=["HREF"])
        for t_, k_ in ((SRET, "SRET"), (SHG, "SHG"), (SSSD, "SSSD"), (ZS5, "ZS5"), (XPH, "XPH"), (MIX, "MIX")):
            S.pool(lambda e, t_=t_: e.memset(t_[:], 0.0), writes=[k_])

        def layer_params(l):
            dma_in("sp", SNORM[:], din["ssd_norm"][l].partition_broadcast(128), "SNORM")
            dma_in("sp", GLUB[:], din["glu_b"][l].partition_broadcast(128), "GLUB")
            dma_in("sp", GLUW[:], din["glu_w"][l].rearrange("(c p) n -> p c n", p=128), "GLUW")
            dma_in("sp", S5C[:], din["s5_cre"][l], "S5C")
            dma_in("sp", S5CI[:], din["s5_cim"][l], "S5CI")
            S.dve(TS(S5CI[:], S5CI[:], -1.0, None, MUL), reads=["S5CI"], writes=["S5CI"])
            S.dve(CP(S5CB[:], S5C[:]), reads=["S5C"], writes=["S5CB"])
            S.dve(CP(S5CIB[:], S5CI[:]), reads=["S5CI"], writes=["S5CIB"])
            for ub in range(2):
                S.dve(TS(DIAGD[:, ub, :], C["ident"][:], S5D[:, l, ub:ub + 1], None, MUL),
                      reads=["C_ident", "S5D"], writes=["DIAGD"])

        def s5_setup(l):
            S.act(ACT(STA[:, 0:8], S5P[:, l, 2, :], AF.Exp), reads=["S5P"], writes=["STA", "STAN"])
            S.dve(TT(RR[:], S5P[:, l, 0, :], STA[:, 0:8], MUL), reads=["S5P", "STA", "STAN"], writes=["RR"])
            S.dve(TT(PHI[:], S5P[:, l, 1, :], STA[:, 0:8], MUL), reads=["S5P", "STA", "STAN"], writes=["PHI"])
            S.act(ACT(RR[:], RR[:], AF.Exp), reads=["RR"], writes=["RR"])
            S.dve(TT(T1[:], bc(PHI[:].unsqueeze(2), [128, 8, 128]), bc(C["tau"][:, 0:1, :], [128, 8, 128]), MUL),
                  reads=["PHI", "C_tau"], writes=["T1"])

            def sincos(dst, dkey, shift):
                S.dve(TS(T2[:], T1[:], shift, 1.0 / (2 * math.pi), ADD, MUL), reads=["T1"], writes=["T2"])
                S.dve(CP(TI[:], T2[:]), reads=["T2"], writes=["TI"])
                S.dve(CP(T2[:], TI[:]), reads=["TI"], writes=["T2"])
                S.dve(STT(T2[:], T2[:], -2 * math.pi, T1[:], MUL, ADD), reads=["T2", "T1"], writes=["T2"])
                S.dve(TS(T2[:], T2[:], shift, 3.1415925, ADD, ALU.min), reads=["T2"], writes=["T2"])
                S.dve(TS(T2[:], T2[:], -3.1415925, None, ALU.max), reads=["T2"], writes=["T2"])
                S.act(ACT(dst[:], T2[:], AF.Sin), reads=["T2"], writes=[dkey])

            sincos(SINR, "SINR", 0.0)
            sincos(COSR, "COSR", math.pi / 2)
            S.dve(TT(ABAR[:, 0, :], RR[:], COSR[:, :, 1], MUL), reads=["RR", "COSR"], writes=["ABAR"])
            S.dve(TT(ABAR[:, 1, :], RR[:], SINR[:, :, 1], MUL), reads=["RR", "SINR"], writes=["ABAR"])
            xx, yy, uu, vv = STA[:, 0:8], ABAR[:, 1, :], S5P[:, l, 0, :], S5P[:, l, 1, :]
            S.dve(TS(STA[:, 0:8], ABAR[:, 0, :], -1.0, None, ADD), reads=["ABAR"], writes=["STA", "STAN"])
            S.dve(TT(STB[:, 0:8], uu, uu, MUL), reads=["S5P"], writes=["STB"])
            S.dve(TT(STB[:, 8:16], vv, vv, MUL), reads=["S5P"], writes=["STB"])
            S.dve(TT(STB[:, 0:8], STB[:, 0:8], STB[:, 8:16], ADD), reads=["STB"], writes=["STB"])
            S.dve(lambda e: e.reciprocal(out=STB[:, 0:8], in_=STB[:, 0:8]), reads=["STB"], writes=["STB"])
            S.dve(TT(STC[:, 0:8], xx, uu, MUL), reads=["STA", "STAN", "S5P"], writes=["STC"])
            S.dve(TT(STC[:, 8:16], yy, vv, MUL), reads=["ABAR", "S5P"], writes=["STC"])
            S.dve(TT(STC[:, 0:8], STC[:, 0:8], STC[:, 8:16], ADD), reads=["STC"], writes=["STC"])
            S.dve(TT(COEF[:, 0, :], STC[:, 0:8], STB[:, 0:8], MUL), reads=["STC", "STB"], writes=["COEF"])
            S.dve(TT(STC[:, 0:8], yy, uu, MUL), reads=["ABAR", "S5P"], writes=["STC"])
            S.dve(TT(STC[:, 8:16], xx, vv, MUL), reads=["STA", "STAN", "S5P"], writes=["STC"])
            S.dve(TT(STC[:, 0:8], STC[:, 0:8], STC[:, 8:16], SUB), reads=["STC"], writes=["STC"])
            S.dve(TT(COEF[:, 1, :], STC[:, 0:8], STB[:, 0:8], MUL), reads=["STC", "STB"], writes=["COEF"])
            dma_in("sp", PR[:], din["s5_bre"][l], "PR")
            dma_in("sp", PI[:], din["s5_bim"][l], "PI")
            for ri, dst, dk in ((0, XR, "XR"), (1, XI, "XI")):
                for half in range(2):
                    bnk = 4 + 2 * ri + half
                    for cc in range(4):
                        c = half * 4 + cc
                        S.dve(TS(WR[:, c, :], C["ident"][:], COEF[:, ri, c:c + 1], None, MUL),
                              reads=["C_ident", "COEF"], writes=["WR"])
                        S.pe(MM(PS[:, bnk, cc * 128:(cc + 1) * 128], C["sameseq"][:, 0, :], WR[:, c, :]),
                             reads=["C_sameseq", "WR"], writes=[bk(bnk)])
                    S.act(ACT(dst[:, half * 4:half * 4 + 4, :], PS[:, bnk, :].rearrange("p (c n) -> p c n", c=4), AF.Copy),
                          reads=[bk(bnk)], writes=[dk])
            S.dve(TT(T1[:], PR[:], XR[:], MUL), reads=["PR", "XR"], writes=["T1"])
            S.dve(TT(T2[:], PI[:], XI[:], MUL), reads=["PI", "XI"], writes=["T2"])
            S.pool(TT(BCF[:, 0], T1[:], T2[:], SUB), reads=["T1", "T2"], writes=["BCF"])
            S.dve(TT(T1[:], PR[:], XI[:], MUL), reads=["PR", "XI"], writes=["T1"])
            S.dve(TT(T2[:], PI[:], XR[:], MUL), reads=["PI", "XR"], writes=["T2"])
            S.pool(TT(BCF[:, 1], T1[:], T2[:], ADD), reads=["T1", "T2"], writes=["BCF"])

        def s5_sample_tables():
            for tb, k in ((COSR, "COSR"), (SINR, "SINR")):
                v = tb[:, :, 8:128].rearrange("p c (b e) -> p c b e", e=8)
                S.dve(CP(v, bc(tb[:, :, 0:8].unsqueeze(2), [128, 8, 15, 8])), reads=[k], writes=[k])

        def norm_T(xap, xkey, wfm, wkey, dst, dkey):
            S.act(ACT(JUNK[:], xap, AF.Square, accum_out=STA[:, 0:1]), reads=[xkey], writes=["JUNK", "STAN"])
            S.act(ACT(STA[:, 1:2], STA[:, 0:1], AF.Ln, scale=1.0 / D, bias=EPS), reads=["STAN"], writes=["STAN"])
            S.act(ACT(STA[:, 2:3], STA[:, 1:2], AF.Exp, scale=-0.5), reads=["STAN"], writes=["STAN"])
            S.act(ACT(XN[:], xap, AF.Copy, scale=STA[:, 2:3]), reads=[xkey, "STAN"], writes=["XN"])
            for half in range(2):
                for cc in range(4):
                    c = half * 4 + cc
                    S.pe(TR(PS[:, half, cc * 128:(cc + 1) * 128], XN[:, c * 128:(c + 1) * 128]),
                         reads=["XN", "C_ident"], writes=[bk(half)])
                for cc in range(4):
                    c = half * 4 + cc
                    S.act(ACT(dst(half)[:, cc, :], PS[:, half, cc * 128:(cc + 1) * 128], AF.Copy, scale=wfm[:, c:c + 1]),
                          reads=[bk(half), wkey], writes=[dkey])
            return STA[:, 2:3]

        TMG = [(0, 512, 0), (512, 1024, 512), (1536, 2048, 1024), (2048, 2304, 1536), (3072, 3076, 1792)]
        FMB = [1024 + 128 * b for b in range(4)] + [2304 + 128 * b for b in range(6)] + [3076, 3076 + 128]

        def proj_in(par):
            HT = HTS[par]
            hk = "HT%d" % par
            nb = 0
            for (c0, c1, d0) in TMG:
                bnk = 2 + nb % 2
                nb += 1
                w = c1 - c0
                for k in range(8):
                    S.pe(MM(PS[:, bnk, 0:w], HT[:, k, :], WIN[:, k, c0:c1], start=(k == 0), stop=(k == 7)),
                         reads=[hk, "WIN"], writes=[bk(bnk)])
                S.act(ACT(TM[:, d0:d0 + w], PS[:, bnk, 0:w], AF.Copy), reads=[bk(bnk)], writes=["TM"])
                if c0 == 512:
                    S.act(ACT(VB[:, 0:256], PS[:, bnk, 0:256], AF.Copy), reads=[bk(bnk)], writes=["VB"])
                if c0 == 1536:
                    S.act(ACT(VB[:, 256:512], PS[:, bnk, 0:256], AF.Copy), reads=[bk(bnk)], writes=["VB"])
            for g in range(3):
                bnk = 2 + nb % 2
                nb += 1
                for bb in range(4):
                    b = g * 4 + bb
                    for k in range(8):
                        S.pe(MM(PS[:, bnk, bb * 128:(bb + 1) * 128], WIN[:, k, FMB[b]:FMB[b] + 128], HT[:, k, :],
                                start=(k == 0), stop=(k == 7)), reads=[hk, "WIN"], writes=[bk(bnk)])
                fn = CP(FM[:, g * 4:g * 4 + 4, :], PS[:, bnk, :].rearrange("p (c n) -> p c n", c=4))
                S.dve(fn, reads=[bk(bnk)], writes=["FM"])

        def load_sst(name, l, g):
            src = din[name][l, 4 * g:4 * g + 4].rearrange("b (blk hl) k v -> hl k b blk v", hl=2)
            for hl in range(2):
                dma_in("sp", SST[64 * hl:64 * hl + 64, :, :, 64 * hl:64 * hl + 64], src[hl], "SST")

        def store_sst(name, l, g):
            dst = dout[name][l, 4 * g:4 * g + 4].rearrange("b (blk hl) k v -> hl k b blk v", hl=2)
            for hl in range(2):
                dma_out("sp", dst[hl], SST[64 * hl:64 * hl + 64, :, :, 64 * hl:64 * hl + 64], "SST")

        def retention(l, i, var):
            cos = bc(C["cos"][:, i, :].unsqueeze(1), [128, 8, 32])
            sin = bc(C["sin"][:, i, :].unsqueeze(1), [128, 8, 32])
            qk = TM[:, 0:512].rearrange("p (h i two) -> p h i two", h=8, two=2)
            qo = QKR[:].rearrange("p (h i two) -> p h i two", h=8, two=2)
            xe, xo = qk[:, :, :, 0], qk[:, :, :, 1]
            S.pool(TT(RT1[:], xe, cos, MUL), reads=["TM", "C_cos"], writes=["RT1"])
            S.pool(TT(RT2[:], xo, sin, MUL), reads=["TM", "C_sin"], writes=["RT2"])
            S.pool(TT(qo[:, :, :, 0], RT1[:], RT2[:], SUB), reads=["RT1", "RT2"], writes=["QKR"])
            S.pool(TT(RT1[:], xe, sin, MUL), reads=["TM", "C_sin"], writes=["RT1"])
            S.pool(TT(RT2[:], xo, cos, MUL), reads=["TM", "C_cos"], writes=["RT2"])
            S.pool(TT(qo[:, :, :, 1], RT1[:], RT2[:], ADD), reads=["RT1", "RT2"], writes=["QKR"])
            for b in range(4):
                S.pe(TR(PS[:, 4, b * 128:(b + 1) * 128], QKR[:, b * 128:(b + 1) * 128]), reads=["QKR", "C_ident"], writes=[bk(4)])
            S.act(ACT(QKTB[:], PS[:, 4, :].rearrange("p (c n) -> p c n", c=4), AF.Copy), reads=[bk(4)], writes=["QKTB"])
            S.dve(TT(QH[:], bc(QKTB[:, 0:2, :].unsqueeze(2), [128, 2, 2, 128]), bc(C["hmask"][:].unsqueeze(1).unsqueeze(3), [128, 2, 2, 128]), MUL),
                  reads=["QKTB", "C_hmask"], writes=["QH"])
            for h in range(4):
                blk, hl = h // 2, h % 2
                S.pe(MM(PS[:, 5, h * 128:(h + 1) * 128], QKTB[:, 2 + blk, :], QH[:, blk, hl, :]),
                     reads=["QKTB", "QH"], writes=[bk(5)])
            S.dve(TT(PTB[:], PS[:, 5, :].rearrange("p (c n) -> p c n", c=4), C["retd"][:, var], MUL),
                  reads=[bk(5), "C_retd"], writes=["PTB"])
            S.pool(TT(QD[:], QKTB[:, 0:2, :], C["ret_qs"][:, var], MUL), reads=["QKTB", "C_ret_qs"], writes=["QD"])
            S.pool(TT(KD[:].rearrange("p (h k) -> p h k", h=4), QKR[:, 256:512].rearrange("p (h k) -> p h k", h=4),
                      bc(C["ret_ks"][:, var, :].unsqueeze(2), [128, 4, 64]), MUL), reads=["QKR", "C_ret_ks"], writes=["KD"])
            for h in range(4):
                S.pe(MM(PS[:, 6, h * 64:(h + 1) * 64], PTB[:, h, :], VB[:, h * 64:(h + 1) * 64], start=(h == 0), stop=False),
                     reads=["PTB", "VB"], writes=[bk(6)])
            if var == 0:
                for blk in range(2):
                    S.pe(MM(PS[:, 6, blk * 128:(blk + 1) * 128], QD[:, blk, :], SRET[:, l, blk, :], start=False, stop=(blk == 1)),
                         reads=["QD", "SRET"], writes=[bk(6)])
                for blk in range(2):
                    S.pe(MM(PS[:, 7, blk * 128:(blk + 1) * 128], KD[:, blk * 128:(blk + 1) * 128], TM[:, 512 + blk * 128:512 + (blk + 1) * 128]),
                         reads=["KD", "TM"], writes=[bk(7)])
                for blk in range(2):
                    for hl in range(2):
                        sl = slice(64 * hl, 64 * hl + 64)
                        S.dve(STT(SRET[sl, l, blk, sl], SRET[sl, l, blk, sl], C["ret_a"][sl, 0, blk:blk + 1],
                                  PS[sl, 7, blk * 128 + 64 * hl:blk * 128 + 64 * hl + 64], MUL, ADD),
                              reads=["SRET", "C_ret_a", bk(7)], writes=["SRET"])
            else:
                for g in range(4):
                    load_sst("st_ret", l, g)
                    for blk in range(2):
                        S.dve(TT(QM[:], bc(QD[:, blk, :].unsqueeze(1), [128, 4, 128]), C["colmask"][:, 4 * g:4 * g + 4, :], MUL),
                              reads=["QD", "C_colmask"], writes=["QM"])
                        for bb in range(4):
                            last = (g == 3 and blk == 1 and bb == 3)
                            S.pe(MM(PS[:, 6, blk * 128:(blk + 1) * 128], QM[:, bb, :], SST[:, bb, blk, :], start=False, stop=last),
                                 reads=["QM", "SST"], writes=[bk(6)])
                    S.dve(TT(KDM[:], bc(KD[:].unsqueeze(1), [128, 4, 256]),
                             bc(C["rowmask"][:, 4 * g:4 * g + 4].unsqueeze(2), [128, 4, 256]), MUL),
                          reads=["KD", "C_rowmask"], writes=["KDM"])
                    for bb in range(4):
                        for blk in range(2):
                            S.pe(MM(PS[:, 4 + bb // 2, (bb % 2) * 256 + blk * 128:(bb % 2) * 256 + (blk + 1) * 128],
                                    KDM[:, bb, blk * 128:(blk + 1) * 128], TM[:, 512 + blk * 128:512 + (blk + 1) * 128]),
                                 reads=["KDM", "TM"], writes=[bk(4 + bb // 2)])
                    psv = PS[:, 4:6, :].rearrange("p a (b blk n) -> p (a b) blk n", b=2, blk=2)
                    for blk in range(2):
                        for hl in range(2):
                            sl = slice(64 * hl, 64 * hl + 64)
                            S.dve(STT(SST[sl, :, blk, sl], SST[sl, :, blk, sl], C["ret_a"][sl, 1, blk:blk + 1],
                                      psv[sl, :, blk, sl], MUL, ADD),
                                  reads=["SST", "C_ret_a", bk(4), bk(5)], writes=["SST"])
                    store_sst("s_ret", l, g)
            S.act(ACT(OB[:], PS[:, 6, 0:256], AF.Copy), reads=[bk(6)], writes=["OB"])
            S.pool(TT(OB2[:], OB[:], OB[:], MUL), reads=["OB"], writes=["OB2"])
            S.dve(lambda e: e.tensor_reduce(out=STA[:, 4:8], in_=OB[:].rearrange("p (h v) -> p h v", h=4), axis=AX.X, op=ADD),
                  reads=["OB"], writes=["STA"])
            S.dve(lambda e: e.tensor_reduce(out=STA[:, 8:12], in_=OB2[:].rearrange("p (h v) -> p h v", h=4), axis=AX.X, op=ADD),
                  reads=["OB2"], writes=["STA"])
            S.dve(TS(STA[:, 4:8], STA[:, 4:8], 1.0 / 64, None, MUL), reads=["STA"], writes=["STA"])
            S.dve(TT(STB[:, 0:4], STA[:, 4:8], STA[:, 4:8], MUL), reads=["STA"], writes=["STB"])
            S.dve(STT(STB[:, 0:4], STA[:, 8:12], 1.0 / 64, STB[:, 0:4], MUL, SUB), reads=["STA", "STB"], writes=["STB"])
            S.dve(TS(STB[:, 0:4], STB[:, 0:4], 0.0, None, ALU.max), reads=["STB"], writes=["STB"])
            S.act(ACT(STB[:, 0:4], STB[:, 0:4], AF.Ln, bias=EPS), reads=["STB"], writes=["STB"])
            S.act(ACT(STB[:, 0:4], STB[:, 0:4], AF.Exp, scale=-0.5), reads=["STB"], writes=["STB"])
            ov = OB[:].rearrange("p (h v) -> p h v", h=4)
            S.dve(TT(ov, ov, bc(STA[:, 4:8].unsqueeze(2), [128, 4, 64]), SUB), reads=["OB", "STA"], writes=["OB"])
            S.dve(TT(ov, ov, bc(STB[:, 0:4].unsqueeze(2), [128, 4, 64]), MUL), reads=["OB", "STB"], writes=["OB"])
            S.dve(TT(MIX[:, 0:256], OB[:], SGR[:], MUL), reads=["OB", "SGR"], writes=["MIX"])
            if var == 0 and i == LASTP:
                dst = dout["p_ret"][l].rearrange("(blk hl) k v -> hl k blk v", hl=2)
                for hl in range(2):
                    sl = slice(64 * hl, 64 * hl + 64)
                    dma_out("sp", dst[hl], SRET[sl, l, :, sl], "SRET")

        def hgrn(l, i, var):
            nb, e8 = (1, 128) if var == 0 else (SB, SL)
            HBv = HB[:].rearrange("p k (b e) -> p k b e", e=e8)
            S.act(ACT(HE[:], FM[:, 2:4, :], AF.Exp, scale=-1.0), reads=["FM"], writes=["HE"])
            S.act(ACT(HSIG[:], HE[:], AF.Ln, bias=1.0), reads=["HE"], writes=["HSIG"])
            S.act(ACT(HSIG[:], HSIG[:], AF.Exp, scale=-1.0), reads=["HSIG"], writes=["HSIG"])
            for blk in range(2):
                S.dve(TS(HD[:, blk, :], HSIG[:, blk, :], OML[:, l, blk:blk + 1], LOW[:, l, blk:blk + 1], MUL, ADD),
                      reads=["HSIG", "OML", "LOW"], writes=["HD"])
            S.act(ACT(HD[:], HD[:], AF.Ln), reads=["HD"], writes=["HD"])
            S.dve(TT(HKEY[:], HE[:], HSIG[:], MUL), reads=["HE", "HSIG"], writes=["HKEY"])
            S.dve(TT(HKEY[:], HKEY[:], bc(OML[:, l, :].unsqueeze(2), [128, 2, 128]), MUL), reads=["HKEY", "OML"], writes=["HKEY"])
            for blk in range(2):
                S.dve(lambda e, blk=blk: e.tensor_tensor_scan(out=HB[:, blk, :], data0=C["rst"][:, var, :], data1=HD[:, blk, :],
                                                              initial=0.0, op0=MUL, op1=ADD),
                      reads=["HD", "C_rst"], writes=["HB"])
            HB32 = HB[:].rearrange("p k (i e) -> p k i e", e=32)
            S.dve(TS(HREF[:, :, 1:4], HB32[:, :, 0:3, 31], -1.0, None, MUL), reads=["HB"], writes=["HREF"])
            S.dve(TT(HKV[:], bc(HB[:].unsqueeze(1), [128, 4, 2, 128]),
                     bc(HREF[:].rearrange("p k i -> p i k").unsqueeze(3), [128, 4, 2, 128]), ADD), reads=["HB", "HREF"], writes=["HKV"])
            S.dve(TS(HKV[:], HKV[:], -60.0, None, ALU.max), reads=["HKV"], writes=["HKV"])
            S.dve(TT(HKV4[:].rearrange("p k (b e) -> p k b e", e=e8), HBv, bc(HBv[:, :, :, e8 - 1:e8], [128, 2, nb, e8]), SUB),
                  reads=["HB"], writes=["HKV4"])
            S.act(ACT(HKV[:], HKV[:], AF.Exp, scale=-1.0), reads=["HKV"], writes=["HKV"])
            S.act(ACT(HKV4[:], HKV4[:], AF.Exp, scale=-1.0), reads=["HKV4"], writes=["HKV4"])
            S.dve(TT(HKVB[:], HKV[:], bc(HKEY[:].unsqueeze(1), [128, 4, 2, 128]), MUL), reads=["HKV", "HKEY"], writes=["HKVB"])
            S.dve(TT(HKV4[:], HKV4[:], HKEY[:], MUL), reads=["HKV4", "HKEY"], writes=["HKV4"])
            S.act(ACT(HEX[:], HB[:], AF.Exp), reads=["HB"], writes=["HEX"])
            S.dve(TT(HQD[:], HQ[:], HEX[:], MUL), reads=["HQ", "HEX"], writes=["HQD"])
            S.dve(TT(HQA[:].rearrange("p k (i e) -> p k i e", e=32), HB32, bc(HREF[:].unsqueeze(3), [128, 2, 4, 32]), ADD),
                  reads=["HB", "HREF"], writes=["HQA"])
            S.act(ACT(HQA[:], HQA[:], AF.Exp), reads=["HQA"], writes=["HQA"])
            S.dve(TT(HQA[:], HQA[:], HQ[:], MUL), reads=["HQA", "HQ"], writes=["HQA"])
            S.dve(TT(QH[:], bc(HQA[:].unsqueeze(2), [128, 2, 2, 128]), bc(C["hmask"][:].unsqueeze(1).unsqueeze(3), [128, 2, 2, 128]), MUL),
                  reads=["HQA", "C_hmask"], writes=["QH"])
            for h in range(4):
                blk, hl = h // 2, h % 2
                for iv in range(4):
                    S.pe(MM(PS[:, 5, h * 128 + 32 * iv:h * 128 + 32 * iv + 32], HKVB[:, iv, blk, :],
                            QH[:, blk, hl, 32 * iv:32 * iv + 32]), reads=["HKVB", "QH"], writes=[bk(5)])
            S.dve(TT(PTB[:], PS[:, 5, :].rearrange("p (c n) -> p c n", c=4), bc(C["causal"][:, var:var + 1, :], [128, 4, 128]), MUL),
                  reads=[bk(5), "C_causal"], writes=["PTB"])
            VOFF = 1024
            for h in range(4):
                S.pe(MM(PS[:, 6, h * 64:(h + 1) * 64], PTB[:, h, :], VB[:, 256 + h * 64:256 + (h + 1) * 64], start=(h == 0), stop=False),
                     reads=["PTB", "VB"], writes=[bk(6)])
            for blk in range(2):
                S.pe(TR(PS[:, 4, blk * 128:(blk + 1) * 128], HKV4[:, blk, :]), reads=["HKV4", "C_ident"], writes=[bk(4)])
            S.act(ACT(HKT[:], PS[:, 4, 0:256], AF.Copy), reads=[bk(4)], writes=["HKT"])
            S.act(ACT(EBL[:, :, 0:nb], HBv[:, :, :, e8 - 1], AF.Exp), reads=["HB"], writes=["EBL"])
            if var == 0:
                for blk in range(2):
                    S.pe(MM(PS[:, 6, blk * 128:(blk + 1) * 128], HQD[:, blk, :], SHG[:, l, blk, :], start=False, stop=(blk == 1)),
                         reads=["HQD", "SHG"], writes=[bk(6)])
                for blk in range(2):
                    S.pe(MM(PS[:, 7, blk * 128:(blk + 1) * 128], HKT[:, blk * 128:(blk + 1) * 128], TM[:, VOFF + blk * 128:VOFF + (blk + 1) * 128]),
                         reads=["HKT", "TM"], writes=[bk(7)])
                for blk in range(2):
                    for hl in range(2):
                        sl = slice(64 * hl, 64 * hl + 64)
                        S.dve(STT(SHG[sl, l, blk, sl], SHG[sl, l, blk, sl], EBL[sl, blk, 0:1],
                                  PS[sl, 7, blk * 128 + 64 * hl:blk * 128 + 64 * hl + 64], MUL, ADD),
                              reads=["SHG", "EBL", bk(7)], writes=["SHG"])
            else:
                for g in range(4):
                    load_sst("st_hg", l, g)
                    for blk in range(2):
                        S.dve(TT(QM[:], bc(HQD[:, blk, :].unsqueeze(1), [128, 4, 128]), C["colmask"][:, 4 * g:4 * g + 4, :], MUL),
                              reads=["HQD", "C_colmask"], writes=["QM"])
                        for bb in range(4):
                            last = (g == 3 and blk == 1 and bb == 3)
                            S.pe(MM(PS[:, 6, blk * 128:(blk + 1) * 128], QM[:, bb, :], SST[:, bb, blk, :], start=False, stop=last),
                                 reads=["QM", "SST"], writes=[bk(6)])
                    S.dve(TT(KDM[:], bc(HKT[:].unsqueeze(1), [128, 4, 256]),
                             bc(C["rowmask"][:, 4 * g:4 * g + 4].unsqueeze(2), [128, 4, 256]), MUL),
                          reads=["HKT", "C_rowmask"], writes=["KDM"])
                    for bb in range(4):
                        for blk in range(2):
                            S.pe(MM(PS[:, 2 + bb // 2, (bb % 2) * 256 + blk * 128:(bb % 2) * 256 + (blk + 1) * 128],
                                    KDM[:, bb, blk * 128:(blk + 1) * 128], TM[:, VOFF + blk * 128:VOFF + (blk + 1) * 128]),
                                 reads=["KDM", "TM"], writes=[bk(2 + bb // 2)])
                    psv = PS[:, 2:4, :].rearrange("p a (b blk n) -> p (a b) blk n", b=2, blk=2)
                    for blk in range(2):
                        for hl in range(2):
                            sl = slice(64 * hl, 64 * hl + 64)
                            S.dve(TT(SST[sl, :, blk, sl], SST[sl, :, blk, sl], bc(EBL[sl, blk, 4 * g:4 * g + 4].unsqueeze(2), [64, 4, 64]), MUL),
                                  reads=["SST", "EBL"], writes=["SST"])
                            S.dve(TT(SST[sl, :, blk, sl], SST[sl, :, blk, sl], psv[sl, :, blk, sl], ADD),
                                  reads=["SST", bk(2), bk(3)], writes=["SST"])
                    store_sst("s_hg", l, g)
            S.act(ACT(OB[:], PS[:, 6, 0:256], AF.Copy), reads=[bk(6)], writes=["OB"])
            S.pool(TT(OB2[:], OB[:], OB[:], MUL), reads=["OB"], writes=["OB2"])
            S.dve(lambda e: e.tensor_reduce(out=STA[:, 8:12], in_=OB2[:].rearrange("p (h v) -> p h v", h=4), axis=AX.X, op=ADD),
                  reads=["OB2"], writes=["STA"])
            S.act(ACT(STB[:, 0:4], STA[:, 8:12], AF.Ln, scale=1.0 / 64, bias=EPS), reads=["STA"], writes=["STB"])
            S.act(ACT(STB[:, 0:4], STB[:, 0:4], AF.Exp, scale=-0.5), reads=["STB"], writes=["STB"])
            ov = OB[:].rearrange("p (h v) -> p h v", h=4)
            S.dve(TT(ov, ov, bc(STB[:, 0:4].unsqueeze(2), [128, 4, 64]), MUL), reads=["OB", "STB"], writes=["OB"])
            S.dve(TT(MIX[:, 256:512], OB[:], SGH[:], MUL), reads=["OB", "SGH"], writes=["MIX"])
            if var == 0 and i == LASTP:
                dst = dout["p_hg"][l].rearrange("(blk hl) k v -> hl k blk v", hl=2)
                for hl in range(2):
                    sl = slice(64 * hl, 64 * hl + 64)
                    dma_out("sp", dst[hl], SHG[sl, l, :, sl], "SHG")

        def conv_phase(l, i, var):
            if var == 0:
                S.pool(CP(XPW[:, :, 0:3], XPH[:, l]), reads=["XPH"], writes=["XPW"])
                S.pool(CP(XPW[:, :, 3:131], FM[:, 4:10, :]), reads=["FM"], writes=["XPW"])
                for j in range(4):
                    src = XPW[:, :, j:j + 128]
                    w = bc(CW[:, l, :, j:j + 1], [128, 6, 128])
                    if j == 0:
                        S.pool(TT(XC[:], src, w, MUL), reads=["XPW", "CW"], writes=["XC"])
                    else:
                        S.pool(TT(CTMPP[:], src, w, MUL), reads=["XPW", "CW"], writes=["CTMPP"])
                        S.pool(TT(XC[:], XC[:], CTMPP[:], ADD), reads=["XC", "CTMPP"], writes=["XC"])
                S.pool(CP(XPH[:, l], XPW[:, :, 128:131]), reads=["XPW"], writes=["XPH"])
                if i == LASTP:
                    for j in range(3):
                        S.dma("sp", lambda e, j=j: e.dma_start(out=dout["p_conv"][l, j].rearrange("(b p) -> p b", p=128),
                                                               in_=XPH[:, l, :, j], allow_slow_non_contiguous=True),
                              reads=["XPH"], dkey=("st", "XPH"))
            else:
                dma_in("sp", CVT[0:48, :], din["st_conv"][l], "CVT")
                for c6 in range(6):
                    S.pe(TR(PS[:, 4, c6 * 48:(c6 + 1) * 48], CVT[0:48, c6 * 128:(c6 + 1) * 128], n=48), reads=["CVT", "C_ident"], writes=[bk(4)])
                S.act(ACT(XPS[:, :, :, 0:3], PS[:, 4, 0:288].rearrange("p (c b j) -> p c b j", c=6, b=SB), AF.Copy),
                      reads=[bk(4)], writes=["XPS"])
                S.pool(CP(XPS[:, :, :, 3:11], FM[:, 4:10, :].rearrange("p c (b e) -> p c b e", e=SL)), reads=["FM"], writes=["XPS"])
                cav = CACC[:].rearrange("p c (b e) -> p c b e", e=SL)
                ctv = CTMPS[:].rearrange("p c (b e) -> p c b e", e=SL)
                for j in range(4):
                    src = XPS[:, :, :, j:j + SL]
                    w = bc(CW[:, l, :, j:j + 1].unsqueeze(3), [128, 6, SB, SL])
                    if j == 0:
                        S.pool(TT(cav, src, w, MUL), reads=["XPS", "CW"], writes=["CACC"])
                    else:
                        S.pool(TT(ctv, src, w, MUL), reads=["XPS", "CW"], writes=["CTMPS"])
                        S.pool(TT(CACC[:], CACC[:], CTMPS[:], ADD), reads=["CACC", "CTMPS"], writes=["CACC"])
                S.pool(CP(CTMPS[:, :, 0:48].rearrange("p c (b j) -> p c b j", j=3), XPS[:, :, :, 8:11]), reads=["XPS", "CACC"], writes=["CTMPS"])
                for c6 in range(6):
                    bnk, off = (5, c6 * 128) if c6 < 4 else (6, (c6 - 4) * 128)
                    S.pe(TR(PS[0:48, bnk, off:off + 128], CTMPS[:, c6, 0:48]), reads=["CTMPS", "C_ident"], writes=[bk(bnk)])
                S.act(ACT(CVT[0:48, 0:512], PS[0:48, 5, :], AF.Copy), reads=[bk(5)], writes=["CVT"])
                S.act(ACT(CVT[0:48, 512:768], PS[0:48, 6, 0:256], AF.Copy), reads=[bk(6)], writes=["CVT"])
                dma_out("sp", dout["s_conv"][l], CVT[0:48, :], "CVT")

        def conv_silu(l, var):
            for c6 in range(6):
                if var == 0:
                    S.act(ACT(XC[:, c6, :], XC[:, c6, :], AF.Silu, bias=CB[:, l, c6:c6 + 1]), reads=["XC", "CB"], writes=["XC"])
                else:
                    S.act(ACT(XC[:, c6, :], CACC[:, c6, :], AF.Silu, bias=CB[:, l, c6:c6 + 1]), reads=["CACC", "CB"], writes=["XC"])

        def silu_phase(l):
            S.act(ACT(SGR[:], TM[:, 768:1024], AF.Silu), reads=["TM"], writes=["SGR"])
            S.act(ACT(SGH[:], TM[:, 1280:1536], AF.Silu), reads=["TM"], writes=["SGH"])
            S.act(ACT(SSZ[:], TM[:, 1536:1792], AF.Silu), reads=["TM"], writes=["SSZ"])
            S.act(ACT(HQ[:], FM[:, 0:2, :], AF.Silu), reads=["FM"], writes=["HQ"])

        def ssd(l, i, var):
            nb, e8 = (1, 128) if var == 0 else (SB, SL)
            S.dve(TT(DEL[:], TM[:, 1792:1796], DTB[:, l, :], ADD), reads=["TM", "DTB"], writes=["DEL"])
            S.act(ACT(DEL[:], DEL[:], AF.Exp), reads=["DEL"], writes=["DEL"])
            S.act(ACT(DEL[:], DEL[:], AF.Ln, bias=1.0), reads=["DEL"], writes=["DEL"])
            S.act(ACT(LND[:], DEL[:], AF.Ln), reads=["DEL"], writes=["LND"])
            S.dve(TT(LOGA[:], DEL[:], NEGA[:, l, :], MUL), reads=["DEL", "NEGA"], writes=["LOGA"])
            S.pe(MM(PS[:, 7, 0:4], C["causal"][:, var, :], LOGA[:]), reads=["C_causal", "LOGA"], writes=[bk(7)])
            S.pe(MM(PS[:, 7, 4:8], C["sameseq"][:, var, :], LOGA[:]), reads=["C_sameseq", "LOGA"], writes=[bk(7)])
            S.dve(TT(BCOL[:], PS[:, 7, 0:4], LND[:], SUB), reads=[bk(7), "LND"], writes=["BCOL"])
            S.dve(TT(BLAST[:], PS[:, 7, 4:8], BCOL[:], SUB), reads=[bk(7), "BCOL"], writes=["BLAST"])
            S.act(ACT(KSC[:], BLAST[:], AF.Exp), reads=["BLAST"], writes=["KSC"])
            S.pool(CP(LB[:], bc(LOGA[:].unsqueeze(2), [128, 4, 128])), reads=["LOGA"], writes=["LB"])
            for h in range(4):
                S.pe(MM(PS[:, 4, h * 128:(h + 1) * 128], LB[:, h, :], C["causal"][:, var, :]), reads=["LB", "C_causal"], writes=[bk(4)])
            p4 = PS[:, 4, :].rearrange("p (c n) -> p c n", c=4)
            S.dve(TT(D1[:], p4, bc(BCOL[:].unsqueeze(2), [128, 4, 128]), SUB), reads=[bk(4), "BCOL"], writes=["D1"])
            S.pool(TT(D1[:], D1[:], bc(C["negmask"][:, var:var + 1, :], [128, 4, 128]), ADD), reads=["D1", "C_negmask"], writes=["D1"])
            S.act(ACT(DEC[:], D1[:], AF.Exp), reads=["D1"], writes=["DEC"])
            S.act(ACT(EBT[:], p4, AF.Exp), reads=[bk(4)], writes=["EBT"])
            for g in range(2):
                S.pe(MM(PS[:, 5, g * 128:(g + 1) * 128], XC[:, 2 + g, :], XC[:, 4 + g, :]), reads=["XC"], writes=[bk(5)])
            for g in range(2):
                S.dve(TT(PT2B[:, 2 * g:2 * g + 2, :], bc(PS[:, 5, g * 128:(g + 1) * 128].unsqueeze(1), [128, 2, 128]), DEC[:, 2 * g:2 * g + 2, :], MUL),
                      reads=[bk(5), "DEC"], writes=["PT2B"])
                S.pool(TT(CMD[:, 2 * g:2 * g + 2, :], bc(XC[:, 4 + g, :].unsqueeze(1), [128, 2, 128]), EBT[:, 2 * g:2 * g + 2, :], MUL),
                       reads=["XC", "EBT"], writes=["CMD"])
            for b4 in range(4):
                S.pe(TR(PS[:, 6, b4 * 128:(b4 + 1) * 128], XC[:, b4, :]), reads=["XC", "C_ident"], writes=[bk(6)])
            S.act(ACT(XS[:], PS[:, 6, 0:256], AF.Copy), reads=[bk(6)], writes=["XS"])
            S.act(ACT(XSB[:], PS[:, 6, 0:256], AF.Copy), reads=[bk(6)], writes=["XSB"])
            S.act(ACT(BMT[:], PS[:, 6, 256:512].rearrange("p (g n) -> p g n", g=2), AF.Copy), reads=[bk(6)], writes=["BMT"])
            for g in range(2):
                S.pool(TT(BD[:, 2 * g:2 * g + 2, :], bc(BMT[:, g, :].unsqueeze(1), [128, 2, 128]),
                          bc(KSC[:, 2 * g:2 * g + 2].unsqueeze(2), [128, 2, 128]), MUL), reads=["BMT", "KSC"], writes=["BD"])
            for h in range(4):
                S.pe(MM(PS[:, 7, h * 64:(h + 1) * 64], PT2B[:, h, :], XSB[:, h * 64:(h + 1) * 64], start=(h == 0), stop=False),
                     reads=["PT2B", "XSB"], writes=[bk(7)])
            if var == 0:
                for h in range(4):
                    S.pe(MM(PS[:, 7, h * 64:(h + 1) * 64], CMD[:, h, :], SSSD[:, l, h, :], start=False, stop=(h == 3)),
                         reads=["CMD", "SSSD"], writes=[bk(7)])
                for h in range(4):
                    S.pe(MM(PS[:, 5, h * 64:(h + 1) * 64], BD[:, h, :], XS[:, h * 64:(h + 1) * 64]), reads=["BD", "XS"], writes=[bk(5)])
                S.dve(TT(SSSD[:, l], SSSD[:, l], bc(EBT[:, :, 127:128], [128, 4, 64]), MUL), reads=["SSSD", "EBT"], writes=["SSSD"])
                S.dve(TT(SSSD[:, l], SSSD[:, l], PS[:, 5, 0:256].rearrange("p (h v) -> p h v", h=4), ADD),
                      reads=["SSSD", bk(5)], writes=["SSSD"])
            else:
                EBTv = EBT[:].rearrange("p h (b e) -> p h b e", e=SL)
                for g in range(4):
                    dma_in("sp", SST2[:], din["st_ssd"][l, 4 * g:4 * g + 4].rearrange("b h n v -> n b h v"), "SST2")
                    for h in range(4):
                        S.dve(TT(QM[:], bc(CMD[:, h, :].unsqueeze(1), [128, 4, 128]), C["colmask"][:, 4 * g:4 * g + 4, :], MUL),
                              reads=["CMD", "C_colmask"], writes=["QM"])
                        for bb in range(4):
                            last = (g == 3 and h == 3 and bb == 3)
                            S.pe(MM(PS[:, 7, h * 64:(h + 1) * 64], QM[:, bb, :], SST2[:, bb, h, :], start=False, stop=last),
                                 reads=["QM", "SST2"], writes=[bk(7)])
                        S.dve(TT(BDM[:], bc(BD[:, h, :].unsqueeze(1), [128, 4, 128]),
                                 bc(C["rowmask"][:, 4 * g:4 * g + 4].unsqueeze(2), [128, 4, 128]), MUL),
                              reads=["BD", "C_rowmask"], writes=["BDM"])
                        for bb in range(4):
                            col = ((bb % 2) * 4 + h) * 64
                            S.pe(MM(PS[:, 2 + bb // 2, col:col + 64], BDM[:, bb, :], XS[:, h * 64:(h + 1) * 64]),
                                 reads=["BDM", "XS"], writes=[bk(2 + bb // 2)])
                    psv = PS[:, 2:4, :].rearrange("p a (b h v) -> p (a b) h v", b=2, h=4)
                    ach = EBTv[:, :, 4 * g:4 * g + 4, SL - 1].rearrange("p h b -> p b h")
                    S.dve(TT(SST2[:], SST2[:], bc(ach.unsqueeze(3), [128, 4, 4, 64]), MUL), reads=["SST2", "EBT"], writes=["SST2"])
                    S.dve(TT(SST2[:], SST2[:], psv, ADD), reads=["SST2", bk(2), bk(3)], writes=["SST2"])
                    dma_out("sp", dout["s_ssd"][l, 4 * g:4 * g + 4].rearrange("b h n v -> n b h v"), SST2[:], "SST2")
            S.act(ACT(OB[:], PS[:, 7, 0:256], AF.Copy), reads=[bk(7)], writes=["OB"])
            ov = OB[:].rearrange("p (h v) -> p h v", h=4)
            S.pool(TT(OB2[:].rearrange("p (h v) -> p h v", h=4), XS[:].rearrange("p (h v) -> p h v", h=4),
                      bc(SSDD[:, l, :].unsqueeze(2), [128, 4, 64]), MUL), reads=["XS", "SSDD"], writes=["OB2"])
            S.dve(TT(OB[:], OB[:], OB2[:], ADD), reads=["OB", "OB2"], writes=["OB"])
            S.dve(TT(OB[:], OB[:], SSZ[:], MUL), reads=["OB", "SSZ"], writes=["OB"])
            S.pool(TT(OB2[:], OB[:], OB[:], MUL), reads=["OB"], writes=["OB2"])
            S.dve(lambda e: e.tensor_reduce(out=STA[:, 8:10], in_=OB2[:].rearrange("p (g v) -> p g v", g=2), axis=AX.X, op=ADD),
                  reads=["OB2"], writes=["STA"])
            S.act(ACT(STB[:, 0:2], STA[:, 8:10], AF.Ln, scale=1.0 / 128, bias=EPS), reads=["STA"], writes=["STB"])
            S.act(ACT(STB[:, 0:2], STB[:, 0:2], AF.Exp, scale=-0.5), reads=["STB"], writes=["STB"])
            og = OB[:].rearrange("p (g v) -> p g v", g=2)
            S.dve(TT(og, og, bc(STB[:, 0:2].unsqueeze(2), [128, 2, 128]), MUL), reads=["OB", "STB"], writes=["OB"])
            S.dve(TT(MIX[:, 512:768], OB[:], SNORM[:], MUL), reads=["OB", "SNORM"], writes=["MIX"])
            if var == 0 and i == LASTP:
                dma_out("sp", dout["p_ssd"][l].rearrange("h n v -> n h v"), SSSD[:, l], "SSSD")

        def s5(l, i, var, hook=None):
            nb, e8 = (1, 128) if var == 0 else (SB, SL)
            if var == 1:
                for ri, nm in ((0, "st_s5re"), (1, "st_s5im")):
                    dma_in("sp", ZT[0:16, :], din[nm][l], "ZT")
                    for c in range(8):
                        S.pe(TR(PS[:, 4, c * 16:(c + 1) * 16], ZT[0:16, c * 128:(c + 1) * 128], n=16), reads=["ZT", "C_ident"], writes=[bk(4)])
                    S.act(ACT(Z0[:, ri], PS[:, 4, 0:128].rearrange("p (c b) -> p c b", b=SB), AF.Copy), reads=[bk(4)], writes=["Z0"])
            if var == 0:
                zr, zi, zk = ZS5[:, l, 0, :].unsqueeze(2), ZS5[:, l, 1, :].unsqueeze(2), "ZS5"
            else:
                zr, zi, zk = Z0[:, 0], Z0[:, 1], "Z0"
            ar = bc(ABAR[:, 0, :].unsqueeze(2), [128, 8, nb])
            ai = bc(ABAR[:, 1, :].unsqueeze(2), [128, 8, nb])
            S.dve(TT(INJ[:, 0, :, 0:nb], zr, ar, MUL), reads=[zk, "ABAR"], writes=["INJ"])
            S.dve(TT(INJ2[:, 0, :, 0:nb], zi, ai, MUL), reads=[zk, "ABAR"], writes=["INJ2"])
            S.dve(TT(INJ[:, 0, :, 0:nb], INJ[:, 0, :, 0:nb], INJ2[:, 0, :, 0:nb], SUB), reads=["INJ", "INJ2"], writes=["INJ"])
            S.dve(TT(INJ[:, 1, :, 0:nb], zi, ar, MUL), reads=[zk, "ABAR"], writes=["INJ"])
            S.dve(TT(INJ2[:, 1, :, 0:nb], zr, ai, MUL), reads=[zk, "ABAR"], writes=["INJ2"])
            S.dve(TT(INJ[:, 1, :, 0:nb], INJ[:, 1, :, 0:nb], INJ2[:, 1, :, 0:nb], ADD), reads=["INJ", "INJ2"], writes=["INJ"])
            S.dve(TT(PI[:], bc(RR[:].unsqueeze(2), [128, 8, 128]), bc(C["rst"][:, var:var + 1, :], [128, 8, 128]), MUL),
                  reads=["RR", "C_rst"], writes=["PI"])
            for ri in range(2):
                for c in range(8):
                    bnk = 4 + 2 * ri + c // 4
                    S.pe(MM(PS[:, bnk, (c % 4) * 128:(c % 4 + 1) * 128], BCF[:, ri, c, :], FM[:, 10 + c // 4, :]),
                         reads=["BCF", "FM"], writes=[bk(bnk)])
            pre = PS[:, 4:6, :].rearrange("p a (c n) -> p (a c) n", n=128)
            pim = PS[:, 6:8, :].rearrange("p a (c n) -> p (a c) n", n=128)
            S.dve(TT(T1[:], pre, COSR[:], MUL), reads=[bk(4), bk(5), "COSR"], writes=["T1"])
            S.dve(TT(T2[:], pim, SINR[:], MUL), reads=[bk(6), bk(7), "SINR"], writes=["T2"])
            S.pool(TT(XR[:], T1[:], T2[:], ADD), reads=["T1", "T2"], writes=["XR"])
            S.dve(TT(WR[:], pim, COSR[:], MUL), reads=[bk(6), bk(7), "COSR"], writes=["WR"])
            S.dve(TT(WI[:], pre, SINR[:], MUL), reads=[bk(4), bk(5), "SINR"], writes=["WI"])
            S.pool(TT(XI[:], WR[:], WI[:], SUB), reads=["WR", "WI"], writes=["XI"])
            xrv = XR[:].rearrange("p c (b e) -> p c b e", e=e8)
            xiv = XI[:].rearrange("p c (b e) -> p c b e", e=e8)
            S.dve(TT(xrv[:, :, :, 0], xrv[:, :, :, 0], INJ[:, 0, :, 0:nb], ADD), reads=["XR", "INJ"], writes=["XR"])
            S.dve(TT(xiv[:, :, :, 0], xiv[:, :, :, 0], INJ[:, 1, :, 0:nb], ADD), reads=["XI", "INJ"], writes=["XI"])
            fl = "p c t -> p (c t)"
            S.dve(lambda e: e.tensor_tensor_scan(out=WR[:].rearrange(fl), data0=PI[:].rearrange(fl), data1=XR[:].rearrange(fl),
                                                 initial=0.0, op0=MUL, op1=ADD), reads=["PI", "XR"], writes=["WR"])
            S.dve(lambda e: e.tensor_tensor_scan(out=WI[:].rearrange(fl), data0=PI[:].rearrange(fl), data1=XI[:].rearrange(fl),
                                                 initial=0.0, op0=MUL, op1=ADD), reads=["PI", "XI"], writes=["WI"])
            if hook is not None:
                hook()
            S.dve(TT(PR[:], WR[:], COSR[:], MUL), reads=["WR", "COSR"], writes=["PR"])
            S.dve(TT(PI[:], WI[:], SINR[:], MUL), reads=["WI", "SINR"], writes=["PI"])
            S.dve(TT(T2[:], WR[:], SINR[:], MUL), reads=["WR", "SINR"], writes=["T2"])
            S.dve(TT(T1[:], WI[:], COSR[:], MUL), reads=["WI", "COSR"], writes=["T1"])
            if var == 0:
                S.dve(TT(ZS5[:, l, 0, :], PR[:, :, 127], PI[:, :, 127], SUB), reads=["PR", "PI"], writes=["ZS5"])
                S.dve(TT(ZS5[:, l, 1, :], T1[:, :, 127], T2[:, :, 127], ADD), reads=["T1", "T2"], writes=["ZS5"])
            else:
                def lastv(t_):
                    return t_[:].rearrange("p c (b e) -> p c b e", e=SL)[:, :, :, SL - 1]
                S.dve(TT(ZL[:, 0], lastv(PR), lastv(PI), SUB), reads=["PR", "PI"], writes=["ZL"])
                S.dve(TT(ZL[:, 1], lastv(T1), lastv(T2), ADD), reads=["T1", "T2"], writes=["ZL"])
            S.dve(TT(XRB[:], PR[:], PI[:], SUB), reads=["PR", "PI"], writes=["XRB"])
            S.dve(TT(XIB[:], T1[:], T2[:], ADD), reads=["T1", "T2"], writes=["XIB"])
            for c in range(8):
                S.pe(MM(PS[:, 4, 32 * c:32 * c + 32], XRB[:, c, :], S5CB[:, c, :], start=(c == 0), stop=False), reads=["XRB", "S5CB"], writes=[bk(4)])
                S.pe(MM(PS[:, 4, 32 * c:32 * c + 32], XIB[:, c, :], S5CIB[:, c, :], start=False, stop=False), reads=["XIB", "S5CIB"], writes=[bk(4)])
            for ub in range(2):
                S.pe(MM(PS[:, 4, ub * 128:(ub + 1) * 128], FM[:, 10 + ub, :], DIAGD[:, ub, :], start=False, stop=(ub == 1)),
                     reads=["FM", "DIAGD"], writes=[bk(4)])
            S.act(ACT(YG[:], PS[:, 4, 0:256], AF.Copy), reads=[bk(4)], writes=["YG"])
            S.act(ACT(YU[:], PS[:, 4, 0:256], AF.Square), reads=[bk(4)], writes=["YU"])
            S.dve(TS(YU[:], YU[:], 0.044715, 1.0, MUL, ADD), reads=["YU"], writes=["YU"])
            S.dve(TT(YU[:], YU[:], YG[:], MUL), reads=["YU", "YG"], writes=["YU"])
            S.dve(TS(YU[:], YU[:], -26.0, None, ALU.max), reads=["YU"], writes=["YU"])
            S.act(ACT(YU[:], YU[:], AF.Exp, scale=-2.0 * math.sqrt(2.0 / math.pi)), reads=["YU"], writes=["YU"])
            S.act(ACT(YU[:], YU[:], AF.Ln, bias=1.0), reads=["YU"], writes=["YU"])
            S.act(ACT(YU[:], YU[:], AF.Exp, scale=-1.0), reads=["YU"], writes=["YU"])
            S.dve(TT(YE[:], YG[:], YU[:], MUL), reads=["YG", "YU"], writes=["YE"])
            for ub in range(2):
                S.pe(TR(PS[:, 5, ub * 128:(ub + 1) * 128], YE[:, ub * 128:(ub + 1) * 128]), reads=["YE", "C_ident"], writes=[bk(5)])
            S.act(ACT(YGT[:], PS[:, 5, 0:256].rearrange("p (k n) -> p k n", k=2), AF.Copy), reads=[bk(5)], writes=["YGT"])
            for k in range(2):
                S.pe(MM(PS[:, 6, 0:256], YGT[:, k, :], GLUW[:, k, :], start=(k == 0), stop=(k == 1)), reads=["YGT", "GLUW"], writes=[bk(6)])
            S.dve(STT(YU[:], PS[:, 6, 0:256], -40.0, GLUB[:], ALU.max, ADD), reads=[bk(6), "GLUB"], writes=["YU"])
            S.act(ACT(YU[:], YU[:], AF.Exp, scale=-1.0), reads=["YU"], writes=["YU"])
            S.act(ACT(YU[:], YU[:], AF.Ln, bias=1.0), reads=["YU"], writes=["YU"])
            S.act(ACT(YU[:], YU[:], AF.Exp, scale=-1.0), reads=["YU"], writes=["YU"])
            S.dve(TT(MIX[:, 768:1024], YE[:], YU[:], MUL), reads=["YE", "YU"], writes=["MIX"])
            if var == 0 and i == LASTP:
                for ri in range(2):
                    S.pe(TR(PS[0:8, 7, ri * 128:(ri + 1) * 128], ZS5[:, l, ri, :]), reads=["ZS5", "C_ident"], writes=[bk(7)])
                S.act(ACT(ZT[0:8, 0:256], PS[0:8, 7, 0:256], AF.Copy), reads=[bk(7)], writes=["ZT"])
                dma_out("sp", dout["p_s5re"][l], ZT[0:8, 0:128], "ZT")
                dma_out("sp", dout["p_s5im"][l], ZT[0:8, 128:256], "ZT")
            if var == 1:
                for ri, nm in ((0, "s_s5re"), (1, "s_s5im")):
                    for c in range(8):
                        S.pe(TR(PS[0:16, 6 + c // 4, (c % 4) * 128:(c % 4 + 1) * 128], ZL[:, ri, c, :]), reads=["ZL", "C_ident"],
                             writes=[bk(6 + c // 4)])
                    S.act(ACT(ZT[0:16, 0:512], PS[0:16, 6, :], AF.Copy), reads=[bk(6)], writes=["ZT"])
                    S.act(ACT(ZT[0:16, 512:1024], PS[0:16, 7, :], AF.Copy), reads=[bk(7)], writes=["ZT"])
                    dma_out("sp", dout[nm][l], ZT[0:16, :], "ZT")

        def phase1(l, j, i, hook=None):
            var = 0 if i < NPT else 1
            xk = ("X", j)
            if var == 1 and "ssd" in mixers:
                conv_phase(l, i, var)
                conv_silu(l, var)
            silu_phase(l)
            if var == 1:
                S.pool(lambda e: e.memset(SST[:], 0.0), writes=["SST"])
                if "s5" in mixers:
                    s5_sample_tables()
            if "ret" in mixers:
                retention(l, i, var)
            if var == 0 and "ssd" in mixers:
                conv_phase(l, i, var)
            if "hg" in mixers:
                hgrn(l, i, var)
            if "ssd" in mixers:
                if var == 0:
                    conv_silu(l, var)
                ssd(l, i, var)
            if "s5" in mixers:
                s5(l, i, var, hook)
            elif hook is not None:
                hook()
            for half in range(2):
                for cc in range(4):
                    c = half * 4 + cc
                    S.pe(TR(PS[:, half, cc * 128:(cc + 1) * 128], MIX[:, c * 128:(c + 1) * 128]), reads=["MIX", "C_ident"], writes=[bk(half)])
                src = PS[:, half, :].rearrange("p (c n) -> p c n", c=4)
                if half == 0:
                    S.act(ACT(MIXT[:, 0:4, :], src, AF.Copy), reads=[bk(0)], writes=["MIXT"])
                else:
                    S.dve(CP(MIXT[:, 4:8, :], src), reads=[bk(1)], writes=["MIXT"])
            for n in range(2):
                for k in range(8):
                    S.pe(MM(PS[:, 2 + n, :], MIXT[:, k, :], WOUT[:, k, n * 512:(n + 1) * 512], start=(k == 0), stop=(k == 7)),
                         reads=["MIXT", "WOUT"], writes=[bk(2 + n)])
            S.dve(TT(X[:, j, :], X[:, j, :], PS[:, 2:4, :].rearrange("p a n -> p (a n)"), ADD),
                  reads=[xk, bk(2), bk(3)], writes=[xk])

        def norm1(l, j, par):
            norm_T(X[:, j, :], ("X", j), NM[:, l, :], "NM", lambda h: HTS[par][:, 4 * h:4 * h + 4, :], "HT%d" % par)

        def norm2(l, j):
            if do_mlp:
                norm_T(X[:, j, :], ("X", j), NMLP[:, l, :], "NMLP", lambda h: H2T[:, j, 4 * h:4 * h + 4, :], ("H2T", j))

        def mlp_issue(l, ch):
            r = ch % 2
            S.dma("pool", lambda e: e.dma_start(out=WUP[r], in_=din["w_up"][l][:, ch * FCH:(ch + 1) * FCH].rearrange("(c p) n -> p c n", p=128)),
                  writes=["WUP%d" % r])
            S.dma("pool", lambda e: e.dma_start(out=WDN[r], in_=din["w_down"][l][ch * FCH:(ch + 1) * FCH, :].rearrange("(f p) n -> p f n", p=128)),
                  writes=["WDN%d" % r])

        def mlp(l, ntile):
            pairs = [(a, min(2, ntile - a)) for a in range(0, ntile, 2)]
            mlp_issue(l, 0)
            mlp_issue(l, 1)
            NF = FCH // 128
            dbank = [0]
            for ch in range(NFC):
                r = ch % 2
                for pi2, (j0, ln) in enumerate(pairs):
                    N = 128 * ln
                    hk = [("H2T", j0 + t) for t in range(ln)]
                    at = ACTT2[pi2 % 2]
                    ak = "ACTT%d" % (pi2 % 2)
                    for f in range(NF):
                        ub = f % 2
                        for k in range(8):
                            S.pe(MM(PS[:, ub, 0:N], WUP[r][:, k, f * 128:(f + 1) * 128], H2T[:, j0:j0 + ln, k, :], start=(k == 0), stop=(k == 7)),
                                 reads=hk + ["WUP%d" % r], writes=[bk(ub)])
                        S.act(ACT(RELU[ub][:, 0:N], PS[:, ub, 0:N], AF.Relu), reads=[bk(ub)], writes=["RELU%d" % ub])
                        S.pool(TT(at[:, f, 0:N], RELU[ub][:, 0:N], RELU[ub][:, 0:N], MUL), reads=["RELU%d" % ub], writes=[ak])
                for pi2, (j0, ln) in enumerate(pairs):
                    at = ACTT2[pi2 % 2]
                    ak = "ACTT%d" % (pi2 % 2)
                    for t in range(ln):
                        b0 = 2 + 2 * (dbank[0] % 3)
                        dbank[0] += 1
                        for n in range(2):
                            for f in range(NF):
                                S.pe(MM(PS[:, b0 + n, :], at[:, f, t * 128:(t + 1) * 128], WDN[r][:, f, n * 512:(n + 1) * 512],
                                        start=(f == 0), stop=(f == NF - 1)),
                                     reads=[ak, "WDN%d" % r], writes=[bk(b0 + n)])
                        xk = ("X", j0 + t)
                        S.dve(TT(X[:, j0 + t, :], X[:, j0 + t, :], PS[:, b0:b0 + 2, :].rearrange("p a n -> p (a n)"), ADD),
                              reads=[xk, bk(b0), bk(b0 + 1)], writes=[xk])
                if ch + 2 < NFC:
                    mlp_issue(l, ch + 2)

        def w_issue(l):
            S.dma("pool", lambda e: e.dma_start(out=WIN[:], in_=din["w_in"][l].rearrange("(c p) n -> p c n", p=128)), writes=["WIN"])
            S.dma("pool", lambda e: e.dma_start(out=WOUT[:], in_=din["w_out"][l].rearrange("(c p) n -> p c n", p=128)), writes=["WOUT"])

        def final_norm(tiles):
            dma_in("sp", WF[:], din["norm_final"].partition_broadcast(128), "WF")
            for j, i in enumerate(tiles):
                xk = ("X", j)
                S.act(ACT(JUNK, X[:, j, :], AF.Square, accum_out=STA[:, 0:1]), reads=[xk], writes=["JUNK", "STAN"])
                S.act(ACT(STA[:, 1:2], STA[:, 0:1], AF.Ln, scale=1.0 / D, bias=EPS), reads=["STAN"], writes=["STAN"])
                S.act(ACT(STA[:, 2:3], STA[:, 1:2], AF.Exp, scale=-0.5), reads=["STAN"], writes=["STAN"])
                S.act(ACT(YOUT[:], X[:, j, :], AF.Copy, scale=STA[:, 2:3]), reads=[xk, "STAN"], writes=["YOUT"])
                S.dve(TT(YOUT[:], YOUT[:], WF[:], MUL), reads=["YOUT", "WF"], writes=["YOUT"])
                if i < NPT:
                    dma_out("sp", dout["yp"][i * 128:(i + 1) * 128, :], YOUT[:], "YOUT")
                else:
                    dma_out("sp", dout["ys"], YOUT[:], "YOUT")

        plan = [(g, l) for g in range(len(sgs)) for l in range(n_layers)]
        w_issue(plan[0][1])
        for pi_, (g, l) in enumerate(plan):
            tiles = sgs[g]
            if l == 0:
                for j, i in enumerate(tiles):
                    src = din["xp"][i * 128:(i + 1) * 128, :] if i < NPT else din["xs"]
                    dma_in("sp", X[:, j, :], src, ("X", j))
            layer_params(l)
            if "s5" in mixers:
                s5_setup(l)
            norm1(l, 0, 0)
            for j, i in enumerate(tiles):
                proj_in(j % 2)

                def hook(j=j, l=l, n=len(tiles)):
                    if j > 0:
                        norm2(l, j - 1)
                    if j + 1 < n:
                        norm1(l, j + 1, (j + 1) % 2)

                phase1(l, j, i, hook)
            norm2(l, len(tiles) - 1)
            if pi_ + 1 < len(plan):
                w_issue(plan[pi_ + 1][1])
            if do_mlp:
                mlp(l, len(tiles))
            if l == n_layers - 1:
                final_norm(tiles)
        S.emit()
    return nc, S


_CONSTS = None


def _host_params(inp):
    f = np.float32
    p = {}
    p["nm_fm"] = np.ascontiguousarray(inp["norm_mix"].reshape(2, 8, 128).transpose(0, 2, 1))
    p["nmlp_fm"] = np.ascontiguousarray(inp["norm_mlp"].reshape(2, 8, 128).transpose(0, 2, 1))
    p["norm_final"] = np.ascontiguousarray(inp["norm_final"])
    for k in ("w_in", "w_out", "w_up", "w_down"):
        p[k] = np.ascontiguousarray(inp[k])
    p["hg_lg"] = np.ascontiguousarray(inp["hg_lb_logits"].reshape(2, 2, 128).transpose(2, 1, 0))
    p["conv_w"] = np.ascontiguousarray(inp["ssd_conv_w"].reshape(2, 4, 6, 128).transpose(0, 3, 2, 1))
    p["conv_b"] = np.ascontiguousarray(inp["ssd_conv_b"].reshape(2, 6, 128).transpose(0, 2, 1))
    p["dt_bias"] = np.ascontiguousarray(inp["ssd_dt_bias"])
    p["a_log"] = np.ascontiguousarray(inp["ssd_a_log"])
    p["ssd_d"] = np.ascontiguousarray(inp["ssd_d"])
    p["ssd_norm"] = np.ascontiguousarray(inp["ssd_norm"])

    def fm8(a):
        return np.ascontiguousarray(a.reshape(2, 8, 2, 64).transpose(0, 2, 3, 1).reshape(2, 128, 8))

    p["s5_are"] = fm8(inp["s5_a_re"])
    p["s5_aim"] = fm8(inp["s5_a_im"])
    p["s5_ldt"] = fm8(np.broadcast_to(inp["s5_log_dt"][:, :, None], (2, 16, 64)))
    for nm, src in (("s5_bre", inp["s5_b_re"]), ("s5_bim", inp["s5_b_im"])):
        o = np.zeros((2, 128, 8, 2, 64), f)
        for c in range(8):
            for gl in range(2):
                g = 2 * c + gl
                r0 = 32 * (c % 4) + 16 * gl
                o[:, r0:r0 + 16, c, gl, :] = src[:, g].transpose(0, 2, 1)
        p[nm] = o.reshape(2, 128, 8, 128)
    for nm, src in (("s5_cre", inp["s5_c_re"]), ("s5_cim", inp["s5_c_im"])):
        o = np.zeros((2, 2, 64, 8, 2, 16), f)
        for c in range(8):
            for gl in range(2):
                o[:, gl, :, c, gl, :] = src[:, 2 * c + gl].transpose(0, 2, 1)
        p[nm] = o.reshape(2, 128, 8, 32)
    p["s5_dfm"] = np.ascontiguousarray(inp["s5_d"].reshape(2, 2, 128).transpose(0, 2, 1))
    p["glu_w"] = np.ascontiguousarray(inp["s5_glu_w"])
    p["glu_b"] = np.ascontiguousarray(inp["s5_glu_b"])
    return p


def _core_inputs(inp, shared, consts, c):
    m = dict(shared)
    for k, v in consts.items():
        m["c_" + k] = v
    b0 = c * SB
    m["xp"] = np.ascontiguousarray(inp["x_prompt"][c])
    m["xs"] = np.ascontiguousarray(inp["x_sample"][b0:b0 + SB].reshape(128, D))
    m["st_ret"] = np.ascontiguousarray(inp["state_ret"][:, b0:b0 + SB])
    m["st_hg"] = np.ascontiguousarray(inp["state_hgrn"][:, b0:b0 + SB])
    m["st_ssd"] = np.ascontiguousarray(inp["state_ssd"][:, b0:b0 + SB])
    m["st_conv"] = np.ascontiguousarray(inp["state_ssd_conv"][:, b0:b0 + SB].reshape(2, SB * 3, 768))
    m["st_s5re"] = np.ascontiguousarray(inp["state_s5_re"][:, b0:b0 + SB].reshape(2, SB, 1024))
    m["st_s5im"] = np.ascontiguousarray(inp["state_s5_im"][:, b0:b0 + SB].reshape(2, SB, 1024))
    return m


def _assemble(res):
    f = np.float32
    yp = np.stack([r["yp"] for r in res], 0).astype(f)
    ys = np.concatenate([r["ys"].reshape(SB, SL, D) for r in res], 0).astype(f)

    def pst(k, shp):
        return np.stack([r[k].reshape(shp) for r in res], 1).astype(f)

    def sst(k, shp):
        return np.concatenate([r[k].reshape(shp) for r in res], 1).astype(f)

    return (yp, ys,
            pst("p_ret", (2, 4, 64, 64)), pst("p_hg", (2, 4, 64, 64)), pst("p_ssd", (2, 4, 128, 64)),
            pst("p_conv", (2, 3, 768)), pst("p_s5re", (2, 16, 64)), pst("p_s5im", (2, 16, 64)),
            sst("s_ret", (2, SB, 4, 64, 64)), sst("s_hg", (2, SB, 4, 64, 64)), sst("s_ssd", (2, SB, 4, 128, 64)),
            sst("s_conv", (2, SB, 3, 768)), sst("s_s5re", (2, SB, 16, 64)), sst("s_s5im", (2, SB, 16, 64)))


def kernel(**inputs):
    global _CONSTS
    if _CONSTS is None:
        _CONSTS = _const_tables()
    inp = {k: np.asarray(v, dtype=np.float32) for k, v in inputs.items()}
    shared = _host_params(inp)
    nc, _ = build_program()
    in_maps = [_core_inputs(inp, shared, _CONSTS, c) for c in range(NCORE)]
    res = run_bass_kernel_spmd(nc, in_maps, core_ids=list(range(NCORE)))
    return _assemble(res.results)
```

```python
import contextlib
import math
import numpy as np
import concourse.bass as bass
import concourse.mybir as mybir
from concourse.bass_utils import run_bass_kernel_spmd

F32 = mybir.dt.float32
BF16 = mybir.dt.bfloat16
I32 = mybir.dt.int32
AF = mybir.ActivationFunctionType
ALU = mybir.AluOpType
AX = mybir.AxisListType

ENGS = ("pe", "act", "dve", "pool", "sp")
MAXV = 30000

D = 1024
NCORE = 8
SEQ = 2048
NPT = 16
NT = 17
SB = 16
SL = 8
PAST = 16384
INC = 3332
DFF = 4096
EPS = 1e-6
SGS = [list(range(0, 4)), list(range(4, 8)), list(range(8, 12)), list(range(12, 16)), [16]]
NEG = -30000.0
FCH = 512
NFC = DFF // FCH


class Op:
    __slots__ = ("eng", "fn", "reads", "writes", "dma", "dkey", "deps", "sig", "idx")

    def __init__(self, eng, fn, reads, writes, dma, dkey):
        self.eng, self.fn, self.reads, self.writes = eng, fn, tuple(reads), tuple(writes)
        self.dma, self.dkey = dma, dkey
        self.deps = []
        self.sig = None
        self.idx = -1


class Sched:
    def __init__(self, nc):
        self.nc = nc
        self.ops = []
        self.lastw = {}
        self.readers = {}
        self.alias = {}

    def _expand(self, keys):
        out = []
        for k in keys:
            for kk in self.alias.get(k, (k,)) if isinstance(k, str) else (k,):
                for k3 in (self.alias.get(kk, (kk,)) if isinstance(kk, str) else (kk,)):
                    if k3 not in out:
                        out.append(k3)
        return out

    def op(self, eng, fn, reads=(), writes=(), dma=False, dkey=None):
        reads, writes = self._expand(reads), self._expand(writes)
        o = Op(eng, fn, reads, writes, dma, dkey)
        o.idx = len(self.ops)
        deps = set()
        for r in o.reads:
            w = self.lastw.get(r)
            if w is not None:
                deps.add(w)
        for r in o.writes:
            w = self.lastw.get(r)
            if w is not None:
                deps.add(w)
            for rd in self.readers.get(r, ()):
                deps.add(rd)
        deps.discard(o.idx)
        o.deps = sorted(deps)
        for r in o.writes:
            self.lastw[r] = o.idx
            self.readers[r] = []
        for r in o.reads:
            if r not in o.writes:
                self.readers.setdefault(r, []).append(o.idx)
        self.ops.append(o)
        return o

    def pe(self, fn, reads=(), writes=()):
        return self.op("pe", fn, reads, writes)

    def act(self, fn, reads=(), writes=()):
        return self.op("act", fn, reads, writes)

    def dve(self, fn, reads=(), writes=()):
        return self.op("dve", fn, reads, writes)

    def pool(self, fn, reads=(), writes=()):
        return self.op("pool", fn, reads, writes)

    def dma(self, eng, fn, reads=(), writes=(), dkey=None):
        if dkey is None:
            dkey = writes[0] if len(writes) else ("st", reads[0])
        return self.op(eng, fn, reads, writes, dma=True, dkey=dkey)

    def emit(self, final_eng="sp"):
        nc = self.nc
        ops = self.ops
        needed = [False] * len(ops)
        for o in ops:
            for d in o.deps:
                p = ops[d]
                if p.eng == "pe" and o.eng == "pe" and not p.dma and not o.dma:
                    continue
                needed[d] = True
        for o in ops:
            if o.dma:
                needed[o.idx] = True
        cnt = {e: 0 for e in ENGS}
        dcnt = {}
        for o in ops:
            if not needed[o.idx]:
                continue
            if o.dma:
                dcnt[o.dkey] = dcnt.get(o.dkey, 0) + 16
                o.sig = ("d", o.dkey, dcnt[o.dkey])
            else:
                cnt[o.eng] += 1
                o.sig = ("e", o.eng, cnt[o.eng])
        stack = contextlib.ExitStack()
        esems = {}
        for e in ENGS:
            n = (cnt[e] + MAXV - 1) // MAXV
            esems[e] = [stack.enter_context(nc.semaphore(f"s_{e}{i}")) for i in range(max(n, 1))]
        dsems = {}
        for i, k in enumerate(dcnt):
            dsems[k] = stack.enter_context(nc.semaphore(f"d{i}"))
        self.n_sems = sum(len(v) for v in esems.values()) + len(dsems)

        def semval(sig):
            if sig[0] == "d":
                return dsems[sig[1]], sig[2]
            k = sig[2] - 1
            return esems[sig[1]][k // MAXV], (k % MAXV) + 1

        by_eng = {e: [o for o in ops if o.eng == e] for e in ENGS}
        final = {}
        for o in ops:
            if o.dma:
                final[o.dkey] = o.sig[2]

        def run(e, eh):
            waited = {}
            for o in by_eng[e]:
                need = {}
                for d in o.deps:
                    p = ops[d]
                    if p.sig is None:
                        continue
                    if p.eng == "pe" and e == "pe" and not p.dma and not o.dma:
                        continue
                    s, v = semval(p.sig)
                    key = (p.sig[0], p.sig[1], (p.sig[2] - 1) // MAXV if p.sig[0] == "e" else 0)
                    if key not in need or need[key][1] < v:
                        need[key] = (s, v)
                for key, (s, v) in need.items():
                    if waited.get(key, 0) >= v:
                        continue
                    waited[key] = v
                    eh.wait_ge(s, v)
                ins = o.fn(eh)
                if o.sig is not None:
                    s, v = semval(o.sig)
                    ins.then_inc(s, 16 if o.dma else 1)
            if e == final_eng:
                for k, v in final.items():
                    eh.wait_ge(dsems[k], v)

        with stack:
            with nc.Block() as block:
                @block.tensor
                def _(eh):
                    run("pe", eh)

                @block.scalar
                def _(eh):
                    run("act", eh)

                @block.vector
                def _(eh):
                    run("dve", eh)

                @block.gpsimd
                def _(eh):
                    run("pool", eh)

                @block.sync
                def _(eh):
                    run("sp", eh)


def _const_tables():
    c = {}
    c["ident"] = np.eye(128, dtype=np.float32)
    s = np.arange(128)[:, None]
    t = np.arange(128)[None, :]
    cp = (s <= t).astype(np.float32)
    cs = ((s <= t) & (s // SL == t // SL)).astype(np.float32)
    c["causal"] = np.stack([cp, cs], 1)
    c["negmask"] = np.stack([(1 - cp) * NEG, (1 - cs) * NEG], 1).astype(np.float32)
    ssame = (s // SL == t // SL).astype(np.float32)
    c["sameseq"] = np.stack([np.ones((128, 128), np.float32), ssame], 1)
    gam = 1.0 - np.exp2(-5.0 - np.arange(4, dtype=np.float64))
    lg = np.log(gam)
    kscale = 64.0 ** -0.5
    retd = np.zeros((128, 2, 4, 128), np.float64)
    for h in range(4):
        retd[:, 0, h, :] = np.exp(lg[h] * np.clip(t - s, 0, None)) * cp * kscale
        retd[:, 1, h, :] = np.exp(lg[h] * np.clip(t - s, 0, None)) * cs * kscale
    c["retd"] = retd.astype(np.float32)
    qs = np.zeros((128, 2, 2, 128), np.float64)
    ks = np.zeros((128, 2, 4), np.float64)
    ra = np.zeros((128, 2, 2), np.float64)
    tt = np.arange(128)
    for h in range(4):
        blk, hl = h // 2, h % 2
        qs[64 * hl:64 * hl + 64, 0, blk, :] = np.exp(lg[h] * (tt + 1))[None, :]
        qs[64 * hl:64 * hl + 64, 1, blk, :] = np.exp(lg[h] * (tt % SL + 1))[None, :]
        ks[:, 0, h] = np.exp(lg[h] * (127 - tt)) * kscale
        ks[:, 1, h] = np.exp(lg[h] * (SL - 1 - tt % SL)) * kscale
        ra[64 * hl:64 * hl + 64, 0, blk] = np.exp(lg[h] * 128)
        ra[64 * hl:64 * hl + 64, 1, blk] = np.exp(lg[h] * SL)
    c["ret_qs"] = qs.astype(np.float32)
    c["ret_ks"] = ks.astype(np.float32)
    c["ret_a"] = ra.astype(np.float32)
    ang = (1.0 / (10000.0 ** np.linspace(0.0, 1.0, 32, dtype=np.float32))).astype(np.float32)
    pos = np.zeros((128, NT), np.float32)
    for i in range(NPT):
        pos[:, i] = i * 128 + tt
    pos[:, NPT] = PAST + tt % SL
    theta = (pos[:, :, None] * ang[None, None, :]).astype(np.float32)
    c["cos"] = np.cos(theta).astype(np.float32)
    c["sin"] = np.sin(theta).astype(np.float32)
    b = np.arange(SB)
    c["colmask"] = np.broadcast_to((tt[None, :] // SL == b[:, None]).astype(np.float32)[None], (128, SB, 128)).copy()
    c["rowmask"] = (tt[:, None] // SL == b[None, :]).astype(np.float32)
    rst = np.ones((128, 2, 128), np.float32)
    rst[:, 1, :] = (tt % SL != 0).astype(np.float32)[None, :]
    rst[:, 0, 0] = 0.0
    c["rst"] = rst
    tau = np.zeros((128, 2, 128), np.float32)
    tau[:, 0, :] = tt[None, :]
    tau[:, 1, :] = (tt % SL)[None, :]
    c["tau"] = tau
    hm = np.zeros((128, 2), np.float32)
    hm[:64, 0] = 1.0
    hm[64:, 1] = 1.0
    c["hmask"] = hm
    return c


CONST_SHAPES = {
    "ident": [128, 128], "causal": [128, 2, 128], "negmask": [128, 2, 128], "sameseq": [128, 2, 128],
    "retd": [128, 2, 4, 128], "ret_qs": [128, 2, 2, 128], "ret_ks": [128, 2, 4], "ret_a": [128, 2, 2],
    "cos": [128, NT, 32], "sin": [128, NT, 32], "colmask": [128, SB, 128], "rowmask": [128, SB],
    "rst": [128, 2, 128], "tau": [128, 2, 128], "hmask": [128, 2],
}

IN_SHAPES = {
    "xp": [SEQ, D], "xs": [128, D],
    "st_ret": [2, SB, 4, 64, 64], "st_hg": [2, SB, 4, 64, 64], "st_ssd": [2, SB, 4, 128, 64],
    "st_conv": [2, SB * 3, 768], "st_s5re": [2, SB, 1024], "st_s5im": [2, SB, 1024],
    "nm_fm": [2, 128, 8], "nmlp_fm": [2, 128, 8], "norm_final": [D],
    "w_in": [2, D, INC], "w_out": [2, D, D], "w_up": [2, D, DFF], "w_down": [2, DFF, D],
    "hg_lg": [128, 2, 2], "conv_w": [2, 128, 6, 4], "conv_b": [2, 128, 6],
    "dt_bias": [2, 4], "a_log": [2, 4], "ssd_d": [2, 4], "ssd_norm": [2, 256],
    "s5_are": [2, 128, 8], "s5_aim": [2, 128, 8], "s5_ldt": [2, 128, 8],
    "s5_bre": [2, 128, 8, 128], "s5_bim": [2, 128, 8, 128],
    "s5_cre": [2, 128, 8, 32], "s5_cim": [2, 128, 8, 32], "s5_dfm": [2, 128, 2],
    "glu_w": [2, 256, 256], "glu_b": [2, 256],
}
OUT_SHAPES = {
    "yp": [SEQ, D], "ys": [128, D],
    "p_ret": [2, 4, 64, 64], "p_hg": [2, 4, 64, 64], "p_ssd": [2, 4, 128, 64], "p_conv": [2, 3, 768],
    "p_s5re": [2, 8, 128], "p_s5im": [2, 8, 128],
    "s_ret": [2, SB, 4, 64, 64], "s_hg": [2, SB, 4, 64, 64], "s_ssd": [2, SB, 4, 128, 64],
    "s_conv": [2, SB * 3, 768], "s_s5re": [2, SB, 1024], "s_s5im": [2, SB, 1024],
}


def build_program(n_layers=2, sgs=None, mixers=("ret", "hg", "ssd", "s5"), do_mlp=True):
    sgs = SGS if sgs is None else sgs
    LASTP = max([i for g in sgs for i in g if i < NPT] + [-1])
    nc = bass.Bass("TRN2", target_bir_lowering=False)
    S = Sched(nc)
    st = contextlib.ExitStack()
    din = {k: nc.dram_tensor(k, v, F32, kind="ExternalInput").ap() for k, v in IN_SHAPES.items()}
    dcon = {k: nc.dram_tensor("c_" + k, v, F32, kind="ExternalInput").ap() for k, v in CONST_SHAPES.items()}
    dout = {k: nc.dram_tensor(k, v, F32, kind="ExternalOutput").ap() for k, v in OUT_SHAPES.items()}

    def sb(name, shape, dt=F32):
        return st.enter_context(nc.sbuf_tensor(name, shape, dt))

    with st:
        PS = st.enter_context(nc.psum_tensor("PS", [128, 8, 512], F32))

        def bk(k):
            return "b%d" % k

        NSG = max(len(g) for g in sgs)
        X = sb("X", [128, NSG, D])
        H2T = sb("H2T", [128, NSG, 8, 128], BF16)
        WIN = sb("WIN", [128, 8, INC], BF16)
        WOUT = sb("WOUT", [128, 8, D], BF16)
        GB = sb("GB", [128, 8, 1024])
        GBF = GB[:].rearrange("p a b -> p (a b)")

        def galias(name, k, c0, c1, shape):
            ap = GB[:, k, c0 * 128:c1 * 128]
            if len(shape) == 3:
                ap = ap.rearrange("p (a b) -> p a b", b=shape[2])
            elif len(shape) == 4:
                ap = ap.rearrange("p (a b c) -> p a b c", b=shape[2], c=shape[3])
            S.alias[name] = [("G", k, q) for q in range(c0 // 2, (c1 + 1) // 2)]
            return ap

        WUP, WDN = [], []
        for r in range(2):
            WUP.append(GB[:, 4 * r:4 * r + 2, :].bitcast(BF16).rearrange("p a (k f) -> p (a k) f", f=FCH))
            WDN.append(GB[:, 4 * r + 2:4 * r + 4, :].bitcast(BF16).rearrange("p a (f n) -> p (a f) n", n=D))
            S.alias["WUP%d" % r] = [("G", 4 * r + a, q) for a in range(2) for q in range(4)]
            S.alias["WDN%d" % r] = [("G", 4 * r + 2 + a, q) for a in range(2) for q in range(4)]
        C = {k: sb("C_" + k, v, BF16 if k == "colmask" else F32) for k, v in CONST_SHAPES.items()}
        NM = sb("NM", [128, 2, 8]); NMLP = sb("NMLP", [128, 2, 8])
        HGLG = sb("HGLG", [128, 2, 2]); LOW = sb("LOW", [128, 2, 2]); OML = sb("OML", [128, 2, 2])
        CW = sb("CW", [128, 2, 6, 4]); CB = sb("CB", [128, 2, 6])
        DTB = sb("DTB", [128, 2, 4]); NEGA = sb("NEGA", [128, 2, 4]); SSDD = sb("SSDD", [128, 2, 4])
        SNORM = sb("SNORM", [128, 256]); GLUB = sb("GLUB", [128, 256]); GLUW = sb("GLUW", [128, 2, 256])
        S5P = sb("S5P", [128, 2, 3, 8])
        S5C = sb("S5C", [128, 8, 32]); S5CI = sb("S5CI", [128, 8, 32])
        S5D = sb("S5D", [128, 2, 2]); DIAGD = sb("DIAGD", [128, 2, 128])
        COSR = sb("COSR", [128, 8, 128]); SINR = sb("SINR", [128, 8, 128])
        ABAR = sb("ABAR", [128, 2, 8]); COEF = sb("COEF", [128, 2, 8]); RR = sb("RR", [128, 8]); PHI = sb("PHI", [128, 8])
        BCF = sb("BCF", [128, 2, 8, 128])
        SRET = sb("SRET", [128, 2, 2, 128]); SHG = sb("SHG", [128, 2, 2, 128]); SSSD = sb("SSSD", [128, 2, 4, 64])
        ZS5 = sb("ZS5", [128, 2, 2, 8]); XPH = sb("XPH", [128, 2, 6, 3])
        HTS = [sb("HT0", [128, 8, 128], BF16), sb("HT1", [128, 8, 128], BF16)]
        XN = sb("XN", [128, D])
        STA = sb("STA", [128, 16]); STB = sb("STB", [128, 16]); STC = sb("STC", [128, 16])
        TM = sb("TM", [128, 1796]); FM = sb("FM", [128, 12, 128])
        MIX = sb("MIX", [128, D]); MIXT = sb("MIXT", [128, 8, 128], BF16)
        JUNK = MIXT[:].rearrange("p a b -> p (a b)")
        S.alias["JUNK"] = ["MIXT"]
        SGR = sb("SGR", [128, 256]); SGH = sb("SGH", [128, 256]); SSZ = sb("SSZ", [128, 256])
        HREF = sb("HREF", [128, 2, 4]); EBL = sb("EBL", [128, 2, SB])
        DEL = sb("DEL", [128, 4]); LOGA = sb("LOGA", [128, 4]); LND = sb("LND", [128, 4]); BCOL = sb("BCOL", [128, 4])
        BLAST = sb("BLAST", [128, 4]); KSC = sb("KSC", [128, 4])
        RELU = [TM[:, 0:256], TM[:, 256:512]]
        S.alias["TM"] = [("TM", q) for q in range(4)]
        S.alias["RELU0"] = [("TM", 0)]; S.alias["RELU1"] = [("TM", 1)]
        S.alias["ACTT0"] = [("TM", 2)]; S.alias["ACTT1"] = [("TM", 3)]
        ACTT2 = [TM[:, 512:1024].bitcast(BF16).rearrange("p (f n) -> p f n", n=256),
                 TM[:, 1024:1536].bitcast(BF16).rearrange("p (f n) -> p f n", n=256)]
        QKT = galias("QKT", 0, 0, 4, [128, 4, 128]); PT = galias("PT", 0, 4, 8, [128, 4, 128])
        QKR = galias("QKR", 1, 0, 4, [128, 512]); RT1 = galias("RT1", 1, 4, 6, [128, 8, 32]); RT2 = galias("RT2", 1, 6, 8, [128, 8, 32])
        QD = galias("QD", 3, 0, 2, [128, 2, 128]); KD = galias("KD", 3, 2, 4, [128, 256]); QM = galias("QM", 3, 4, 8, [128, 4, 128])
        def galias16(name, k, c0, c1, shape):
            ap = GB[:, k, c0 * 128:c1 * 128].bitcast(BF16)
            if len(shape) == 3:
                ap = ap.rearrange("p (a b) -> p a b", b=shape[2])
            elif len(shape) == 4:
                ap = ap.rearrange("p (a b c) -> p a b c", b=shape[2], c=shape[3])
            S.alias[name] = [("G", k, q) for q in range(c0 // 2, (c1 + 1) // 2)]
            return ap

        QH = galias16("QH", 3, 4, 6, [128, 2, 2, 128])
        QKTB = galias16("QKTB", 0, 0, 2, [128, 4, 128])
        HKVB = galias16("HKVB", 0, 0, 4, [128, 4, 2, 128])
        XRB = galias16("XRB", 2, 0, 4, [128, 8, 128]); XIB = galias16("XIB", 3, 0, 4, [128, 8, 128])
        VB = sb("VB", [128, 512], BF16)
        PTB = galias16("PTB", 0, 4, 6, [128, 4, 128]); PT2B = galias16("PT2B", 5, 0, 2, [128, 4, 128]); XSB = galias16("XSB", 5, 2, 3, [128, 256])
        S5CB = sb("S5CB", [128, 8, 32], BF16); S5CIB = sb("S5CIB", [128, 8, 32], BF16)
        KDM = galias("KDM", 7, 0, 8, [128, 4, 256]); SST2 = galias("SST2", 7, 0, 8, [128, 4, 4, 64])
        CVT = galias("CVT", 7, 0, 6, [128, 768])
        SST = galias("SST", 5, 0, 8, [128, 4, 2, 128])
        HE = galias("HE", 1, 0, 2, [128, 2, 128]); HSIG = galias("HSIG", 1, 2, 4, [128, 2, 128])
        HKEY = galias("HKEY", 1, 4, 6, [128, 2, 128]); HB = galias("HB", 1, 6, 8, [128, 2, 128])
        HQ = galias("HQ", 2, 0, 2, [128, 2, 128]); HD = galias("HD", 2, 2, 4, [128, 2, 128])
        HEX = galias("HEX", 2, 4, 6, [128, 2, 128]); HQA = galias("HQA", 2, 6, 8, [128, 2, 128])
        HQD = galias("HQD", 3, 0, 2, [128, 2, 128]); HKT = galias("HKT", 3, 2, 4, [128, 256])
        LB = galias("LB", 0, 0, 4, [128, 4, 128]); D1 = galias("D1", 0, 4, 8, [128, 4, 128])
        DEC = galias("DEC", 1, 0, 4, [128, 4, 128]); EBT = galias("EBT", 1, 4, 8, [128, 4, 128])
        CMD = galias("CMD", 2, 0, 4, [128, 4, 128]); BD = galias("BD", 2, 4, 8, [128, 4, 128])
        BDM = galias("BDM", 3, 0, 4, [128, 4, 128])
        PT2 = galias("PT2", 5, 0, 4, [128, 4, 128]); XS = galias("XS", 5, 4, 6, [128, 256]); BMT = galias("BMT", 5, 6, 8, [128, 2, 128])
        CACC = galias("CACC", 4, 0, 6, [128, 6, 128]); XC = galias("XC", 6, 0, 6, [128, 6, 128])
        CTMPP = galias("CTMPP", 7, 0, 6, [128, 6, 128]); CTMPS = galias("CTMPS", 5, 0, 6, [128, 6, 128])
        XPW = GB[:, 5, 0:786].rearrange("p (c n) -> p c n", n=131)
        S.alias["XPW"] = [("G", 5, q) for q in range(4)]
        XPS = GBF[:, 0:1056].rearrange("p (c b j) -> p c b j", b=SB, j=11)
        S.alias["XPS"] = [("G", 0, q) for q in range(4)] + [("G", 1, 0)]
        HKV = galias("HKV", 4, 0, 8, [128, 4, 2, 128]); HKV4 = galias("HKV4", 7, 6, 8, [128, 2, 128])
        OB = galias("OB", 4, 0, 2, [128, 256]); OB2 = galias("OB2", 4, 2, 4, [128, 256])
        PR = galias("PR", 0, 0, 8, [128, 8, 128]); PI = galias("PI", 1, 0, 8, [128, 8, 128])
        XR = galias("XR", 2, 0, 8, [128, 8, 128]); XI = galias("XI", 3, 0, 8, [128, 8, 128])
        T1 = galias("T1", 4, 0, 8, [128, 8, 128]); T2 = galias("T2", 5, 0, 8, [128, 8, 128])
        WR = galias("WR", 6, 0, 8, [128, 8, 128]); WI = galias("WI", 7, 0, 8, [128, 8, 128])
        TI = GB[:, 7, :].bitcast(I32).rearrange("p (c n) -> p c n", n=128)
        S.alias["TI"] = [("G", 7, q) for q in range(4)]
        YG = galias("YG", 4, 0, 2, [128, 256]); YU = galias("YU", 4, 2, 4, [128, 256])
        YE = galias("YE", 4, 4, 6, [128, 256]); YGT = galias("YGT", 4, 6, 8, [128, 2, 128])
        ZT = galias("ZT", 5, 0, 8, [128, 1024])
        INJ = galias("INJ", 0, 0, 2, [128, 2, 8, SB]); INJ2 = galias("INJ2", 0, 2, 4, [128, 2, 8, SB])
        ZL = galias("ZL", 6, 0, 2, [128, 2, 8, SB]); Z0 = galias("Z0", 1, 2, 4, [128, 2, 8, SB])
        YOUT = galias("YOUT", 0, 0, 8, [128, 1024]); WF = galias("WF", 1, 0, 8, [128, 1024])

        def dma_in(eng, dst, src, key):
            S.dma(eng, lambda e: e.dma_start(out=dst, in_=src), writes=[key])

        def dma_out(eng, dst, src, key):
            S.dma(eng, lambda e: e.dma_start(out=dst, in_=src), reads=[key], dkey=("st", key))

        def MM(out, lhsT, rhs, start=True, stop=True):
            return lambda e: e.matmul(out, lhsT=lhsT, rhs=rhs, start=start, stop=stop, skip_group_check=True)

        def TR(out, in_, n=128):
            return lambda e: e.transpose(out, in_, C["ident"][0:n, 0:n])

        def ACT(out, in_, func, **kw):
            return lambda e: e.activation(out=out, in_=in_, func=func, **kw)

        def TT(out, in0, in1, op):
            return lambda e: e.tensor_tensor(out=out, in0=in0, in1=in1, op=op)

        def TS(out, in0, s1, s2, op0, op1=None):
            if op1 is None:
                return lambda e: e.tensor_scalar(out=out, in0=in0, scalar1=s1, scalar2=None, op0=op0)
            return lambda e: e.tensor_scalar(out=out, in0=in0, scalar1=s1, scalar2=s2, op0=op0, op1=op1)

        def STT(out, in0, scalar, in1, op0, op1):
            return lambda e: e.scalar_tensor_tensor(out=out, in0=in0, scalar=scalar, in1=in1, op0=op0, op1=op1)

        def CP(out, in_):
            return lambda e: e.tensor_copy(out=out, in_=in_)

        def bc(ap, shape):
            return ap.to_broadcast(shape)

        MUL, ADD, SUB = ALU.mult, ALU.add, ALU.subtract

        for k in CONST_SHAPES:
            dma_in("pool" if k == "colmask" else "sp", C[k][:], dcon[k], "C_" + k)
        dma_in("sp", NM[:], din["nm_fm"].rearrange("l p c -> p l c"), "NM")
        dma_in("sp", NMLP[:], din["nmlp_fm"].rearrange("l p c -> p l c"), "NMLP")
        dma_in("sp", HGLG[:], din["hg_lg"], "HGLG")
        dma_in("sp", CW[:], din["conv_w"].rearrange("l p b j -> p l b j"), "CW")
        dma_in("sp", CB[:], din["conv_b"].rearrange("l p b -> p l b"), "CB")
        for l in range(2):
            dma_in("sp", DTB[:, l, :], din["dt_bias"][l].partition_broadcast(128), "DTB")
            dma_in("sp", NEGA[:, l, :], din["a_log"][l].partition_broadcast(128), "NEGA")
            dma_in("sp", SSDD[:, l, :], din["ssd_d"][l].partition_broadcast(128), "SSDD")
            dma_in("sp", S5P[:, l, 0, :], din["s5_are"][l], "S5P")
            dma_in("sp", S5P[:, l, 1, :], din["s5_aim"][l], "S5P")
            dma_in("sp", S5P[:, l, 2, :], din["s5_ldt"][l], "S5P")
            dma_in("sp", S5D[:, l, :], din["s5_dfm"][l], "S5D")
        S.act(ACT(NEGA[:], NEGA[:], AF.Exp), reads=["NEGA"], writes=["NEGA"])
        S.dve(TS(NEGA[:], NEGA[:], -1.0, None, MUL), reads=["NEGA"], writes=["NEGA"])
        S.act(ACT(HGLG[:], HGLG[:], AF.Exp), reads=["HGLG"], writes=["HGLG"])
        S.dve(TT(LOW[:, 1, :], HGLG[:, :, 0], HGLG[:, :, 1], ADD), reads=["HGLG"], writes=["LOW"])
        S.dve(lambda e: e.reciprocal(out=LOW[:, 1, :], in_=LOW[:, 1, :]), reads=["LOW"], writes=["LOW"])
        S.dve(TT(LOW[:, 1, :], LOW[:, 1, :], HGLG[:, :, 1], MUL), reads=["LOW", "HGLG"], writes=["LOW"])
        S.pool(lambda e: e.memset(LOW[:, 0, :], 0.0), writes=["LOW"])
        S.dve(TS(OML[:], LOW[:], -1.0, 1.0, MUL, ADD), reads=["LOW"], writes=["OML"])
        S.pool(lambda e: e.memset(HREF[:], 0.0), writes=["HREF"])
        for t_, k_ in ((SRET, "SRET"), (SHG, "SHG"), (SSSD, "SSSD"), (ZS5, "ZS5"), (XPH, "XPH"), (MIX, "MIX")):
            S.pool(lambda e, t_=t_: e.memset(t_[:], 0.0), writes=[k_])

        def layer_params(l):
            dma_in("sp", SNORM[:], din["ssd_norm"][l].partition_broadcast(128), "SNORM")
            dma_in("sp", GLUB[:], din["glu_b"][l].partition_broadcast(128), "GLUB")
            dma_in("sp", GLUW[:], din["glu_w"][l].rearrange("(c p) n -> p c n", p=128), "GLUW")
            dma_in("sp", S5C[:], din["s5_cre"][l], "S5C")
            dma_in("sp", S5CI[:], din["s5_cim"][l], "S5CI")
            S.dve(TS(S5CI[:], S5CI[:], -1.0, None, MUL), reads=["S5CI"], writes=["S5CI"])
            S.dve(CP(S5CB[:], S5C[:]), reads=["S5C"], writes=["S5CB"])
            S.dve(CP(S5CIB[:], S5CI[:]), reads=["S5CI"], writes=["S5CIB"])
            for ub in range(2):
                S.dve(TS(DIAGD[:, ub, :], C["ident"][:], S5D[:, l, ub:ub + 1], None, MUL),
                      reads=["C_ident", "S5D"], writes=["DIAGD"])

        def s5_setup(l):
            S.act(ACT(STA[:, 0:8], S5P[:, l, 2, :], AF.Exp), reads=["S5P"], writes=["STA", "STAN"])
            S.dve(TT(RR[:], S5P[:, l, 0, :], STA[:, 0:8], MUL), reads=["S5P", "STA", "STAN"], writes=["RR"])
            S.dve(TT(PHI[:], S5P[:, l, 1, :], STA[:, 0:8], MUL), reads=["S5P", "STA", "STAN"], writes=["PHI"])
            S.act(ACT(RR[:], RR[:], AF.Exp), reads=["RR"], writes=["RR"])
            S.dve(TT(T1[:], bc(PHI[:].unsqueeze(2), [128, 8, 128]), bc(C["tau"][:, 0:1, :], [128, 8, 128]), MUL),
                  reads=["PHI", "C_tau"], writes=["T1"])

            def sincos(dst, dkey, shift):
                S.dve(TS(T2[:], T1[:], shift, 1.0 / (2 * math.pi), ADD, MUL), reads=["T1"], writes=["T2"])
                S.dve(CP(TI[:], T2[:]), reads=["T2"], writes=["TI"])
                S.dve(CP(T2[:], TI[:]), reads=["TI"], writes=["T2"])
                S.dve(STT(T2[:], T2[:], -2 * math.pi, T1[:], MUL, ADD), reads=["T2", "T1"], writes=["T2"])
                S.dve(TS(T2[:], T2[:], shift, 3.1415925, ADD, ALU.min), reads=["T2"], writes=["T2"])
                S.dve(TS(T2[:], T2[:], -3.1415925, None, ALU.max), reads=["T2"], writes=["T2"])
                S.act(ACT(dst[:], T2[:], AF.Sin), reads=["T2"], writes=[dkey])

            sincos(SINR, "SINR", 0.0)
            sincos(COSR, "COSR", math.pi / 2)
            S.dve(TT(ABAR[:, 0, :], RR[:], COSR[:, :, 1], MUL), reads=["RR", "COSR"], writes=["ABAR"])
            S.dve(TT(ABAR[:, 1, :], RR[:], SINR[:, :, 1], MUL), reads=["RR", "SINR"], writes=["ABAR"])
            xx, yy, uu, vv = STA[:, 0:8], ABAR[:, 1, :], S5P[:, l, 0, :], S5P[:, l, 1, :]
            S.dve(TS(STA[:, 0:8], ABAR[:, 0, :], -1.0, None, ADD), reads=["ABAR"], writes=["STA", "STAN"])
            S.dve(TT(STB[:, 0:8], uu, uu, MUL), reads=["S5P"], writes=["STB"])
            S.dve(TT(STB[:, 8:16], vv, vv, MUL), reads=["S5P"], writes=["STB"])
            S.dve(TT(STB[:, 0:8], STB[:, 0:8], STB[:, 8:16], ADD), reads=["STB"], writes=["STB"])
            S.dve(lambda e: e.reciprocal(out=STB[:, 0:8], in_=STB[:, 0:8]), reads=["STB"], writes=["STB"])
            S.dve(TT(STC[:, 0:8], xx, uu, MUL), reads=["STA", "STAN", "S5P"], writes=["STC"])
            S.dve(TT(STC[:, 8:16], yy, vv, MUL), reads=["ABAR", "S5P"], writes=["STC"])
            S.dve(TT(STC[:, 0:8], STC[:, 0:8], STC[:, 8:16], ADD), reads=["STC"], writes=["STC"])
            S.dve(TT(COEF[:, 0, :], STC[:, 0:8], STB[:, 0:8], MUL), reads=["STC", "STB"], writes=["COEF"])
            S.dve(TT(STC[:, 0:8], yy, uu, MUL), reads=["ABAR", "S5P"], writes=["STC"])
            S.dve(TT(STC[:, 8:16], xx, vv, MUL), reads=["STA", "STAN", "S5P"], writes=["STC"])
            S.dve(TT(STC[:, 0:8], STC[:, 0:8], STC[:, 8:16], SUB), reads=["STC"], writes=["STC"])
            S.dve(TT(COEF[:, 1, :], STC[:, 0:8], STB[:, 0:8], MUL), reads=["STC", "STB"], writes=["COEF"])
            dma_in("sp", PR[:], din["s5_bre"][l], "PR")
            dma_in("sp", PI[:], din["s5_bim"][l], "PI")
            for ri, dst, dk in ((0, XR, "XR"), (1, XI, "XI")):
                for half in range(2):
                    bnk = 4 + 2 * ri + half
                    for cc in range(4):
                        c = half * 4 + cc
                        S.dve(TS(WR[:, c, :], C["ident"][:], COEF[:, ri, c:c + 1], None, MUL),
                              reads=["C_ident", "COEF"], writes=["WR"])
                        S.pe(MM(PS[:, bnk, cc * 128:(cc + 1) * 128], C["sameseq"][:, 0, :], WR[:, c, :]),
                             reads=["C_sameseq", "WR"], writes=[bk(bnk)])
                    S.act(ACT(dst[:, half * 4:half * 4 + 4, :], PS[:, bnk, :].rearrange("p (c n) -> p c n", c=4), AF.Copy),
                          reads=[bk(bnk)], writes=[dk])
            S.dve(TT(T1[:], PR[:], XR[:], MUL), reads=["PR", "XR"], writes=["T1"])
            S.dve(TT(T2[:], PI[:], XI[:], MUL), reads=["PI", "XI"], writes=["T2"])
            S.pool(TT(BCF[:, 0], T1[:], T2[:], SUB), reads=["T1", "T2"], writes=["BCF"])
            S.dve(TT(T1[:], PR[:], XI[:], MUL), reads=["PR", "XI"], writes=["T1"])
            S.dve(TT(T2[:], PI[:], XR[:], MUL), reads=["PI", "XR"], writes=["T2"])
            S.pool(TT(BCF[:, 1], T1[:], T2[:], ADD), reads=["T1", "T2"], writes=["BCF"])

        def s5_sample_tables():
            for tb, k in ((COSR, "COSR"), (SINR, "SINR")):
                v = tb[:, :, 8:128].rearrange("p c (b e) -> p c b e", e=8)
                S.dve(CP(v, bc(tb[:, :, 0:8].unsqueeze(2), [128, 8, 15, 8])), reads=[k], writes=[k])

        def norm_T(xap, xkey, wfm, wkey, dst, dkey):
            S.act(ACT(JUNK[:], xap, AF.Square, accum_out=STA[:, 0:1]), reads=[xkey], writes=["JUNK", "STAN"])
            S.act(ACT(STA[:, 1:2], STA[:, 0:1], AF.Ln, scale=1.0 / D, bias=EPS), reads=["STAN"], writes=["STAN"])
            S.act(ACT(STA[:, 2:3], STA[:, 1:2], AF.Exp, scale=-0.5), reads=["STAN"], writes=["STAN"])
            S.act(ACT(XN[:], xap, AF.Copy, scale=STA[:, 2:3]), reads=[xkey, "STAN"], writes=["XN"])
            for half in range(2):
                for cc in range(4):
                    c = half * 4 + cc
                    S.pe(TR(PS[:, half, cc * 128:(cc + 1) * 128], XN[:, c * 128:(c + 1) * 128]),
                         reads=["XN", "C_ident"], writes=[bk(half)])
                for cc in range(4):
                    c = half * 4 + cc
                    S.act(ACT(dst(half)[:, cc, :], PS[:, half, cc * 128:(cc + 1) * 128], AF.Copy, scale=wfm[:, c:c + 1]),
                          reads=[bk(half), wkey], writes=[dkey])
            return STA[:, 2:3]

        TMG = [(0, 512, 0), (512, 1024, 512), (1536, 2048, 1024), (2048, 2304, 1536), (3072, 3076, 1792)]
        FMB = [1024 + 128 * b for b in range(4)] + [2304 + 128 * b for b in range(6)] + [3076, 3076 + 128]

        def proj_in(par):
            HT = HTS[par]
            hk = "HT%d" % par
            nb = 0
            for (c0, c1, d0) in TMG:
                bnk = 2 + nb % 2
                nb += 1
                w = c1 - c0
                for k in range(8):
                    S.pe(MM(PS[:, bnk, 0:w], HT[:, k, :], WIN[:, k, c0:c1], start=(k == 0), stop=(k == 7)),
                         reads=[hk, "WIN"], writes=[bk(bnk)])
                S.act(ACT(TM[:, d0:d0 + w], PS[:, bnk, 0:w], AF.Copy), reads=[bk(bnk)], writes=["TM"])
                if c0 == 512:
                    S.act(ACT(VB[:, 0:256], PS[:, bnk, 0:256], AF.Copy), reads=[bk(bnk)], writes=["VB"])
                if c0 == 1536:
                    S.act(ACT(VB[:, 256:512], PS[:, bnk, 0:256], AF.Copy), reads=[bk(bnk)], writes=["VB"])
            for g in range(3):
                bnk = 2 + nb % 2
                nb += 1
                for bb in range(4):
                    b = g * 4 + bb
                    for k in range(8):
                        S.pe(MM(PS[:, bnk, bb * 128:(bb + 1) * 128], WIN[:, k, FMB[b]:FMB[b] + 128], HT[:, k, :],
                                start=(k == 0), stop=(k == 7)), reads=[hk, "WIN"], writes=[bk(bnk)])
                fn = CP(FM[:, g * 4:g * 4 + 4, :], PS[:, bnk, :].rearrange("p (c n) -> p c n", c=4))
                S.dve(fn, reads=[bk(bnk)], writes=["FM"])

        def load_sst(name, l, g):
            src = din[name][l, 4 * g:4 * g + 4].rearrange("b (blk hl) k v -> hl k b blk v", hl=2)
            for hl in range(2):
                dma_in("sp", SST[64 * hl:64 * hl + 64, :, :, 64 * hl:64 * hl + 64], src[hl], "SST")

        def store_sst(name, l, g):
            dst = dout[name][l, 4 * g:4 * g + 4].rearrange("b (blk hl) k v -> hl k b blk v", hl=2)
            for hl in range(2):
                dma_out("sp", dst[hl], SST[64 * hl:64 * hl + 64, :, :, 64 * hl:64 * hl + 64], "SST")

        def retention(l, i, var):
            cos = bc(C["cos"][:, i, :].unsqueeze(1), [128, 8, 32])
            sin = bc(C["sin"][:, i, :].unsqueeze(1), [128, 8, 32])
            qk = TM[:, 0:512].rearrange("p (h i two) -> p h i two", h=8, two=2)
            qo = QKR[:].rearrange("p (h i two) -> p h i two", h=8, two=2)
            xe, xo = qk[:, :, :, 0], qk[:, :, :, 1]
            S.pool(TT(RT1[:], xe, cos, MUL), reads=["TM", "C_cos"], writes=["RT1"])
            S.pool(TT(RT2[:], xo, sin, MUL), reads=["TM", "C_sin"], writes=["RT2"])
            S.pool(TT(qo[:, :, :, 0], RT1[:], RT2[:], SUB), reads=["RT1", "RT2"], writes=["QKR"])
            S.pool(TT(RT1[:], xe, sin, MUL), reads=["TM", "C_sin"], writes=["RT1"])
            S.pool(TT(RT2[:], xo, cos, MUL), reads=["TM", "C_cos"], writes=["RT2"])
            S.pool(TT(qo[:, :, :, 1], RT1[:], RT2[:], ADD), reads=["RT1", "RT2"], writes=["QKR"])
            for b in range(4):
                S.pe(TR(PS[:, 4, b * 128:(b + 1) * 128], QKR[:, b * 128:(b + 1) * 128]), reads=["QKR", "C_ident"], writes=[bk(4)])
            S.act(ACT(QKTB[:], PS[:, 4, :].rearrange("p (c n) -> p c n", c=4), AF.Copy), reads=[bk(4)], writes=["QKTB"])
            S.dve(TT(QH[:], bc(QKTB[:, 0:2, :].unsqueeze(2), [128, 2, 2, 128]), bc(C["hmask"][:].unsqueeze(1).unsqueeze(3), [128, 2, 2, 128]), MUL),
                  reads=["QKTB", "C_hmask"], writes=["QH"])
            for h in range(4):
                blk, hl = h // 2, h % 2
                S.pe(MM(PS[:, 5, h * 128:(h + 1) * 128], QKTB[:, 2 + blk, :], QH[:, blk, hl, :]),
                     reads=["QKTB", "QH"], writes=[bk(5)])
            S.dve(TT(PTB[:], PS[:, 5, :].rearrange("p (c n) -> p c n", c=4), C["retd"][:, var], MUL),
                  reads=[bk(5), "C_retd"], writes=["PTB"])
            S.pool(TT(QD[:], QKTB[:, 0:2, :], C["ret_qs"][:, var], MUL), reads=["QKTB", "C_ret_qs"], writes=["QD"])
            S.pool(TT(KD[:].rearrange("p (h k) -> p h k", h=4), QKR[:, 256:512].rearrange("p (h k) -> p h k", h=4),
                      bc(C["ret_ks"][:, var, :].unsqueeze(2), [128, 4, 64]), MUL), reads=["QKR", "C_ret_ks"], writes=["KD"])
            for h in range(4):
                S.pe(MM(PS[:, 6, h * 64:(h + 1) * 64], PTB[:, h, :], VB[:, h * 64:(h + 1) * 64], start=(h == 0), stop=False),
                     reads=["PTB", "VB"], writes=[bk(6)])
            if var == 0:
                for blk in range(2):
                    S.pe(MM(PS[:, 6, blk * 128:(blk + 1) * 128], QD[:, blk, :], SRET[:, l, blk, :], start=False, stop=(blk == 1)),
                         reads=["QD", "SRET"], writes=[bk(6)])
                for blk in range(2):
                    S.pe(MM(PS[:, 7, blk * 128:(blk + 1) * 128], KD[:, blk * 128:(blk + 1) * 128], TM[:, 512 + blk * 128:512 + (blk + 1) * 128]),
                         reads=["KD", "TM"], writes=[bk(7)])
                for blk in range(2):
                    for hl in range(2):
                        sl = slice(64 * hl, 64 * hl + 64)
                        S.dve(STT(SRET[sl, l, blk, sl], SRET[sl, l, blk, sl], C["ret_a"][sl, 0, blk:blk + 1],
                                  PS[sl, 7, blk * 128 + 64 * hl:blk * 128 + 64 * hl + 64], MUL, ADD),
                              reads=["SRET", "C_ret_a", bk(7)], writes=["SRET"])
            else:
                for g in range(4):
                    load_sst("st_ret", l, g)
                    for blk in range(2):
                        S.dve(TT(QM[:], bc(QD[:, blk, :].unsqueeze(1), [128, 4, 128]), C["colmask"][:, 4 * g:4 * g + 4, :], MUL),
                              reads=["QD", "C_colmask"], writes=["QM"])
                        for bb in range(4):
                            last = (g == 3 and blk == 1 and bb == 3)
                            S.pe(MM(PS[:, 6, blk * 128:(blk + 1) * 128], QM[:, bb, :], SST[:, bb, blk, :], start=False, stop=last),
                                 reads=["QM", "SST"], writes=[bk(6)])
                    S.dve(TT(KDM[:], bc(KD[:].unsqueeze(1), [128, 4, 256]),
                             bc(C["rowmask"][:, 4 * g:4 * g + 4].unsqueeze(2), [128, 4, 256]), MUL),
                          reads=["KD", "C_rowmask"], writes=["KDM"])
                    for bb in range(4):
                        for blk in range(2):
                            S.pe(MM(PS[:, 4 + bb // 2, (bb % 2) * 256 + blk * 128:(bb % 2) * 256 + (blk + 1) * 128],
                                    KDM[:, bb, blk * 128:(blk + 1) * 128], TM[:, 512 + blk * 128:512 + (blk + 1) * 128]),
                                 reads=["KDM", "TM"], writes=[bk(4 + bb // 2)])
                    psv = PS[:, 4:6, :].rearrange("p a (b blk n) -> p (a b) blk n", b=2, blk=2)
                    for blk in range(2):
                        for hl in range(2):
                            sl = slice(64 * hl, 64 * hl + 64)
                            S.dve(STT(SST[sl, :, blk, sl], SST[sl, :, blk, sl], C["ret_a"][sl, 1, blk:blk + 1],
                                      psv[sl, :, blk, sl], MUL, ADD),
                                  reads=["SST", "C_ret_a", bk(4), bk(5)], writes=["SST"])
                    store_sst("s_ret", l, g)
            S.act(ACT(OB[:], PS[:, 6, 0:256], AF.Copy), reads=[bk(6)], writes=["OB"])
            S.pool(TT(OB2[:], OB[:], OB[:], MUL), reads=["OB"], writes=["OB2"])
            S.dve(lambda e: e.tensor_reduce(out=STA[:, 4:8], in_=OB[:].rearrange("p (h v) -> p h v", h=4), axis=AX.X, op=ADD),
                  reads=["OB"], writes=["STA"])
            S.dve(lambda e: e.tensor_reduce(out=STA[:, 8:12], in_=OB2[:].rearrange("p (h v) -> p h v", h=4), axis=AX.X, op=ADD),
                  reads=["OB2"], writes=["STA"])
            S.dve(TS(STA[:, 4:8], STA[:, 4:8], 1.0 / 64, None, MUL), reads=["STA"], writes=["STA"])
            S.dve(TT(STB[:, 0:4], STA[:, 4:8], STA[:, 4:8], MUL), reads=["STA"], writes=["STB"])
            S.dve(STT(STB[:, 0:4], STA[:, 8:12], 1.0 / 64, STB[:, 0:4], MUL, SUB), reads=["STA", "STB"], writes=["STB"])
            S.dve(TS(STB[:, 0:4], STB[:, 0:4], 0.0, None, ALU.max), reads=["STB"], writes=["STB"])
            S.act(ACT(STB[:, 0:4], STB[:, 0:4], AF.Ln, bias=EPS), reads=["STB"], writes=["STB"])
            S.act(ACT(STB[:, 0:4], STB[:, 0:4], AF.Exp, scale=-0.5), reads=["STB"], writes=["STB"])
            ov = OB[:].rearrange("p (h v) -> p h v", h=4)
            S.dve(TT(ov, ov, bc(STA[:, 4:8].unsqueeze(2), [128, 4, 64]), SUB), reads=["OB", "STA"], writes=["OB"])
            S.dve(TT(ov, ov, bc(STB[:, 0:4].unsqueeze(2), [128, 4, 64]), MUL), reads=["OB", "STB"], writes=["OB"])
            S.dve(TT(MIX[:, 0:256], OB[:], SGR[:], MUL), reads=["OB", "SGR"], writes=["MIX"])
            if var == 0 and i == LASTP:
                dst = dout["p_ret"][l].rearrange("(blk hl) k v -> hl k blk v", hl=2)
                for hl in range(2):
                    sl = slice(64 * hl, 64 * hl + 64)
                    dma_out("sp", dst[hl], SRET[sl, l, :, sl], "SRET")

        def hgrn(l, i, var):
            nb, e8 = (1, 128) if var == 0 else (SB, SL)
            HBv = HB[:].rearrange("p k (b e) -> p k b e", e=e8)
            S.act(ACT(HE[:], FM[:, 2:4, :], AF.Exp, scale=-1.0), reads=["FM"], writes=["HE"])
            S.act(ACT(HSIG[:], HE[:], AF.Ln, bias=1.0), reads=["HE"], writes=["HSIG"])
            S.act(ACT(HSIG[:], HSIG[:], AF.Exp, scale=-1.0), reads=["HSIG"], writes=["HSIG"])
            for blk in range(2):
                S.dve(TS(HD[:, blk, :], HSIG[:, blk, :], OML[:, l, blk:blk + 1], LOW[:, l, blk:blk + 1], MUL, ADD),
                      reads=["HSIG", "OML", "LOW"], writes=["HD"])
            S.act(ACT(HD[:], HD[:], AF.Ln), reads=["HD"], writes=["HD"])
            S.dve(TT(HKEY[:], HE[:], HSIG[:], MUL), reads=["HE", "HSIG"], writes=["HKEY"])
            S.dve(TT(HKEY[:], HKEY[:], bc(OML[:, l, :].unsqueeze(2), [128, 2, 128]), MUL), reads=["HKEY", "OML"], writes=["HKEY"])
            for blk in range(2):
                S.dve(lambda e, blk=blk: e.tensor_tensor_scan(out=HB[:, blk, :], data0=C["rst"][:, var, :], data1=HD[:, blk, :],
                                                              initial=0.0, op0=MUL, op1=ADD),
                      reads=["HD", "C_rst"], writes=["HB"])
            HB32 = HB[:].rearrange("p k (i e) -> p k i e", e=32)
            S.dve(TS(HREF[:, :, 1:4], HB32[:, :, 0:3, 31], -1.0, None, MUL), reads=["HB"], writes=["HREF"])
            S.dve(TT(HKV[:], bc(HB[:].unsqueeze(1), [128, 4, 2, 128]),
                     bc(HREF[:].rearrange("p k i -> p i k").unsqueeze(3), [128, 4, 2, 128]), ADD), reads=["HB", "HREF"], writes=["HKV"])
            S.dve(TS(HKV[:], HKV[:], -60.0, None, ALU.max), reads=["HKV"], writes=["HKV"])
            S.dve(TT(HKV4[:].rearrange("p k (b e) -> p k b e", e=e8), HBv, bc(HBv[:, :, :, e8 - 1:e8], [128, 2, nb, e8]), SUB),
                  reads=["HB"], writes=["HKV4"])
            S.act(ACT(HKV[:], HKV[:], AF.Exp, scale=-1.0), reads=["HKV"], writes=["HKV"])
            S.act(ACT(HKV4[:], HKV4[:], AF.Exp, scale=-1.0), reads=["HKV4"], writes=["HKV4"])
            S.dve(TT(HKVB[:], HKV[:], bc(HKEY[:].unsqueeze(1), [128, 4, 2, 128]), MUL), reads=["HKV", "HKEY"], writes=["HKVB"])
            S.dve(TT(HKV4[:], HKV4[:], HKEY[:], MUL), reads=["HKV4", "HKEY"], writes=["HKV4"])
            S.act(ACT(HEX[:], HB[:], AF.Exp), reads=["HB"], writes=["HEX"])
            S.dve(TT(HQD[:], HQ[:], HEX[:], MUL), reads=["HQ", "HEX"], writes=["HQD"])
            S.dve(TT(HQA[:].rearrange("p k (i e) -> p k i e", e=32), HB32, bc(HREF[:].unsqueeze(3), [128, 2, 4, 32]), ADD),
                  reads=["HB", "HREF"], writes=["HQA"])
            S.act(ACT(HQA[:], HQA[:], AF.Exp), reads=["HQA"], writes=["HQA"])
            S.dve(TT(HQA[:], HQA[:], HQ[:], MUL), reads=["HQA", "HQ"], writes=["HQA"])
            S.dve(TT(QH[:], bc(HQA[:].unsqueeze(2), [128, 2, 2, 128]), bc(C["hmask"][:].unsqueeze(1).unsqueeze(3), [128, 2, 2, 128]), MUL),
                  reads=["HQA", "C_hmask"], writes=["QH"])
            for h in range(4):
                blk, hl = h // 2, h % 2
                for iv in range(4):
                    S.pe(MM(PS[:, 5, h * 128 + 32 * iv:h * 128 + 32 * iv + 32], HKVB[:, iv, blk, :],
                            QH[:, blk, hl, 32 * iv:32 * iv + 32]), reads=["HKVB", "QH"], writes=[bk(5)])
            S.dve(TT(PTB[:], PS[:, 5, :].rearrange("p (c n) -> p c n", c=4), bc(C["causal"][:, var:var + 1, :], [128, 4, 128]), MUL),
                  reads=[bk(5), "C_causal"], writes=["PTB"])
            VOFF = 1024
            for h in range(4):
                S.pe(MM(PS[:, 6, h * 64:(h + 1) * 64], PTB[:, h, :], VB[:, 256 + h * 64:256 + (h + 1) * 64], start=(h == 0), stop=False),
                     reads=["PTB", "VB"], writes=[bk(6)])
            for blk in range(2):
                S.pe(TR(PS[:, 4, blk * 128:(blk + 1) * 128], HKV4[:, blk, :]), reads=["HKV4", "C_ident"], writes=[bk(4)])
            S.act(ACT(HKT[:], PS[:, 4, 0:256], AF.Copy), reads=[bk(4)], writes=["HKT"])
            S.act(ACT(EBL[:, :, 0:nb], HBv[:, :, :, e8 - 1], AF.Exp), reads=["HB"], writes=["EBL"])
            if var == 0:
                for blk in range(2):
                    S.pe(MM(PS[:, 6, blk * 128:(blk + 1) * 128], HQD[:, blk, :], SHG[:, l, blk, :], start=False, stop=(blk == 1)),
                         reads=["HQD", "SHG"], writes=[bk(6)])
                for blk in range(2):
                    S.pe(MM(PS[:, 7, blk * 128:(blk + 1) * 128], HKT[:, blk * 128:(blk + 1) * 128], TM[:, VOFF + blk * 128:VOFF + (blk + 1) * 128]),
                         reads=["HKT", "TM"], writes=[bk(7)])
                for blk in range(2):
                    for hl in range(2):
                        sl = slice(64 * hl, 64 * hl + 64)
                        S.dve(STT(SHG[sl, l, blk, sl], SHG[sl, l, blk, sl], EBL[sl, blk, 0:1],
                                  PS[sl, 7, blk * 128 + 64 * hl:blk * 128 + 64 * hl + 64], MUL, ADD),
                              reads=["SHG", "EBL", bk(7)], writes=["SHG"])
            else:
                for g in range(4):
                    load_sst("st_hg", l, g)
                    for blk in range(2):
                        S.dve(TT(QM[:], bc(HQD[:, blk, :].unsqueeze(1), [128, 4, 128]), C["colmask"][:, 4 * g:4 * g + 4, :], MUL),
                              reads=["HQD", "C_colmask"], writes=["QM"])
                        for bb in range(4):
                            last = (g == 3 and blk == 1 and bb == 3)
                            S.pe(MM(PS[:, 6, blk * 128:(blk + 1) * 128], QM[:, bb, :], SST[:, bb, blk, :], start=False, stop=last),
                                 reads=["QM", "SST"], writes=[bk(6)])
                    S.dve(TT(KDM[:], bc(HKT[:].unsqueeze(1), [128, 4, 256]),
                             bc(C["rowmask"][:, 4 * g:4 * g + 4].unsqueeze(2), [128, 4, 256]), MUL),
                          reads=["HKT", "C_rowmask"], writes=["KDM"])
                    for bb in range(4):
                        for blk in range(2):
                            S.pe(MM(PS[:, 2 + bb // 2, (bb % 2) * 256 + blk * 128:(bb % 2) * 256 + (blk + 1) * 128],
                                    KDM[:, bb, blk * 128:(blk + 1) * 128], TM[:, VOFF + blk * 128:VOFF + (blk + 1) * 128]),
                                 reads=["KDM", "TM"], writes=[bk(2 + bb // 2)])
                    psv = PS[:, 2:4, :].rearrange("p a (b blk n) -> p (a b) blk n", b=2, blk=2)
                    for blk in range(2):
                        for hl in range(2):
                            sl = slice(64 * hl, 64 * hl + 64)
                            S.dve(TT(SST[sl, :, blk, sl], SST[sl, :, blk, sl], bc(EBL[sl, blk, 4 * g:4 * g + 4].unsqueeze(2), [64, 4, 64]), MUL),
                                  reads=["SST", "EBL"], writes=["SST"])
                            S.dve(TT(SST[sl, :, blk, sl], SST[sl, :, blk, sl], psv[sl, :, blk, sl], ADD),
                                  reads=["SST", bk(2), bk(3)], writes=["SST"])
                    store_sst("s_hg", l, g)
            S.act(ACT(OB[:], PS[:, 6, 0:256], AF.Copy), reads=[bk(6)], writes=["OB"])
            S.pool(TT(OB2[:], OB[:], OB[:], MUL), reads=["OB"], writes=["OB2"])
            S.dve(lambda e: e.tensor_reduce(out=STA[:, 8:12], in_=OB2[:].rearrange("p (h v) -> p h v", h=4), axis=AX.X, op=ADD),
                  reads=["OB2"], writes=["STA"])
            S.act(ACT(STB[:, 0:4], STA[:, 8:12], AF.Ln, scale=1.0 / 64, bias=EPS), reads=["STA"], writes=["STB"])
            S.act(ACT(STB[:, 0:4], STB[:, 0:4], AF.Exp, scale=-0.5), reads=["STB"], writes=["STB"])
            ov = OB[:].rearrange("p (h v) -> p h v", h=4)
            S.dve(TT(ov, ov, bc(STB[:, 0:4].unsqueeze(2), [128, 4, 64]), MUL), reads=["OB", "STB"], writes=["OB"])
            S.dve(TT(MIX[:, 256:512], OB[:], SGH[:], MUL), reads=["OB", "SGH"], writes=["MIX"])
            if var == 0 and i == LASTP:
                dst = dout["p_hg"][l].rearrange("(blk hl) k v -> hl k blk v", hl=2)
                for hl in range(2):
                    sl = slice(64 * hl, 64 * hl + 64)
                    dma_out("sp", dst[hl], SHG[sl, l, :, sl], "SHG")

        def conv_phase(l, i, var):
            if var == 0:
                S.pool(CP(XPW[:, :, 0:3], XPH[:, l]), reads=["XPH"], writes=["XPW"])
                S.pool(CP(XPW[:, :, 3:131], FM[:, 4:10, :]), reads=["FM"], writes=["XPW"])
                for j in range(4):
                    src = XPW[:, :, j:j + 128]
                    w = bc(CW[:, l, :, j:j + 1], [128, 6, 128])
                    if j == 0:
                        S.pool(TT(XC[:], src, w, MUL), reads=["XPW", "CW"], writes=["XC"])
                    else:
                        S.pool(TT(CTMPP[:], src, w, MUL), reads=["XPW", "CW"], writes=["CTMPP"])
                        S.pool(TT(XC[:], XC[:], CTMPP[:], ADD), reads=["XC", "CTMPP"], writes=["XC"])
                S.pool(CP(XPH[:, l], XPW[:, :, 128:131]), reads=["XPW"], writes=["XPH"])
                if i == LASTP:
                    for j in range(3):
                        S.dma("sp", lambda e, j=j: e.dma_start(out=dout["p_conv"][l, j].rearrange("(b p) -> p b", p=128),
                                                               in_=XPH[:, l, :, j], allow_slow_non_contiguous=True),
                              reads=["XPH"], dkey=("st", "XPH"))
            else:
                dma_in("sp", CVT[0:48, :], din["st_conv"][l], "CVT")
                for c6 in range(6):
                    S.pe(TR(PS[:, 4, c6 * 48:(c6 + 1) * 48], CVT[0:48, c6 * 128:(c6 + 1) * 128], n=48), reads=["CVT", "C_ident"], writes=[bk(4)])
                S.act(ACT(XPS[:, :, :, 0:3], PS[:, 4, 0:288].rearrange("p (c b j) -> p c b j", c=6, b=SB), AF.Copy),
                      reads=[bk(4)], writes=["XPS"])
                S.pool(CP(XPS[:, :, :, 3:11], FM[:, 4:10, :].rearrange("p c (b e) -> p c b e", e=SL)), reads=["FM"], writes=["XPS"])
                cav = CACC[:].rearrange("p c (b e) -> p c b e", e=SL)
                ctv = CTMPS[:].rearrange("p c (b e) -> p c b e", e=SL)
                for j in range(4):
                    src = XPS[:, :, :, j:j + SL]
                    w = bc(CW[:, l, :, j:j + 1].unsqueeze(3), [128, 6, SB, SL])
                    if j == 0:
                        S.pool(TT(cav, src, w, MUL), reads=["XPS", "CW"], writes=["CACC"])
                    else:
                        S.pool(TT(ctv, src, w, MUL), reads=["XPS", "CW"], writes=["CTMPS"])
                        S.pool(TT(CACC[:], CACC[:], CTMPS[:], ADD), reads=["CACC", "CTMPS"], writes=["CACC"])
                S.pool(CP(CTMPS[:, :, 0:48].rearrange("p c (b j) -> p c b j", j=3), XPS[:, :, :, 8:11]), reads=["XPS", "CACC"], writes=["CTMPS"])
                for c6 in range(6):
                    bnk, off = (5, c6 * 128) if c6 < 4 else (6, (c6 - 4) * 128)
                    S.pe(TR(PS[0:48, bnk, off:off + 128], CTMPS[:, c6, 0:48]), reads=["CTMPS", "C_ident"], writes=[bk(bnk)])
                S.act(ACT(CVT[0:48, 0:512], PS[0:48, 5, :], AF.Copy), reads=[bk(5)], writes=["CVT"])
                S.act(ACT(CVT[0:48, 512:768], PS[0:48, 6, 0:256], AF.Copy), reads=[bk(6)], writes=["CVT"])
                dma_out("sp", dout["s_conv"][l], CVT[0:48, :], "CVT")

        def conv_silu(l, var):
            for c6 in range(6):
                if var == 0:
                    S.act(ACT(XC[:, c6, :], XC[:, c6, :], AF.Silu, bias=CB[:, l, c6:c6 + 1]), reads=["XC", "CB"], writes=["XC"])
                else:
                    S.act(ACT(XC[:, c6, :], CACC[:, c6, :], AF.Silu, bias=CB[:, l, c6:c6 + 1]), reads=["CACC", "CB"], writes=["XC"])

        def silu_phase(l):
            S.act(ACT(SGR[:], TM[:, 768:1024], AF.Silu), reads=["TM"], writes=["SGR"])
            S.act(ACT(SGH[:], TM[:, 1280:1536], AF.Silu), reads=["TM"], writes=["SGH"])
            S.act(ACT(SSZ[:], TM[:, 1536:1792], AF.Silu), reads=["TM"], writes=["SSZ"])
            S.act(ACT(HQ[:], FM[:, 0:2, :], AF.Silu), reads=["FM"], writes=["HQ"])

        def ssd(l, i, var):
            nb, e8 = (1, 128) if var == 0 else (SB, SL)
            S.dve(TT(DEL[:], TM[:, 1792:1796], DTB[:, l, :], ADD), reads=["TM", "DTB"], writes=["DEL"])
            S.act(ACT(DEL[:], DEL[:], AF.Exp), reads=["DEL"], writes=["DEL"])
            S.act(ACT(DEL[:], DEL[:], AF.Ln, bias=1.0), reads=["DEL"], writes=["DEL"])
            S.act(ACT(LND[:], DEL[:], AF.Ln), reads=["DEL"], writes=["LND"])
            S.dve(TT(LOGA[:], DEL[:], NEGA[:, l, :], MUL), reads=["DEL", "NEGA"], writes=["LOGA"])
            S.pe(MM(PS[:, 7, 0:4], C["causal"][:, var, :], LOGA[:]), reads=["C_causal", "LOGA"], writes=[bk(7)])
            S.pe(MM(PS[:, 7, 4:8], C["sameseq"][:, var, :], LOGA[:]), reads=["C_sameseq", "LOGA"], writes=[bk(7)])
            S.dve(TT(BCOL[:], PS[:, 7, 0:4], LND[:], SUB), reads=[bk(7), "LND"], writes=["BCOL"])
            S.dve(TT(BLAST[:], PS[:, 7, 4:8], BCOL[:], SUB), reads=[bk(7), "BCOL"], writes=["BLAST"])
            S.act(ACT(KSC[:], BLAST[:], AF.Exp), reads=["BLAST"], writes=["KSC"])
            S.pool(CP(LB[:], bc(LOGA[:].unsqueeze(2), [128, 4, 128])), reads=["LOGA"], writes=["LB"])
            for h in range(4):
                S.pe(MM(PS[:, 4, h * 128:(h + 1) * 128], LB[:, h, :], C["causal"][:, var, :]), reads=["LB", "C_causal"], writes=[bk(4)])
            p4 = PS[:, 4, :].rearrange("p (c n) -> p c n", c=4)
            S.dve(TT(D1[:], p4, bc(BCOL[:].unsqueeze(2), [128, 4, 128]), SUB), reads=[bk(4), "BCOL"], writes=["D1"])
            S.pool(TT(D1[:], D1[:], bc(C["negmask"][:, var:var + 1, :], [128, 4, 128]), ADD), reads=["D1", "C_negmask"], writes=["D1"])
            S.act(ACT(DEC[:], D1[:], AF.Exp), reads=["D1"], writes=["DEC"])
            S.act(ACT(EBT[:], p4, AF.Exp), reads=[bk(4)], writes=["EBT"])
            for g in range(2):
                S.pe(MM(PS[:, 5, g * 128:(g + 1) * 128], XC[:, 2 + g, :], XC[:, 4 + g, :]), reads=["XC"], writes=[bk(5)])
            for g in range(2):
                S.dve(TT(PT2B[:, 2 * g:2 * g + 2, :], bc(PS[:, 5, g * 128:(g + 1) * 128].unsqueeze(1), [128, 2, 128]), DEC[:, 2 * g:2 * g + 2, :], MUL),
                      reads=[bk(5), "DEC"], writes=["PT2B"])
                S.pool(TT(CMD[:, 2 * g:2 * g + 2, :], bc(XC[:, 4 + g, :].unsqueeze(1), [128, 2, 128]), EBT[:, 2 * g:2 * g + 2, :], MUL),
                       reads=["XC", "EBT"], writes=["CMD"])
            for b4 in range(4):
                S.pe(TR(PS[:, 6, b4 * 128:(b4 + 1) * 128], XC[:, b4, :]), reads=["XC", "C_ident"], writes=[bk(6)])
            S.act(ACT(XS[:], PS[:, 6, 0:256], AF.Copy), reads=[bk(6)], writes=["XS"])
            S.act(ACT(XSB[:], PS[:, 6, 0:256], AF.Copy), reads=[bk(6)], writes=["XSB"])
            S.act(ACT(BMT[:], PS[:, 6, 256:512].rearrange("p (g n) -> p g n", g=2), AF.Copy), reads=[bk(6)], writes=["BMT"])
            for g in range(2):
                S.pool(TT(BD[:, 2 * g:2 * g + 2, :], bc(BMT[:, g, :].unsqueeze(1), [128, 2, 128]),
                          bc(KSC[:, 2 * g:2 * g + 2].unsqueeze(2), [128, 2, 128]), MUL), reads=["BMT", "KSC"], writes=["BD"])
            for h in range(4):
                S.pe(MM(PS[:, 7, h * 64:(h + 1) * 64], PT2B[:, h, :], XSB[:, h * 64:(h + 1) * 64], start=(h == 0), stop=False),
                     reads=["PT2B", "XSB"], writes=[bk(7)])
            if var == 0:
                for h in range(4):
                    S.pe(MM(PS[:, 7, h * 64:(h + 1) * 64], CMD[:, h, :], SSSD[:, l, h, :], start=False, stop=(h == 3)),
                         reads=["CMD", "SSSD"], writes=[bk(7)])
                for h in range(4):
                    S.pe(MM(PS[:, 5, h * 64:(h + 1) * 64], BD[:, h, :], XS[:, h * 64:(h + 1) * 64]), reads=["BD", "XS"], writes=[bk(5)])
                S.dve(TT(SSSD[:, l], SSSD[:, l], bc(EBT[:, :, 127:128], [128, 4, 64]), MUL), reads=["SSSD", "EBT"], writes=["SSSD"])
                S.dve(TT(SSSD[:, l], SSSD[:, l], PS[:, 5, 0:256].rearrange("p (h v) -> p h v", h=4), ADD),
                      reads=["SSSD", bk(5)], writes=["SSSD"])
            else:
                EBTv = EBT[:].rearrange("p h (b e) -> p h b e", e=SL)
                for g in range(4):
                    dma_in("sp", SST2[:], din["st_ssd"][l, 4 * g:4 * g + 4].rearrange("b h n v -> n b h v"), "SST2")
                    for h in range(4):
                        S.dve(TT(QM[:], bc(CMD[:, h, :].unsqueeze(1), [128, 4, 128]), C["colmask"][:, 4 * g:4 * g + 4, :], MUL),
                              reads=["CMD", "C_colmask"], writes=["QM"])
                        for bb in range(4):
                            last = (g == 3 and h == 3 and bb == 3)
                            S.pe(MM(PS[:, 7, h * 64:(h + 1) * 64], QM[:, bb, :], SST2[:, bb, h, :], start=False, stop=last),
                                 reads=["QM", "SST2"], writes=[bk(7)])
                        S.dve(TT(BDM[:], bc(BD[:, h, :].unsqueeze(1), [128, 4, 128]),
                                 bc(C["rowmask"][:, 4 * g:4 * g + 4].unsqueeze(2), [128, 4, 128]), MUL),
                              reads=["BD", "C_rowmask"], writes=["BDM"])
                        for bb in range(4):
                            col = ((bb % 2) * 4 + h) * 64
                            S.pe(MM(PS[:, 2 + bb // 2, col:col + 64], BDM[:, bb, :], XS[:, h * 64:(h + 1) * 64]),
                                 reads=["BDM", "XS"], writes=[bk(2 + bb // 2)])
                    psv = PS[:, 2:4, :].rearrange("p a (b h v) -> p (a b) h v", b=2, h=4)
                    ach = EBTv[:, :, 4 * g:4 * g + 4, SL - 1].rearrange("p h b -> p b h")
                    S.dve(TT(SST2[:], SST2[:], bc(ach.unsqueeze(3), [128, 4, 4, 64]), MUL), reads=["SST2", "EBT"], writes=["SST2"])
                    S.dve(TT(SST2[:], SST2[:], psv, ADD), reads=["SST2", bk(2), bk(3)], writes=["SST2"])
                    dma_out("sp", dout["s_ssd"][l, 4 * g:4 * g + 4].rearrange("b h n v -> n b h v"), SST2[:], "SST2")
            S.act(ACT(OB[:], PS[:, 7, 0:256], AF.Copy), reads=[bk(7)], writes=["OB"])
            ov = OB[:].rearrange("p (h v) -> p h v", h=4)
            S.pool(TT(OB2[:].rearrange("p (h v) -> p h v", h=4), XS[:].rearrange("p (h v) -> p h v", h=4),
                      bc(SSDD[:, l, :].unsqueeze(2), [128, 4, 64]), MUL), reads=["XS", "SSDD"], writes=["OB2"])
            S.dve(TT(OB[:], OB[:], OB2[:], ADD), reads=["OB", "OB2"], writes=["OB"])
            S.dve(TT(OB[:], OB[:], SSZ[:], MUL), reads=["OB", "SSZ"], writes=["OB"])
            S.pool(TT(OB2[:], OB[:], OB[:], MUL), reads=["OB"], writes=["OB2"])
            S.dve(lambda e: e.tensor_reduce(out=STA[:, 8:10], in_=OB2[:].rearrange("p (g v) -> p g v", g=2), axis=AX.X, op=ADD),
                  reads=["OB2"], writes=["STA"])
            S.act(ACT(STB[:, 0:2], STA[:, 8:10], AF.Ln, scale=1.0 / 128, bias=EPS), reads=["STA"], writes=["STB"])
            S.act(ACT(STB[:, 0:2], STB[:, 0:2], AF.Exp, scale=-0.5), reads=["STB"], writes=["STB"])
            og = OB[:].rearrange("p (g v) -> p g v", g=2)
            S.dve(TT(og, og, bc(STB[:, 0:2].unsqueeze(2), [128, 2, 128]), MUL), reads=["OB", "STB"], writes=["OB"])
            S.dve(TT(MIX[:, 512:768], OB[:], SNORM[:], MUL), reads=["OB", "SNORM"], writes=["MIX"])
            if var == 0 and i == LASTP:
                dma_out("sp", dout["p_ssd"][l].rearrange("h n v -> n h v"), SSSD[:, l], "SSSD")

        def s5(l, i, var, hook=None):
            nb, e8 = (1, 128) if var == 0 else (SB, SL)
            if var == 1:
                for ri, nm in ((0, "st_s5re"), (1, "st_s5im")):
                    dma_in("sp", ZT[0:16, :], din[nm][l], "ZT")
                    for c in range(8):
                        S.pe(TR(PS[:, 4, c * 16:(c + 1) * 16], ZT[0:16, c * 128:(c + 1) * 128], n=16), reads=["ZT", "C_ident"], writes=[bk(4)])
                    S.act(ACT(Z0[:, ri], PS[:, 4, 0:128].rearrange("p (c b) -> p c b", b=SB), AF.Copy), reads=[bk(4)], writes=["Z0"])
            if var == 0:
                zr, zi, zk = ZS5[:, l, 0, :].unsqueeze(2), ZS5[:, l, 1, :].unsqueeze(2), "ZS5"
            else:
                zr, zi, zk = Z0[:, 0], Z0[:, 1], "Z0"
            ar = bc(ABAR[:, 0, :].unsqueeze(2), [128, 8, nb])
            ai = bc(ABAR[:, 1, :].unsqueeze(2), [128, 8, nb])
            S.dve(TT(INJ[:, 0, :, 0:nb], zr, ar, MUL), reads=[zk, "ABAR"], writes=["INJ"])
            S.dve(TT(INJ2[:, 0, :, 0:nb], zi, ai, MUL), reads=[zk, "ABAR"], writes=["INJ2"])
            S.dve(TT(INJ[:, 0, :, 0:nb], INJ[:, 0, :, 0:nb], INJ2[:, 0, :, 0:nb], SUB), reads=["INJ", "INJ2"], writes=["INJ"])
            S.dve(TT(INJ[:, 1, :, 0:nb], zi, ar, MUL), reads=[zk, "ABAR"], writes=["INJ"])
            S.dve(TT(INJ2[:, 1, :, 0:nb], zr, ai, MUL), reads=[zk, "ABAR"], writes=["INJ2"])
            S.dve(TT(INJ[:, 1, :, 0:nb], INJ[:, 1, :, 0:nb], INJ2[:, 1, :, 0:nb], ADD), reads=["INJ", "INJ2"], writes=["INJ"])
            S.dve(TT(PI[:], bc(RR[:].unsqueeze(2), [128, 8, 128]), bc(C["rst"][:, var:var + 1, :], [128, 8, 128]), MUL),
                  reads=["RR", "C_rst"], writes=["PI"])
            for ri in range(2):
                for c in range(8):
                    bnk = 4 + 2 * ri + c // 4
                    S.pe(MM(PS[:, bnk, (c % 4) * 128:(c % 4 + 1) * 128], BCF[:, ri, c, :], FM[:, 10 + c // 4, :]),
                         reads=["BCF", "FM"], writes=[bk(bnk)])
            if hook is not None:
                hook()
            pre = PS[:, 4:6, :].rearrange("p a (c n) -> p (a c) n", n=128)
            pim = PS[:, 6:8, :].rearrange("p a (c n) -> p (a c) n", n=128)
            S.dve(TT(T1[:], pre, COSR[:], MUL), reads=[bk(4), bk(5), "COSR"], writes=["T1"])
            S.dve(TT(T2[:], pim, SINR[:], MUL), reads=[bk(6), bk(7), "SINR"], writes=["T2"])
            S.pool(TT(XR[:], T1[:], T2[:], ADD), reads=["T1", "T2"], writes=["XR"])
            S.dve(TT(WR[:], pim, COSR[:], MUL), reads=[bk(6), bk(7), "COSR"], writes=["WR"])
            S.dve(TT(WI[:], pre, SINR[:], MUL), reads=[bk(4), bk(5), "SINR"], writes=["WI"])
            S.pool(TT(XI[:], WR[:], WI[:], SUB), reads=["WR", "WI"], writes=["XI"])
            xrv = XR[:].rearrange("p c (b e) -> p c b e", e=e8)
            xiv = XI[:].rearrange("p c (b e) -> p c b e", e=e8)
            S.dve(TT(xrv[:, :, :, 0], xrv[:, :, :, 0], INJ[:, 0, :, 0:nb], ADD), reads=["XR", "INJ"], writes=["XR"])
            S.dve(TT(xiv[:, :, :, 0], xiv[:, :, :, 0], INJ[:, 1, :, 0:nb], ADD), reads=["XI", "INJ"], writes=["XI"])
            fl = "p c t -> p (c t)"
            S.dve(lambda e: e.tensor_tensor_scan(out=WR[:].rearrange(fl), data0=PI[:].rearrange(fl), data1=XR[:].rearrange(fl),
                                                 initial=0.0, op0=MUL, op1=ADD), reads=["PI", "XR"], writes=["WR"])
            S.dve(lambda e: e.tensor_tensor_scan(out=WI[:].rearrange(fl), data0=PI[:].rearrange(fl), data1=XI[:].rearrange(fl),
                                                 initial=0.0, op0=MUL, op1=ADD), reads=["PI", "XI"], writes=["WI"])
            S.dve(TT(PR[:], WR[:], COSR[:], MUL), reads=["WR", "COSR"], writes=["PR"])
            S.dve(TT(PI[:], WI[:], SINR[:], MUL), reads=["WI", "SINR"], writes=["PI"])
            S.dve(TT(T2[:], WR[:], SINR[:], MUL), reads=["WR", "SINR"], writes=["T2"])
            S.dve(TT(T1[:], WI[:], COSR[:], MUL), reads=["WI", "COSR"], writes=["T1"])
            if var == 0:
                S.dve(TT(ZS5[:, l, 0, :], PR[:, :, 127], PI[:, :, 127], SUB), reads=["PR", "PI"], writes=["ZS5"])
                S.dve(TT(ZS5[:, l, 1, :], T1[:, :, 127], T2[:, :, 127], ADD), reads=["T1", "T2"], writes=["ZS5"])
            else:
                def lastv(t_):
                    return t_[:].rearrange("p c (b e) -> p c b e", e=SL)[:, :, :, SL - 1]
                S.dve(TT(ZL[:, 0], lastv(PR), lastv(PI), SUB), reads=["PR", "PI"], writes=["ZL"])
                S.dve(TT(ZL[:, 1], lastv(T1), lastv(T2), ADD), reads=["T1", "T2"], writes=["ZL"])
            S.dve(TT(XRB[:], PR[:], PI[:], SUB), reads=["PR", "PI"], writes=["XRB"])
            S.dve(TT(XIB[:], T1[:], T2[:], ADD), reads=["T1", "T2"], writes=["XIB"])
            for c in range(8):
                S.pe(MM(PS[:, 4, 32 * c:32 * c + 32], XRB[:, c, :], S5CB[:, c, :], start=(c == 0), stop=False), reads=["XRB", "S5CB"], writes=[bk(4)])
                S.pe(MM(PS[:, 4, 32 * c:32 * c + 32], XIB[:, c, :], S5CIB[:, c, :], start=False, stop=False), reads=["XIB", "S5CIB"], writes=[bk(4)])
            for ub in range(2):
                S.pe(MM(PS[:, 4, ub * 128:(ub + 1) * 128], FM[:, 10 + ub, :], DIAGD[:, ub, :], start=False, stop=(ub == 1)),
                     reads=["FM", "DIAGD"], writes=[bk(4)])
            S.act(ACT(YG[:], PS[:, 4, 0:256], AF.Copy), reads=[bk(4)], writes=["YG"])
            S.act(ACT(YU[:], PS[:, 4, 0:256], AF.Square), reads=[bk(4)], writes=["YU"])
            S.dve(TS(YU[:], YU[:], 0.044715, 1.0, MUL, ADD), reads=["YU"], writes=["YU"])
            S.dve(TT(YU[:], YU[:], YG[:], MUL), reads=["YU", "YG"], writes=["YU"])
            S.dve(TS(YU[:], YU[:], -26.0, None, ALU.max), reads=["YU"], writes=["YU"])
            S.act(ACT(YU[:], YU[:], AF.Exp, scale=-2.0 * math.sqrt(2.0 / math.pi)), reads=["YU"], writes=["YU"])
            S.act(ACT(YU[:], YU[:], AF.Ln, bias=1.0), reads=["YU"], writes=["YU"])
            S.act(ACT(YU[:], YU[:], AF.Exp, scale=-1.0), reads=["YU"], writes=["YU"])
            S.dve(TT(YE[:], YG[:], YU[:], MUL), reads=["YG", "YU"], writes=["YE"])
            for ub in range(2):
                S.pe(TR(PS[:, 5, ub * 128:(ub + 1) * 128], YE[:, ub * 128:(ub + 1) * 128]), reads=["YE", "C_ident"], writes=[bk(5)])
            S.act(ACT(YGT[:], PS[:, 5, 0:256].rearrange("p (k n) -> p k n", k=2), AF.Copy), reads=[bk(5)], writes=["YGT"])
            for k in range(2):
                S.pe(MM(PS[:, 6, 0:256], YGT[:, k, :], GLUW[:, k, :], start=(k == 0), stop=(k == 1)), reads=["YGT", "GLUW"], writes=[bk(6)])
            S.dve(STT(YU[:], PS[:, 6, 0:256], -40.0, GLUB[:], ALU.max, ADD), reads=[bk(6), "GLUB"], writes=["YU"])
            S.act(ACT(YU[:], YU[:], AF.Exp, scale=-1.0), reads=["YU"], writes=["YU"])
            S.act(ACT(YU[:], YU[:], AF.Ln, bias=1.0), reads=["YU"], writes=["YU"])
            S.act(ACT(YU[:], YU[:], AF.Exp, scale=-1.0), reads=["YU"], writes=["YU"])
            S.dve(TT(MIX[:, 768:1024], YE[:], YU[:], MUL), reads=["YE", "YU"], writes=["MIX"])
            if var == 0 and i == LASTP:
                for ri in range(2):
                    S.pe(TR(PS[0:8, 7, ri * 128:(ri + 1) * 128], ZS5[:, l, ri, :]), reads=["ZS5", "C_ident"], writes=[bk(7)])
                S.act(ACT(ZT[0:8, 0:256], PS[0:8, 7, 0:256], AF.Copy), reads=[bk(7)], writes=["ZT"])
                dma_out("sp", dout["p_s5re"][l], ZT[0:8, 0:128], "ZT")
                dma_out("sp", dout["p_s5im"][l], ZT[0:8, 128:256], "ZT")
            if var == 1:
                for ri, nm in ((0, "s_s5re"), (1, "s_s5im")):
                    for c in range(8):
                        S.pe(TR(PS[0:16, 6 + c // 4, (c % 4) * 128:(c % 4 + 1) * 128], ZL[:, ri, c, :]), reads=["ZL", "C_ident"],
                             writes=[bk(6 + c // 4)])
                    S.act(ACT(ZT[0:16, 0:512], PS[0:16, 6, :], AF.Copy), reads=[bk(6)], writes=["ZT"])
                    S.act(ACT(ZT[0:16, 512:1024], PS[0:16, 7, :], AF.Copy), reads=[bk(7)], writes=["ZT"])
                    dma_out("sp", dout[nm][l], ZT[0:16, :], "ZT")

        def phase1(l, j, i, hook=None):
            var = 0 if i < NPT else 1
            xk = ("X", j)
            if var == 1 and "ssd" in mixers:
                conv_phase(l, i, var)
                conv_silu(l, var)
            silu_phase(l)
            if var == 1:
                S.pool(lambda e: e.memset(SST[:], 0.0), writes=["SST"])
                if "s5" in mixers:
                    s5_sample_tables()
            if "ret" in mixers:
                retention(l, i, var)
            if var == 0 and "ssd" in mixers:
                conv_phase(l, i, var)
            if "hg" in mixers:
                hgrn(l, i, var)
            if "ssd" in mixers:
                if var == 0:
                    conv_silu(l, var)
                ssd(l, i, var)
            if "s5" in mixers:
                s5(l, i, var, hook)
            elif hook is not None:
                hook()
            for half in range(2):
                for cc in range(4):
                    c = half * 4 + cc
                    S.pe(TR(PS[:, half, cc * 128:(cc + 1) * 128], MIX[:, c * 128:(c + 1) * 128]), reads=["MIX", "C_ident"], writes=[bk(half)])
                src = PS[:, half, :].rearrange("p (c n) -> p c n", c=4)
                if half == 0:
                    S.act(ACT(MIXT[:, 0:4, :], src, AF.Copy), reads=[bk(0)], writes=["MIXT"])
                else:
                    S.dve(CP(MIXT[:, 4:8, :], src), reads=[bk(1)], writes=["MIXT"])
            for n in range(2):
                for k in range(8):
                    S.pe(MM(PS[:, 2 + n, :], MIXT[:, k, :], WOUT[:, k, n * 512:(n + 1) * 512], start=(k == 0), stop=(k == 7)),
                         reads=["MIXT", "WOUT"], writes=[bk(2 + n)])
            S.dve(TT(X[:, j, :], X[:, j, :], PS[:, 2:4, :].rearrange("p a n -> p (a n)"), ADD),
                  reads=[xk, bk(2), bk(3)], writes=[xk])

        def norm1(l, j, par):
            norm_T(X[:, j, :], ("X", j), NM[:, l, :], "NM", lambda h: HTS[par][:, 4 * h:4 * h + 4, :], "HT%d" % par)

        def norm2(l, j):
            if do_mlp:
                norm_T(X[:, j, :], ("X", j), NMLP[:, l, :], "NMLP", lambda h: H2T[:, j, 4 * h:4 * h + 4, :], ("H2T", j))

        def mlp_issue(l, ch):
            r = ch % 2
            S.dma("pool", lambda e: e.dma_start(out=WUP[r], in_=din["w_up"][l][:, ch * FCH:(ch + 1) * FCH].rearrange("(c p) n -> p c n", p=128)),
                  writes=["WUP%d" % r])
            S.dma("pool", lambda e: e.dma_start(out=WDN[r], in_=din["w_down"][l][ch * FCH:(ch + 1) * FCH, :].rearrange("(f p) n -> p f n", p=128)),
                  writes=["WDN%d" % r])

        def mlp(l, ntile):
            pairs = [(a, min(2, ntile - a)) for a in range(0, ntile, 2)]
            mlp_issue(l, 0)
            mlp_issue(l, 1)
            NF = FCH // 128
            dbank = [0]
            for ch in range(NFC):
                r = ch % 2
                for pi2, (j0, ln) in enumerate(pairs):
                    N = 128 * ln
                    hk = [("H2T", j0 + t) for t in range(ln)]
                    at = ACTT2[pi2 % 2]
                    ak = "ACTT%d" % (pi2 % 2)
                    for f in range(NF):
                        ub = f % 2
                        for k in range(8):
                            S.pe(MM(PS[:, ub, 0:N], WUP[r][:, k, f * 128:(f + 1) * 128], H2T[:, j0:j0 + ln, k, :], start=(k == 0), stop=(k == 7)),
                                 reads=hk + ["WUP%d" % r], writes=[bk(ub)])
                        S.act(ACT(RELU[ub][:, 0:N], PS[:, ub, 0:N], AF.Relu), reads=[bk(ub)], writes=["RELU%d" % ub])
                        S.pool(TT(at[:, f, 0:N], RELU[ub][:, 0:N], RELU[ub][:, 0:N], MUL), reads=["RELU%d" % ub], writes=[ak])
                for pi2, (j0, ln) in enumerate(pairs):
                    at = ACTT2[pi2 % 2]
                    ak = "ACTT%d" % (pi2 % 2)
                    for t in range(ln):
                        b0 = 2 + 2 * (dbank[0] % 3)
                        dbank[0] += 1
                        for n in range(2):
                            for f in range(NF):
                                S.pe(MM(PS[:, b0 + n, :], at[:, f, t * 128:(t + 1) * 128], WDN[r][:, f, n * 512:(n + 1) * 512],
                                        start=(f == 0), stop=(f == NF - 1)),
                                     reads=[ak, "WDN%d" % r], writes=[bk(b0 + n)])
                        xk = ("X", j0 + t)
                        S.dve(TT(X[:, j0 + t, :], X[:, j0 + t, :], PS[:, b0:b0 + 2, :].rearrange("p a n -> p (a n)"), ADD),
                              reads=[xk, bk(b0), bk(b0 + 1)], writes=[xk])
                if ch + 2 < NFC:
                    mlp_issue(l, ch + 2)

        def w_issue(l):
            S.dma("pool", lambda e: e.dma_start(out=WIN[:], in_=din["w_in"][l].rearrange("(c p) n -> p c n", p=128)), writes=["WIN"])
            S.dma("pool", lambda e: e.dma_start(out=WOUT[:], in_=din["w_out"][l].rearrange("(c p) n -> p c n", p=128)), writes=["WOUT"])

        def final_norm(tiles):
            dma_in("sp", WF[:], din["norm_final"].partition_broadcast(128), "WF")
            for j, i in enumerate(tiles):
                xk = ("X", j)
                S.act(ACT(JUNK, X[:, j, :], AF.Square, accum_out=STA[:, 0:1]), reads=[xk], writes=["JUNK", "STAN"])
                S.act(ACT(STA[:, 1:2], STA[:, 0:1], AF.Ln, scale=1.0 / D, bias=EPS), reads=["STAN"], writes=["STAN"])
                S.act(ACT(STA[:, 2:3], STA[:, 1:2], AF.Exp, scale=-0.5), reads=["STAN"], writes=["STAN"])
                S.act(ACT(YOUT[:], X[:, j, :], AF.Copy, scale=STA[:, 2:3]), reads=[xk, "STAN"], writes=["YOUT"])
                S.dve(TT(YOUT[:], YOUT[:], WF[:], MUL), reads=["YOUT", "WF"], writes=["YOUT"])
                if i < NPT:
                    dma_out("sp", dout["yp"][i * 128:(i + 1) * 128, :], YOUT[:], "YOUT")
                else:
                    dma_out("sp", dout["ys"], YOUT[:], "YOUT")

        plan = [(g, l) for g in range(len(sgs)) for l in range(n_layers)]
        w_issue(plan[0][1])
        for pi_, (g, l) in enumerate(plan):
            tiles = sgs[g]
            if l == 0:
                for j, i in enumerate(tiles):
                    src = din["xp"][i * 128:(i + 1) * 128, :] if i < NPT else din["xs"]
                    dma_in("sp", X[:, j, :], src, ("X", j))
            layer_params(l)
            if "s5" in mixers:
                s5_setup(l)
            norm1(l, 0, 0)
            for j, i in enumerate(tiles):
                proj_in(j % 2)

                def hook(j=j, l=l, n=len(tiles)):
                    if j > 0:
                        norm2(l, j - 1)
                    if j + 1 < n:
                        norm1(l, j + 1, (j + 1) % 2)

                phase1(l, j, i, hook)
            norm2(l, len(tiles) - 1)
            if pi_ + 1 < len(plan):
                w_issue(plan[pi_ + 1][1])
            if do_mlp:
                mlp(l, len(tiles))
            if l == n_layers - 1:
                final_norm(tiles)
        S.emit()
    return nc, S


_CONSTS = None


def _host_params(inp):
    f = np.float32
    p = {}
    p["nm_fm"] = np.ascontiguousarray(inp["norm_mix"].reshape(2, 8, 128).transpose(0, 2, 1))
    p["nmlp_fm"] = np.ascontiguousarray(inp["norm_mlp"].reshape(2, 8, 128).transpose(0, 2, 1))
    p["norm_final"] = np.ascontiguousarray(inp["norm_final"])
    for k in ("w_in", "w_out", "w_up", "w_down"):
        p[k] = np.ascontiguousarray(inp[k])
    p["hg_lg"] = np.ascontiguousarray(inp["hg_lb_logits"].reshape(2, 2, 128).transpose(2, 1, 0))
    p["conv_w"] = np.ascontiguousarray(inp["ssd_conv_w"].reshape(2, 4, 6, 128).transpose(0, 3, 2, 1))
    p["conv_b"] = np.ascontiguousarray(inp["ssd_conv_b"].reshape(2, 6, 128).transpose(0, 2, 1))
    p["dt_bias"] = np.ascontiguousarray(inp["ssd_dt_bias"])
    p["a_log"] = np.ascontiguousarray(inp["ssd_a_log"])
    p["ssd_d"] = np.ascontiguousarray(inp["ssd_d"])
    p["ssd_norm"] = np.ascontiguousarray(inp["ssd_norm"])

    def fm8(a):
        return np.ascontiguousarray(a.reshape(2, 8, 2, 64).transpose(0, 2, 3, 1).reshape(2, 128, 8))

    p["s5_are"] = fm8(inp["s5_a_re"])
    p["s5_aim"] = fm8(inp["s5_a_im"])
    p["s5_ldt"] = fm8(np.broadcast_to(inp["s5_log_dt"][:, :, None], (2, 16, 64)))
    for nm, src in (("s5_bre", inp["s5_b_re"]), ("s5_bim", inp["s5_b_im"])):
        o = np.zeros((2, 128, 8, 2, 64), f)
        for c in range(8):
            for gl in range(2):
                g = 2 * c + gl
                r0 = 32 * (c % 4) + 16 * gl
                o[:, r0:r0 + 16, c, gl, :] = src[:, g].transpose(0, 2, 1)
        p[nm] = o.reshape(2, 128, 8, 128)
    for nm, src in (("s5_cre", inp["s5_c_re"]), ("s5_cim", inp["s5_c_im"])):
        o = np.zeros((2, 2, 64, 8, 2, 16), f)
        for c in range(8):
            for gl in range(2):
                o[:, gl, :, c, gl, :] = src[:, 2 * c + gl].transpose(0, 2, 1)
        p[nm] = o.reshape(2, 128, 8, 32)
    p["s5_dfm"] = np.ascontiguousarray(inp["s5_d"].reshape(2, 2, 128).transpose(0, 2, 1))
    p["glu_w"] = np.ascontiguousarray(inp["s5_glu_w"])
    p["glu_b"] = np.ascontiguousarray(inp["s5_glu_b"])
    return p


def _core_inputs(inp, shared, consts, c):
    m = dict(shared)
    for k, v in consts.items():
        m["c_" + k] = v
    b0 = c * SB
    m["xp"] = np.ascontiguousarray(inp["x_prompt"][c])
    m["xs"] = np.ascontiguousarray(inp["x_sample"][b0:b0 + SB].reshape(128, D))
    m["st_ret"] = np.ascontiguousarray(inp["state_ret"][:, b0:b0 + SB])
    m["st_hg"] = np.ascontiguousarray(inp["state_hgrn"][:, b0:b0 + SB])
    m["st_ssd"] = np.ascontiguousarray(inp["state_ssd"][:, b0:b0 + SB])
    m["st_conv"] = np.ascontiguousarray(inp["state_ssd_conv"][:, b0:b0 + SB].reshape(2, SB * 3, 768))
    m["st_s5re"] = np.ascontiguousarray(inp["state_s5_re"][:, b0:b0 + SB].reshape(2, SB, 1024))
    m["st_s5im"] = np.ascontiguousarray(inp["state_s5_im"][:, b0:b0 + SB].reshape(2, SB, 1024))
    return m


def _assemble(res):
    f = np.float32
    yp = np.stack([r["yp"] for r in res], 0).astype(f)
    ys = np.concatenate([r["ys"].reshape(SB, SL, D) for r in res], 0).astype(f)

    def pst(k, shp):
        return np.stack([r[k].reshape(shp) for r in res], 1).astype(f)

    def sst(k, shp):
        return np.concatenate([r[k].reshape(shp) for r in res], 1).astype(f)

    return (yp, ys,
            pst("p_ret", (2, 4, 64, 64)), pst("p_hg", (2, 4, 64, 64)), pst("p_ssd", (2, 4, 128, 64)),
            pst("p_conv", (2, 3, 768)), pst("p_s5re", (2, 16, 64)), pst("p_s5im", (2, 16, 64)),
            sst("s_ret", (2, SB, 4, 64, 64)), sst("s_hg", (2, SB, 4, 64, 64)), sst("s_ssd", (2, SB, 4, 128, 64)),
            sst("s_conv", (2, SB, 3, 768)), sst("s_s5re", (2, SB, 16, 64)), sst("s_s5im", (2, SB, 16, 64)))


def kernel(**inputs):
    global _CONSTS
    if _CONSTS is None:
        _CONSTS = _const_tables()
    inp = {k: np.asarray(v, dtype=np.float32) for k, v in inputs.items()}
    shared = _host_params(inp)
    nc, _ = build_program()
    in_maps = [_core_inputs(inp, shared, _CONSTS, c) for c in range(NCORE)]
    res = run_bass_kernel_spmd(nc, in_maps, core_ids=list(range(NCORE)))
    return _assemble(res.results)
```

```python
import contextlib
import math
import numpy as np
import concourse.bass as bass
import concourse.mybir as mybir
from concourse.bass_utils import run_bass_kernel_spmd

F32 = mybir.dt.float32
BF16 = mybir.dt.bfloat16
I32 = mybir.dt.int32
AF = mybir.ActivationFunctionType
ALU = mybir.AluOpType
AX = mybir.AxisListType

ENGS = ("pe", "act", "dve", "pool", "sp")
MAXV = 30000

D = 1024
NCORE = 8
SEQ = 2048
NPT = 16
NT = 17
SB = 16
SL = 8
PAST = 16384
INC = 3332
DFF = 4096
EPS = 1e-6
SGS = [list(range(0, 4)), list(range(4, 8)), list(range(8, 12)), list(range(12, 16)), [16]]
NEG = -30000.0
FCH = 512
NFC = DFF // FCH


class Op:
    __slots__ = ("eng", "fn", "reads", "writes", "dma", "dkey", "deps", "sig", "idx")

    def __init__(self, eng, fn, reads, writes, dma, dkey):
        self.eng, self.fn, self.reads, self.writes = eng, fn, tuple(reads), tuple(writes)
        self.dma, self.dkey = dma, dkey
        self.deps = []
        self.sig = None
        self.idx = -1


class Sched:
    def __init__(self, nc):
        self.nc = nc
        self.ops = []
        self.lastw = {}
        self.readers = {}
        self.alias = {}

    def _expand(self, keys):
        out = []
        for k in keys:
            for kk in self.alias.get(k, (k,)) if isinstance(k, str) else (k,):
                for k3 in (self.alias.get(kk, (kk,)) if isinstance(kk, str) else (kk,)):
                    if k3 not in out:
                        out.append(k3)
        return out

    def op(self, eng, fn, reads=(), writes=(), dma=False, dkey=None):
        reads, writes = self._expand(reads), self._expand(writes)
        o = Op(eng, fn, reads, writes, dma, dkey)
        o.idx = len(self.ops)
        deps = set()
        for r in o.reads:
            w = self.lastw.get(r)
            if w is not None:
                deps.add(w)
        for r in o.writes:
            w = self.lastw.get(r)
            if w is not None:
                deps.add(w)
            for rd in self.readers.get(r, ()):
                deps.add(rd)
        deps.discard(o.idx)
        o.deps = sorted(deps)
        for r in o.writes:
            self.lastw[r] = o.idx
            self.readers[r] = []
        for r in o.reads:
            if r not in o.writes:
                self.readers.setdefault(r, []).append(o.idx)
        self.ops.append(o)
        return o

    def pe(self, fn, reads=(), writes=()):
        return self.op("pe", fn, reads, writes)

    def act(self, fn, reads=(), writes=()):
        return self.op("act", fn, reads, writes)

    def dve(self, fn, reads=(), writes=()):
        return self.op("dve", fn, reads, writes)

    def pool(self, fn, reads=(), writes=()):
        return self.op("pool", fn, reads, writes)

    def dma(self, eng, fn, reads=(), writes=(), dkey=None):
        if dkey is None:
            dkey = writes[0] if len(writes) else ("st", reads[0])
        return self.op(eng, fn, reads, writes, dma=True, dkey=dkey)

    def emit(self, final_eng="sp"):
        nc = self.nc
        ops = self.ops
        needed = [False] * len(ops)
        for o in ops:
            for d in o.deps:
                p = ops[d]
                if p.eng == "pe" and o.eng == "pe" and not p.dma and not o.dma:
                    continue
                needed[d] = True
        for o in ops:
            if o.dma:
                needed[o.idx] = True
        cnt = {e: 0 for e in ENGS}
        dcnt = {}
        for o in ops:
            if not needed[o.idx]:
                continue
            if o.dma:
                dcnt[o.dkey] = dcnt.get(o.dkey, 0) + 16
                o.sig = ("d", o.dkey, dcnt[o.dkey])
            else:
                cnt[o.eng] += 1
                o.sig = ("e", o.eng, cnt[o.eng])
        stack = contextlib.ExitStack()
        esems = {}
        for e in ENGS:
            n = (cnt[e] + MAXV - 1) // MAXV
            esems[e] = [stack.enter_context(nc.semaphore(f"s_{e}{i}")) for i in range(max(n, 1))]
        dsems = {}
        for i, k in enumerate(dcnt):
            dsems[k] = stack.enter_context(nc.semaphore(f"d{i}"))
        self.n_sems = sum(len(v) for v in esems.values()) + len(dsems)

        def semval(sig):
            if sig[0] == "d":
                return dsems[sig[1]], sig[2]
            k = sig[2] - 1
            return esems[sig[1]][k // MAXV], (k % MAXV) + 1

        by_eng = {e: [o for o in ops if o.eng == e] for e in ENGS}
        final = {}
        for o in ops:
            if o.dma:
                final[o.dkey] = o.sig[2]

        def run(e, eh):
            waited = {}
            for o in by_eng[e]:
                need = {}
                for d in o.deps:
                    p = ops[d]
                    if p.sig is None:
                        continue
                    if p.eng == "pe" and e == "pe" and not p.dma and not o.dma:
                        continue
                    s, v = semval(p.sig)
                    key = (p.sig[0], p.sig[1], (p.sig[2] - 1) // MAXV if p.sig[0] == "e" else 0)
                    if key not in need or need[key][1] < v:
                        need[key] = (s, v)
                for key, (s, v) in need.items():
                    if waited.get(key, 0) >= v:
                        continue
                    waited[key] = v
                    eh.wait_ge(s, v)
                ins = o.fn(eh)
                if o.sig is not None:
                    s, v = semval(o.sig)
                    ins.then_inc(s, 16 if o.dma else 1)
            if e == final_eng:
                for k, v in final.items():
                    eh.wait_ge(dsems[k], v)

        with stack:
            with nc.Block() as block:
                @block.tensor
                def _(eh):
                    run("pe", eh)

                @block.scalar
                def _(eh):
                    run("act", eh)

                @block.vector
                def _(eh):
                    run("dve", eh)

                @block.gpsimd
                def _(eh):
                    run("pool", eh)

                @block.sync
                def _(eh):
                    run("sp", eh)


def _const_tables():
    c = {}
    c["ident"] = np.eye(128, dtype=np.float32)
    s = np.arange(128)[:, None]
    t = np.arange(128)[None, :]
    cp = (s <= t).astype(np.float32)
    cs = ((s <= t) & (s // SL == t // SL)).astype(np.float32)
    c["causal"] = np.stack([cp, cs], 1)
    c["negmask"] = np.stack([(1 - cp) * NEG, (1 - cs) * NEG], 1).astype(np.float32)
    ssame = (s // SL == t // SL).astype(np.float32)
    c["sameseq"] = np.stack([np.ones((128, 128), np.float32), ssame], 1)
    gam = 1.0 - np.exp2(-5.0 - np.arange(4, dtype=np.float64))
    lg = np.log(gam)
    kscale = 64.0 ** -0.5
    retd = np.zeros((128, 2, 4, 128), np.float64)
    for h in range(4):
        retd[:, 0, h, :] = np.exp(lg[h] * np.clip(t - s, 0, None)) * cp * kscale
        retd[:, 1, h, :] = np.exp(lg[h] * np.clip(t - s, 0, None)) * cs * kscale
    c["retd"] = retd.astype(np.float32)
    qs = np.zeros((128, 2, 2, 128), np.float64)
    ks = np.zeros((128, 2, 4), np.float64)
    ra = np.zeros((128, 2, 2), np.float64)
    tt = np.arange(128)
    for h in range(4):
        blk, hl = h // 2, h % 2
        qs[64 * hl:64 * hl + 64, 0, blk, :] = np.exp(lg[h] * (tt + 1))[None, :]
        qs[64 * hl:64 * hl + 64, 1, blk, :] = np.exp(lg[h] * (tt % SL + 1))[None, :]
        ks[:, 0, h] = np.exp(lg[h] * (127 - tt)) * kscale
        ks[:, 1, h] = np.exp(lg[h] * (SL - 1 - tt % SL)) * kscale
        ra[64 * hl:64 * hl + 64, 0, blk] = np.exp(lg[h] * 128)
        ra[64 * hl:64 * hl + 64, 1, blk] = np.exp(lg[h] * SL)
    c["ret_qs"] = qs.astype(np.float32)
    c["ret_ks"] = ks.astype(np.float32)
    c["ret_a"] = ra.astype(np.float32)
    ang = (1.0 / (10000.0 ** np.linspace(0.0, 1.0, 32, dtype=np.float32))).astype(np.float32)
    pos = np.zeros((128, NT), np.float32)
    for i in range(NPT):
        pos[:, i] = i * 128 + tt
    pos[:, NPT] = PAST + tt % SL
    theta = (pos[:, :, None] * ang[None, None, :]).astype(np.float32)
    c["cos"] = np.cos(theta).astype(np.float32)
    c["sin"] = np.sin(theta).astype(np.float32)
    b = np.arange(SB)
    c["colmask"] = np.broadcast_to((tt[None, :] // SL == b[:, None]).astype(np.float32)[None], (128, SB, 128)).copy()
    c["rowmask"] = (tt[:, None] // SL == b[None, :]).astype(np.float32)
    rst = np.ones((128, 2, 128), np.float32)
    rst[:, 1, :] = (tt % SL != 0).astype(np.float32)[None, :]
    rst[:, 0, 0] = 0.0
    c["rst"] = rst
    tau = np.zeros((128, 2, 128), np.float32)
    tau[:, 0, :] = tt[None, :]
    tau[:, 1, :] = (tt % SL)[None, :]
    c["tau"] = tau
    hm = np.zeros((128, 2), np.float32)
    hm[:64, 0] = 1.0
    hm[64:, 1] = 1.0
    c["hmask"] = hm
    return c


CONST_SHAPES = {
    "ident": [128, 128], "causal": [128, 2, 128], "negmask": [128, 2, 128], "sameseq": [128, 2, 128],
    "retd": [128, 2, 4, 128], "ret_qs": [128, 2, 2, 128], "ret_ks": [128, 2, 4], "ret_a": [128, 2, 2],
    "cos": [128, NT, 32], "sin": [128, NT, 32], "colmask": [128, SB, 128], "rowmask": [128, SB],
    "rst": [128, 2, 128], "tau": [128, 2, 128], "hmask": [128, 2],
}

IN_SHAPES = {
    "xp": [SEQ, D], "xs": [128, D],
    "st_ret": [2, SB, 4, 64, 64], "st_hg": [2, SB, 4, 64, 64], "st_ssd": [2, SB, 4, 128, 64],
    "st_conv": [2, SB * 3, 768], "st_s5re": [2, SB, 1024], "st_s5im": [2, SB, 1024],
    "nm_fm": [2, 128, 8], "nmlp_fm": [2, 128, 8], "norm_final": [D],
    "w_in": [2, D, INC], "w_out": [2, D, D], "w_up": [2, D, DFF], "w_down": [2, DFF, D],
    "hg_lg": [128, 2, 2], "conv_w": [2, 128, 6, 4], "conv_b": [2, 128, 6],
    "dt_bias": [2, 4], "a_log": [2, 4], "ssd_d": [2, 4], "ssd_norm": [2, 256],
    "s5_are": [2, 128, 8], "s5_aim": [2, 128, 8], "s5_ldt": [2, 128, 8],
    "s5_bre": [2, 128, 8, 128], "s5_bim": [2, 128, 8, 128],
    "s5_cre": [2, 128, 8, 32], "s5_cim": [2, 128, 8, 32], "s5_dfm": [2, 128, 2],
    "glu_w": [2, 256, 256], "glu_b": [2, 256],
}
OUT_SHAPES = {
    "yp": [SEQ, D], "ys": [128, D],
    "p_ret": [2, 4, 64, 64], "p_hg": [2, 4, 64, 64], "p_ssd": [2, 4, 128, 64], "p_conv": [2, 3, 768],
    "p_s5re": [2, 8, 128], "p_s5im": [2, 8, 128],
    "s_ret": [2, SB, 4, 64, 64], "s_hg": [2, SB, 4, 64, 64], "s_ssd": [2, SB, 4, 128, 64],
    "s_conv": [2, SB * 3, 768], "s_s5re": [2, SB, 1024], "s_s5im": [2, SB, 1024],
}


def build_program(n_layers=2, sgs=None, mixers=("ret", "hg", "ssd", "s5"), do_mlp=True):
    sgs = SGS if sgs is None else sgs
    LASTP = max([i for g in sgs for i in g if i < NPT] + [-1])
    nc = bass.Bass("TRN2", target_bir_lowering=False)
    S = Sched(nc)
    st = contextlib.ExitStack()
    din = {k: nc.dram_tensor(k, v, F32, kind="ExternalInput").ap() for k, v in IN_SHAPES.items()}
    dcon = {k: nc.dram_tensor("c_" + k, v, F32, kind="ExternalInput").ap() for k, v in CONST_SHAPES.items()}
    dout = {k: nc.dram_tensor(k, v, F32, kind="ExternalOutput").ap() for k, v in OUT_SHAPES.items()}

    def sb(name, shape, dt=F32):
        return st.enter_context(nc.sbuf_tensor(name, shape, dt))

    with st:
        PS = st.enter_context(nc.psum_tensor("PS", [128, 8, 512], F32))

        def bk(k):
            return "b%d" % k

        NSG = max(len(g) for g in sgs)
        X = sb("X", [128, NSG, D])
        H2T = sb("H2T", [128, NSG, 8, 128], BF16)
        WIN = sb("WIN", [128, 8, INC], BF16)
        WOUT = sb("WOUT", [128, 8, D], BF16)
        GB = sb("GB", [128, 8, 1024])
        GBF = GB[:].rearrange("p a b -> p (a b)")

        def galias(name, k, c0, c1, shape):
            ap = GB[:, k, c0 * 128:c1 * 128]
            if len(shape) == 3:
                ap = ap.rearrange("p (a b) -> p a b", b=shape[2])
            elif len(shape) == 4:
                ap = ap.rearrange("p (a b c) -> p a b c", b=shape[2], c=shape[3])
            S.alias[name] = [("G", k, q) for q in range(c0 // 2, (c1 + 1) // 2)]
            return ap

        WUP, WDN = [], []
        for r in range(2):
            WUP.append(GB[:, 4 * r:4 * r + 2, :].bitcast(BF16).rearrange("p a (k f) -> p (a k) f", f=FCH))
            WDN.append(GB[:, 4 * r + 2:4 * r + 4, :].bitcast(BF16).rearrange("p a (f n) -> p (a f) n", n=D))
            S.alias["WUP%d" % r] = [("G", 4 * r + a, q) for a in range(2) for q in range(4)]
            S.alias["WDN%d" % r] = [("G", 4 * r + 2 + a, q) for a in range(2) for q in range(4)]
        C = {k: sb("C_" + k, v, BF16 if k == "colmask" else F32) for k, v in CONST_SHAPES.items()}
        NM = sb("NM", [128, 2, 8]); NMLP = sb("NMLP", [128, 2, 8])
        HGLG = sb("HGLG", [128, 2, 2]); LOW = sb("LOW", [128, 2, 2]); OML = sb("OML", [128, 2, 2])
        CW = sb("CW", [128, 2, 6, 4]); CB = sb("CB", [128, 2, 6])
        DTB = sb("DTB", [128, 2, 4]); NEGA = sb("NEGA", [128, 2, 4]); SSDD = sb("SSDD", [128, 2, 4])
        SNORM = sb("SNORM", [128, 256]); GLUB = sb("GLUB", [128, 256]); GLUW = sb("GLUW", [128, 2, 256])
        S5P = sb("S5P", [128, 2, 3, 8])
        S5C = sb("S5C", [128, 8, 32]); S5CI = sb("S5CI", [128, 8, 32])
        S5D = sb("S5D", [128, 2, 2]); DIAGD = sb("DIAGD", [128, 2, 128])
        COSR = sb("COSR", [128, 8, 128]); SINR = sb("SINR", [128, 8, 128])
        ABAR = sb("ABAR", [128, 2, 8]); COEF = sb("COEF", [128, 2, 8]); RR = sb("RR", [128, 8]); PHI = sb("PHI", [128, 8])
        BCF = sb("BCF", [128, 2, 8, 128])
        SRET = sb("SRET", [128, 2, 2, 128]); SHG = sb("SHG", [128, 2, 2, 128]); SSSD = sb("SSSD", [128, 2, 4, 64])
        ZS5 = sb("ZS5", [128, 2, 2, 8]); XPH = sb("XPH", [128, 2, 6, 3])
        HTS = [sb("HT0", [128, 8, 128], BF16), sb("HT1", [128, 8, 128], BF16)]
        XN = sb("XN", [128, D])
        STA = sb("STA", [128, 16]); STB = sb("STB", [128, 16]); STC = sb("STC", [128, 16])
        TM = sb("TM", [128, 1796]); FM = sb("FM", [128, 12, 128])
        MIX = sb("MIX", [128, D]); MIXT = sb("MIXT", [128, 8, 128], BF16)
        JUNK = MIXT[:].rearrange("p a b -> p (a b)")
        S.alias["JUNK"] = ["MIXT"]
        SGR = sb("SGR", [128, 256]); SGH = sb("SGH", [128, 256]); SSZ = sb("SSZ", [128, 256])
        HREF = sb("HREF", [128, 2, 4]); EBL = sb("EBL", [128, 2, SB])
        DEL = sb("DEL", [128, 4]); LOGA = sb("LOGA", [128, 4]); LND = sb("LND", [128, 4]); BCOL = sb("BCOL", [128, 4])
        BLAST = sb("BLAST", [128, 4]); KSC = sb("KSC", [128, 4])
        RELU = [TM[:, 0:256], TM[:, 256:512]]
        S.alias["TM"] = [("TM", q) for q in range(4)]
        S.alias["RELU0"] = [("TM", 0)]; S.alias["RELU1"] = [("TM", 1)]
        S.alias["ACTT0"] = [("TM", 2)]; S.alias["ACTT1"] = [("TM", 3)]
        ACTT2 = [TM[:, 512:1024].bitcast(BF16).rearrange("p (f n) -> p f n", n=256),
                 TM[:, 1024:1536].bitcast(BF16).rearrange("p (f n) -> p f n", n=256)]
        QKT = galias("QKT", 0, 0, 4, [128, 4, 128]); PT = galias("PT", 0, 4, 8, [128, 4, 128])
        QKR = galias("QKR", 1, 0, 4, [128, 512]); RT1 = galias("RT1", 1, 4, 6, [128, 8, 32]); RT2 = galias("RT2", 1, 6, 8, [128, 8, 32])
        QD = galias("QD", 3, 0, 2, [128, 2, 128]); KD = galias("KD", 3, 2, 4, [128, 256]); QM = galias("QM", 3, 4, 8, [128, 4, 128])
        def galias16(name, k, c0, c1, shape):
            ap = GB[:, k, c0 * 128:c1 * 128].bitcast(BF16)
            if len(shape) == 3:
                ap = ap.rearrange("p (a b) -> p a b", b=shape[2])
            elif len(shape) == 4:
                ap = ap.rearrange("p (a b c) -> p a b c", b=shape[2], c=shape[3])
            S.alias[name] = [("G", k, q) for q in range(c0 // 2, (c1 + 1) // 2)]
            return ap

        QH = galias16("QH", 3, 4, 6, [128, 2, 2, 128])
        QKTB = galias16("QKTB", 0, 0, 2, [128, 4, 128])
        HKVB = galias16("HKVB", 0, 0, 4, [128, 4, 2, 128])
        XRB = galias16("XRB", 2, 0, 4, [128, 8, 128]); XIB = galias16("XIB", 3, 0, 4, [128, 8, 128])
        VB = sb("VB", [128, 512], BF16)
        PTB = galias16("PTB", 0, 4, 6, [128, 4, 128]); PT2B = galias16("PT2B", 5, 0, 2, [128, 4, 128]); XSB = galias16("XSB", 5, 2, 3, [128, 256])
        S5CB = sb("S5CB", [128, 8, 32], BF16); S5CIB = sb("S5CIB", [128, 8, 32], BF16)
        KDM = galias("KDM", 7, 0, 8, [128, 4, 256]); SST2 = galias("SST2", 7, 0, 8, [128, 4, 4, 64])
        CVT = galias("CVT", 7, 0, 6, [128, 768])
        SST = galias("SST", 5, 0, 8, [128, 4, 2, 128])
        HE = galias("HE", 1, 0, 2, [128, 2, 128]); HSIG = galias("HSIG", 1, 2, 4, [128, 2, 128])
        HKEY = galias("HKEY", 1, 4, 6, [128, 2, 128]); HB = galias("HB", 1, 6, 8, [128, 2, 128])
        HQ = galias("HQ", 2, 0, 2, [128, 2, 128]); HD = galias("HD", 2, 2, 4, [128, 2, 128])
        HEX = galias("HEX", 2, 4, 6, [128, 2, 128]); HQA = galias("HQA", 2, 6, 8, [128, 2, 128])
        HQD = galias("HQD", 3, 0, 2, [128, 2, 128]); HKT = galias("HKT", 3, 2, 4, [128, 256])
        LB = galias("LB", 0, 0, 4, [128, 4, 128]); D1 = galias("D1", 0, 4, 8, [128, 4, 128])
        DEC = galias("DEC", 1, 0, 4, [128, 4, 128]); EBT = galias("EBT", 1, 4, 8, [128, 4, 128])
        CMD = galias("CMD", 2, 0, 4, [128, 4, 128]); BD = galias("BD", 2, 4, 8, [128, 4, 128])
        BDM = galias("BDM", 3, 0, 4, [128, 4, 128])
        PT2 = galias("PT2", 5, 0, 4, [128, 4, 128]); XS = galias("XS", 5, 4, 6, [128, 256]); BMT = galias("BMT", 5, 6, 8, [128, 2, 128])
        CACC = galias("CACC", 4, 0, 6, [128, 6, 128]); XC = galias("XC", 6, 0, 6, [128, 6, 128])
        CTMPP = galias("CTMPP", 7, 0, 6, [128, 6, 128]); CTMPS = galias("CTMPS", 5, 0, 6, [128, 6, 128])
        XPW = GB[:, 5, 0:786].rearrange("p (c n) -> p c n", n=131)
        S.alias["XPW"] = [("G", 5, q) for q in range(4)]
        XPS = GBF[:, 0:1056].rearrange("p (c b j) -> p c b j", b=SB, j=11)
        S.alias["XPS"] = [("G", 0, q) for q in range(4)] + [("G", 1, 0)]
        HKV = galias("HKV", 4, 0, 8, [128, 4, 2, 128]); HKV4 = galias("HKV4", 7, 6, 8, [128, 2, 128])
        OB = galias("OB", 4, 0, 2, [128, 256]); OB2 = galias("OB2", 4, 2, 4, [128, 256])
        PR = galias("PR", 0, 0, 8, [128, 8, 128]); PI = galias("PI", 1, 0, 8, [128, 8, 128])
        XR = galias("XR", 2, 0, 8, [128, 8, 128]); XI = galias("XI", 3, 0, 8, [128, 8, 128])
        T1 = galias("T1", 4, 0, 8, [128, 8, 128]); T2 = galias("T2", 5, 0, 8, [128, 8, 128])
        WR = galias("WR", 6, 0, 8, [128, 8, 128]); WI = galias("WI", 7, 0, 8, [128, 8, 128])
        TI = GB[:, 7, :].bitcast(I32).rearrange("p (c n) -> p c n", n=128)
        S.alias["TI"] = [("G", 7, q) for q in range(4)]
        YG = galias("YG", 4, 0, 2, [128, 256]); YU = galias("YU", 4, 2, 4, [128, 256])
        YE = galias("YE", 4, 4, 6, [128, 256]); YGT = galias("YGT", 4, 6, 8, [128, 2, 128])
        ZT = galias("ZT", 5, 0, 8, [128, 1024])
        INJ = galias("INJ", 0, 0, 2, [128, 2, 8, SB]); INJ2 = galias("INJ2", 0, 2, 4, [128, 2, 8, SB])
        ZL = galias("ZL", 6, 0, 2, [128, 2, 8, SB]); Z0 = galias("Z0", 1, 2, 4, [128, 2, 8, SB])
        YOUT = galias("YOUT", 0, 0, 8, [128, 1024]); WF = galias("WF", 1, 0, 8, [128, 1024])

        def dma_in(eng, dst, src, key):
            S.dma(eng, lambda e: e.dma_start(out=dst, in_=src), writes=[key])

        def dma_out(eng, dst, src, key):
            S.dma(eng, lambda e: e.dma_start(out=dst, in_=src), reads=[key], dkey=("st", key))

        def MM(out, lhsT, rhs, start=True, stop=True):
            return lambda e: e.matmul(out, lhsT=lhsT, rhs=rhs, start=start, stop=stop, skip_group_check=True)

        def TR(out, in_, n=128):
            return lambda e: e.transpose(out, in_, C["ident"][0:n, 0:n])

        def ACT(out, in_, func, **kw):
            return lambda e: e.activation(out=out, in_=in_, func=func, **kw)

        def TT(out, in0, in1, op):
            return lambda e: e.tensor_tensor(out=out, in0=in0, in1=in1, op=op)

        def TS(out, in0, s1, s2, op0, op1=None):
            if op1 is None:
                return lambda e: e.tensor_scalar(out=out, in0=in0, scalar1=s1, scalar2=None, op0=op0)
            return lambda e: e.tensor_scalar(out=out, in0=in0, scalar1=s1, scalar2=s2, op0=op0, op1=op1)

        def STT(out, in0, scalar, in1, op0, op1):
            return lambda e: e.scalar_tensor_tensor(out=out, in0=in0, scalar=scalar, in1=in1, op0=op0, op1=op1)

        def CP(out, in_):
            return lambda e: e.tensor_copy(out=out, in_=in_)

        def bc(ap, shape):
            return ap.to_broadcast(shape)

        MUL, ADD, SUB = ALU.mult, ALU.add, ALU.subtract

        for k in CONST_SHAPES:
            dma_in("pool" if k == "colmask" else "sp", C[k][:], dcon[k], "C_" + k)
        dma_in("sp", NM[:], din["nm_fm"].rearrange("l p c -> p l c"), "NM")
        dma_in("sp", NMLP[:], din["nmlp_fm"].rearrange("l p c -> p l c"), "NMLP")
        dma_in("sp", HGLG[:], din["hg_lg"], "HGLG")
        dma_in("sp", CW[:], din["conv_w"].rearrange("l p b j -> p l b j"), "CW")
        dma_in("sp", CB[:], din["conv_b"].rearrange("l p b -> p l b"), "CB")
        for l in range(2):
            dma_in("sp", DTB[:, l, :], din["dt_bias"][l].partition_broadcast(128), "DTB")
            dma_in("sp", NEGA[:, l, :], din["a_log"][l].partition_broadcast(128), "NEGA")
            dma_in("sp", SSDD[:, l, :], din["ssd_d"][l].partition_broadcast(128), "SSDD")
            dma_in("sp", S5P[:, l, 0, :], din["s5_are"][l], "S5P")
            dma_in("sp", S5P[:, l, 1, :], din["s5_aim"][l], "S5P")
            dma_in("sp", S5P[:, l, 2, :], din["s5_ldt"][l], "S5P")
            dma_in("sp", S5D[:, l, :], din["s5_dfm"][l], "S5D")
        S.act(ACT(NEGA[:], NEGA[:], AF.Exp), reads=["NEGA"], writes=["NEGA"])
        S.dve(TS(NEGA[:], NEGA[:], -1.0, None, MUL), reads=["NEGA"], writes=["NEGA"])
        S.act(ACT(HGLG[:], HGLG[:], AF.Exp), reads=["HGLG"], writes=["HGLG"])
        S.dve(TT(LOW[:, 1, :], HGLG[:, :, 0], HGLG[:, :, 1], ADD), reads=["HGLG"], writes=["LOW"])
        S.dve(lambda e: e.reciprocal(out=LOW[:, 1, :], in_=LOW[:, 1, :]), reads=["LOW"], writes=["LOW"])
        S.dve(TT(LOW[:, 1, :], LOW[:, 1, :], HGLG[:, :, 1], MUL), reads=["LOW", "HGLG"], writes=["LOW"])
        S.pool(lambda e: e.memset(LOW[:, 0, :], 0.0), writes=["LOW"])
        S.dve(TS(OML[:], LOW[:], -1.0, 1.0, MUL, ADD), reads=["LOW"], writes=["OML"])
        S.pool(lambda e: e.memset(HREF[:], 0.0), writes=["HREF"])
        for t_, k_ in ((SRET, "SRET"), (SHG, "SHG"), (SSSD, "SSSD"), (ZS5, "ZS5"), (XPH, "XPH"), (MIX, "MIX")):
            S.pool(lambda e, t_=t_: e.memset(t_[:], 0.0), writes=[k_])

        def layer_params(l):
            dma_in("sp", SNORM[:], din["ssd_norm"][l].partition_broadcast(128), "SNORM")
            dma_in("sp", GLUB[:], din["glu_b"][l].partition_broadcast(128), "GLUB")
            dma_in("sp", GLUW[:], din["glu_w"][l].rearrange("(c p) n -> p c n", p=128), "GLUW")
            dma_in("sp", S5C[:], din["s5_cre"][l], "S5C")
            dma_in("sp", S5CI[:], din["s5_cim"][l], "S5CI")
            S.dve(TS(S5CI[:], S5CI[:], -1.0, None, MUL), reads=["S5CI"], writes=["S5CI"])
            S.dve(CP(S5CB[:], S5C[:]), reads=["S5C"], writes=["S5CB"])
            S.dve(CP(S5CIB[:], S5CI[:]), reads=["S5CI"], writes=["S5CIB"])
            for ub in range(2):
                S.dve(TS(DIAGD[:, ub, :], C["ident"][:], S5D[:, l, ub:ub + 1], None, MUL),
                      reads=["C_ident", "S5D"], writes=["DIAGD"])

        def s5_setup(l):
            S.act(ACT(STA[:, 0:8], S5P[:, l, 2, :], AF.Exp), reads=["S5P"], writes=["STA", "STAN"])
            S.dve(TT(RR[:], S5P[:, l, 0, :], STA[:, 0:8], MUL), reads=["S5P", "STA", "STAN"], writes=["RR"])
            S.dve(TT(PHI[:], S5P[:, l, 1, :], STA[:, 0:8], MUL), reads=["S5P", "STA", "STAN"], writes=["PHI"])
            S.act(ACT(RR[:], RR[:], AF.Exp), reads=["RR"], writes=["RR"])
            S.dve(TT(T1[:], bc(PHI[:].unsqueeze(2), [128, 8, 128]), bc(C["tau"][:, 0:1, :], [128, 8, 128]), MUL),
                  reads=["PHI", "C_tau"], writes=["T1"])

            def sincos(dst, dkey, shift):
                S.dve(TS(T2[:], T1[:], shift, 1.0 / (2 * math.pi), ADD, MUL), reads=["T1"], writes=["T2"])
                S.dve(CP(TI[:], T2[:]), reads=["T2"], writes=["TI"])
                S.dve(CP(T2[:], TI[:]), reads=["TI"], writes=["T2"])
                S.dve(STT(T2[:], T2[:], -2 * math.pi, T1[:], MUL, ADD), reads=["T2", "T1"], writes=["T2"])
                S.dve(TS(T2[:], T2[:], shift, 3.1415925, ADD, ALU.min), reads=["T2"], writes=["T2"])
                S.dve(TS(T2[:], T2[:], -3.1415925, None, ALU.max), reads=["T2"], writes=["T2"])
                S.act(ACT(dst[:], T2[:], AF.Sin), reads=["T2"], writes=[dkey])

            sincos(SINR, "SINR", 0.0)
            sincos(COSR, "COSR", math.pi / 2)
            S.dve(TT(ABAR[:, 0, :], RR[:], COSR[:, :, 1], MUL), reads=["RR", "COSR"], writes=["ABAR"])
            S.dve(TT(ABAR[:, 1, :], RR[:], SINR[:, :, 1], MUL), reads=["RR", "SINR"], writes=["ABAR"])
            xx, yy, uu, vv = STA[:, 0:8], ABAR[:, 1, :], S5P[:, l, 0, :], S5P[:, l, 1, :]
            S.dve(TS(STA[:, 0:8], ABAR[:, 0, :], -1.0, None, ADD), reads=["ABAR"], writes=["STA", "STAN"])
            S.dve(TT(STB[:, 0:8], uu, uu, MUL), reads=["S5P"], writes=["STB"])
            S.dve(TT(STB[:, 8:16], vv, vv, MUL), reads=["S5P"], writes=["STB"])
            S.dve(TT(STB[:, 0:8], STB[:, 0:8], STB[:, 8:16], ADD), reads=["STB"], writes=["STB"])
            S.dve(lambda e: e.reciprocal(out=STB[:, 0:8], in_=STB[:, 0:8]), reads=["STB"], writes=["STB"])
            S.dve(TT(STC[:, 0:8], xx, uu, MUL), reads=["STA", "STAN", "S5P"], writes=["STC"])
            S.dve(TT(STC[:, 8:16], yy, vv, MUL), reads=["ABAR", "S5P"], writes=["STC"])
            S.dve(TT(STC[:, 0:8], STC[:, 0:8], STC[:, 8:16], ADD), reads=["STC"], writes=["STC"])
            S.dve(TT(COEF[:, 0, :], STC[:, 0:8], STB[:, 0:8], MUL), reads=["STC", "STB"], writes=["COEF"])
            S.dve(TT(STC[:, 0:8], yy, uu, MUL), reads=["ABAR", "S5P"], writes=["STC"])
            S.dve(TT(STC[:, 8:16], xx, vv, MUL), reads=["STA", "STAN", "S5P"], writes=["STC"])
            S.dve(TT(STC[:, 0:8], STC[:, 0:8], STC[:, 8:16], SUB), reads=["STC"], writes=["STC"])
            S.dve(TT(COEF[:, 1, :], STC[:, 0:8], STB[:, 0:8], MUL), reads=["STC", "STB"], writes=["COEF"])
            dma_in("sp", PR[:], din["s5_bre"][l], "PR")
            dma_in("sp", PI[:], din["s5_bim"][l], "PI")
            for ri, dst, dk in ((0, XR, "XR"), (1, XI, "XI")):
                for half in range(2):
                    bnk = 4 + 2 * ri + half
                    for cc in range(4):
                        c = half * 4 + cc
                        S.dve(TS(WR[:, c, :], C["ident"][:], COEF[:, ri, c:c + 1], None, MUL),
                              reads=["C_ident", "COEF"], writes=["WR"])
                        S.pe(MM(PS[:, bnk, cc * 128:(cc + 1) * 128], C["sameseq"][:, 0, :], WR[:, c, :]),
                             reads=["C_sameseq", "WR"], writes=[bk(bnk)])
                    S.act(ACT(dst[:, half * 4:half * 4 + 4, :], PS[:, bnk, :].rearrange("p (c n) -> p c n", c=4), AF.Copy),
                          reads=[bk(bnk)], writes=[dk])
            S.dve(TT(T1[:], PR[:], XR[:], MUL), reads=["PR", "XR"], writes=["T1"])
            S.dve(TT(T2[:], PI[:], XI[:], MUL), reads=["PI", "XI"], writes=["T2"])
            S.pool(TT(BCF[:, 0], T1[:], T2[:], SUB), reads=["T1", "T2"], writes=["BCF"])
            S.dve(TT(T1[:], PR[:], XI[:], MUL), reads=["PR", "XI"], writes=["T1"])
            S.dve(TT(T2[:], PI[:], XR[:], MUL), reads=["PI", "XR"], writes=["T2"])
            S.pool(TT(BCF[:, 1], T1[:], T2[:], ADD), reads=["T1", "T2"], writes=["BCF"])

        def s5_sample_tables():
            for tb, k in ((COSR, "COSR"), (SINR, "SINR")):
                v = tb[:, :, 8:128].rearrange("p c (b e) -> p c b e", e=8)
                S.dve(CP(v, bc(tb[:, :, 0:8].unsqueeze(2), [128, 8, 15, 8])), reads=[k], writes=[k])

        def norm_T(xap, xkey, wfm, wkey, dst, dkey):
            S.act(ACT(JUNK[:], xap, AF.Square, accum_out=STA[:, 0:1]), reads=[xkey], writes=["JUNK", "STAN"])
            S.act(ACT(STA[:, 1:2], STA[:, 0:1], AF.Ln, scale=1.0 / D, bias=EPS), reads=["STAN"], writes=["STAN"])
            S.act(ACT(STA[:, 2:3], STA[:, 1:2], AF.Exp, scale=-0.5), reads=["STAN"], writes=["STAN"])
            S.act(ACT(XN[:], xap, AF.Copy, scale=STA[:, 2:3]), reads=[xkey, "STAN"], writes=["XN"])
            for half in range(2):
                for cc in range(4):
                    c = half * 4 + cc
                    S.pe(TR(PS[:, half, cc * 128:(cc + 1) * 128], XN[:, c * 128:(c + 1) * 128]),
                         reads=["XN", "C_ident"], writes=[bk(half)])
                for cc in range(4):
                    c = half * 4 + cc
                    S.act(ACT(dst(half)[:, cc, :], PS[:, half, cc * 128:(cc + 1) * 128], AF.Copy, scale=wfm[:, c:c + 1]),
                          reads=[bk(half), wkey], writes=[dkey])
            return STA[:, 2:3]

        TMG = [(0, 512, 0), (512, 1024, 512), (1536, 2048, 1024), (2048, 2304, 1536), (3072, 3076, 1792)]
        FMB = [1024 + 128 * b for b in range(4)] + [2304 + 128 * b for b in range(6)] + [3076, 3076 + 128]

        def proj_tm(par):
            HT = HTS[par]
            hk = "HT%d" % par
            nb = 0
            for (c0, c1, d0) in TMG:
                bnk = 2 + nb % 2
                nb += 1
                w = c1 - c0
                for k in range(8):
                    S.pe(MM(PS[:, bnk, 0:w], HT[:, k, :], WIN[:, k, c0:c1], start=(k == 0), stop=(k == 7)),
                         reads=[hk, "WIN"], writes=[bk(bnk)])
                S.act(ACT(TM[:, d0:d0 + w], PS[:, bnk, 0:w], AF.Copy), reads=[bk(bnk)], writes=["TM"])
                if c0 == 512:
                    S.act(ACT(VB[:, 0:256], PS[:, bnk, 0:256], AF.Copy), reads=[bk(bnk)], writes=["VB"])
                if c0 == 1536:
                    S.act(ACT(VB[:, 256:512], PS[:, bnk, 0:256], AF.Copy), reads=[bk(bnk)], writes=["VB"])

        def proj_fm(par):
            HT = HTS[par]
            hk = "HT%d" % par
            nb = 1
            for g in range(3):
                bnk = 2 + nb % 2
                nb += 1
                for bb in range(4):
                    b = g * 4 + bb
                    for k in range(8):
                        S.pe(MM(PS[:, bnk, bb * 128:(bb + 1) * 128], WIN[:, k, FMB[b]:FMB[b] + 128], HT[:, k, :],
                                start=(k == 0), stop=(k == 7)), reads=[hk, "WIN"], writes=[bk(bnk)])
                fn = CP(FM[:, g * 4:g * 4 + 4, :], PS[:, bnk, :].rearrange("p (c n) -> p c n", c=4))
                S.dve(fn, reads=[bk(bnk)], writes=["FM"])

        def load_sst(name, l, g):
            src = din[name][l, 4 * g:4 * g + 4].rearrange("b (blk hl) k v -> hl k b blk v", hl=2)
            for hl in range(2):
                dma_in("sp", SST[64 * hl:64 * hl + 64, :, :, 64 * hl:64 * hl + 64], src[hl], "SST")

        def store_sst(name, l, g):
            dst = dout[name][l, 4 * g:4 * g + 4].rearrange("b (blk hl) k v -> hl k b blk v", hl=2)
            for hl in range(2):
                dma_out("sp", dst[hl], SST[64 * hl:64 * hl + 64, :, :, 64 * hl:64 * hl + 64], "SST")

        def retention(l, i, var):
            cos = bc(C["cos"][:, i, :].unsqueeze(1), [128, 8, 32])
            sin = bc(C["sin"][:, i, :].unsqueeze(1), [128, 8, 32])
            qk = TM[:, 0:512].rearrange("p (h i two) -> p h i two", h=8, two=2)
            qo = QKR[:].rearrange("p (h i two) -> p h i two", h=8, two=2)
            xe, xo = qk[:, :, :, 0], qk[:, :, :, 1]
            S.pool(TT(RT1[:], xe, cos, MUL), reads=["TM", "C_cos"], writes=["RT1"])
            S.pool(TT(RT2[:], xo, sin, MUL), reads=["TM", "C_sin"], writes=["RT2"])
            S.pool(TT(qo[:, :, :, 0], RT1[:], RT2[:], SUB), reads=["RT1", "RT2"], writes=["QKR"])
            S.pool(TT(RT1[:], xe, sin, MUL), reads=["TM", "C_sin"], writes=["RT1"])
            S.pool(TT(RT2[:], xo, cos, MUL), reads=["TM", "C_cos"], writes=["RT2"])
            S.pool(TT(qo[:, :, :, 1], RT1[:], RT2[:], ADD), reads=["RT1", "RT2"], writes=["QKR"])
            for b in range(4):
                S.pe(TR(PS[:, 4, b * 128:(b + 1) * 128], QKR[:, b * 128:(b + 1) * 128]), reads=["QKR", "C_ident"], writes=[bk(4)])
            S.act(ACT(QKTB[:], PS[:, 4, :].rearrange("p (c n) -> p c n", c=4), AF.Copy), reads=[bk(4)], writes=["QKTB"])
            S.dve(TT(QH[:], bc(QKTB[:, 0:2, :].unsqueeze(2), [128, 2, 2, 128]), bc(C["hmask"][:].unsqueeze(1).unsqueeze(3), [128, 2, 2, 128]), MUL),
                  reads=["QKTB", "C_hmask"], writes=["QH"])
            for h in range(4):
                blk, hl = h // 2, h % 2
                S.pe(MM(PS[:, 5, h * 128:(h + 1) * 128], QKTB[:, 2 + blk, :], QH[:, blk, hl, :]),
                     reads=["QKTB", "QH"], writes=[bk(5)])
            S.dve(TT(PTB[:], PS[:, 5, :].rearrange("p (c n) -> p c n", c=4), C["retd"][:, var], MUL),
                  reads=[bk(5), "C_retd"], writes=["PTB"])
            S.pool(TT(QD[:], QKTB[:, 0:2, :], C["ret_qs"][:, var], MUL), reads=["QKTB", "C_ret_qs"], writes=["QD"])
            S.pool(TT(KD[:].rearrange("p (h k) -> p h k", h=4), QKR[:, 256:512].rearrange("p (h k) -> p h k", h=4),
                      bc(C["ret_ks"][:, var, :].unsqueeze(2), [128, 4, 64]), MUL), reads=["QKR", "C_ret_ks"], writes=["KD"])
            for h in range(4):
                S.pe(MM(PS[:, 6, h * 64:(h + 1) * 64], PTB[:, h, :], VB[:, h * 64:(h + 1) * 64], start=(h == 0), stop=False),
                     reads=["PTB", "VB"], writes=[bk(6)])
            if var == 0:
                for blk in range(2):
                    S.pe(MM(PS[:, 6, blk * 128:(blk + 1) * 128], QD[:, blk, :], SRET[:, l, blk, :], start=False, stop=(blk == 1)),
                         reads=["QD", "SRET"], writes=[bk(6)])
                for blk in range(2):
                    S.pe(MM(PS[:, 7, blk * 128:(blk + 1) * 128], KD[:, blk * 128:(blk + 1) * 128], TM[:, 512 + blk * 128:512 + (blk + 1) * 128]),
                         reads=["KD", "TM"], writes=[bk(7)])
                for blk in range(2):
                    for hl in range(2):
                        sl = slice(64 * hl, 64 * hl + 64)
                        S.dve(STT(SRET[sl, l, blk, sl], SRET[sl, l, blk, sl], C["ret_a"][sl, 0, blk:blk + 1],
                                  PS[sl, 7, blk * 128 + 64 * hl:blk * 128 + 64 * hl + 64], MUL, ADD),
                              reads=["SRET", "C_ret_a", bk(7)], writes=["SRET"])
            else:
                for g in range(4):
                    load_sst("st_ret", l, g)
                    for blk in range(2):
                        S.dve(TT(QM[:], bc(QD[:, blk, :].unsqueeze(1), [128, 4, 128]), C["colmask"][:, 4 * g:4 * g + 4, :], MUL),
                              reads=["QD", "C_colmask"], writes=["QM"])
                        for bb in range(4):
                            last = (g == 3 and blk == 1 and bb == 3)
                            S.pe(MM(PS[:, 6, blk * 128:(blk + 1) * 128], QM[:, bb, :], SST[:, bb, blk, :], start=False, stop=last),
                                 reads=["QM", "SST"], writes=[bk(6)])
                    S.dve(TT(KDM[:], bc(KD[:].unsqueeze(1), [128, 4, 256]),
                             bc(C["rowmask"][:, 4 * g:4 * g + 4].unsqueeze(2), [128, 4, 256]), MUL),
                          reads=["KD", "C_rowmask"], writes=["KDM"])
                    for bb in range(4):
                        for blk in range(2):
                            S.pe(MM(PS[:, 4 + bb // 2, (bb % 2) * 256 + blk * 128:(bb % 2) * 256 + (blk + 1) * 128],
                                    KDM[:, bb, blk * 128:(blk + 1) * 128], TM[:, 512 + blk * 128:512 + (blk + 1) * 128]),
                                 reads=["KDM", "TM"], writes=[bk(4 + bb // 2)])
                    psv = PS[:, 4:6, :].rearrange("p a (b blk n) -> p (a b) blk n", b=2, blk=2)
                    for blk in range(2):
                        for hl in range(2):
                            sl = slice(64 * hl, 64 * hl + 64)
                            S.dve(STT(SST[sl, :, blk, sl], SST[sl, :, blk, sl], C["ret_a"][sl, 1, blk:blk + 1],
                                      psv[sl, :, blk, sl], MUL, ADD),
                                  reads=["SST", "C_ret_a", bk(4), bk(5)], writes=["SST"])
                    store_sst("s_ret", l, g)
            S.act(ACT(OB[:], PS[:, 6, 0:256], AF.Copy), reads=[bk(6)], writes=["OB"])
            S.pool(TT(OB2[:], OB[:], OB[:], MUL), reads=["OB"], writes=["OB2"])
            S.dve(lambda e: e.tensor_reduce(out=STA[:, 4:8], in_=OB[:].rearrange("p (h v) -> p h v", h=4), axis=AX.X, op=ADD),
                  reads=["OB"], writes=["STA"])
            S.dve(lambda e: e.tensor_reduce(out=STA[:, 8:12], in_=OB2[:].rearrange("p (h v) -> p h v", h=4), axis=AX.X, op=ADD),
                  reads=["OB2"], writes=["STA"])
            S.dve(TS(STA[:, 4:8], STA[:, 4:8], 1.0 / 64, None, MUL), reads=["STA"], writes=["STA"])
            S.dve(TT(STB[:, 0:4], STA[:, 4:8], STA[:, 4:8], MUL), reads=["STA"], writes=["STB"])
            S.dve(STT(STB[:, 0:4], STA[:, 8:12], 1.0 / 64, STB[:, 0:4], MUL, SUB), reads=["STA", "STB"], writes=["STB"])
            S.dve(TS(STB[:, 0:4], STB[:, 0:4], 0.0, None, ALU.max), reads=["STB"], writes=["STB"])
            S.act(ACT(STB[:, 0:4], STB[:, 0:4], AF.Ln, bias=EPS), reads=["STB"], writes=["STB"])
            S.act(ACT(STB[:, 0:4], STB[:, 0:4], AF.Exp, scale=-0.5), reads=["STB"], writes=["STB"])
            ov = OB[:].rearrange("p (h v) -> p h v", h=4)
            S.dve(TT(ov, ov, bc(STA[:, 4:8].unsqueeze(2), [128, 4, 64]), SUB), reads=["OB", "STA"], writes=["OB"])
            S.dve(TT(ov, ov, bc(STB[:, 0:4].unsqueeze(2), [128, 4, 64]), MUL), reads=["OB", "STB"], writes=["OB"])
            S.dve(TT(MIX[:, 0:256], OB[:], SGR[:], MUL), reads=["OB", "SGR"], writes=["MIX"])
            if var == 0 and i == LASTP:
                dst = dout["p_ret"][l].rearrange("(blk hl) k v -> hl k blk v", hl=2)
                for hl in range(2):
                    sl = slice(64 * hl, 64 * hl + 64)
                    dma_out("sp", dst[hl], SRET[sl, l, :, sl], "SRET")

        def hgrn(l, i, var):
            nb, e8 = (1, 128) if var == 0 else (SB, SL)
            HBv = HB[:].rearrange("p k (b e) -> p k b e", e=e8)
            S.act(ACT(HE[:], FM[:, 2:4, :], AF.Exp, scale=-1.0), reads=["FM"], writes=["HE"])
            S.act(ACT(HSIG[:], HE[:], AF.Ln, bias=1.0), reads=["HE"], writes=["HSIG"])
            S.act(ACT(HSIG[:], HSIG[:], AF.Exp, scale=-1.0), reads=["HSIG"], writes=["HSIG"])
            for blk in range(2):
                S.dve(TS(HD[:, blk, :], HSIG[:, blk, :], OML[:, l, blk:blk + 1], LOW[:, l, blk:blk + 1], MUL, ADD),
                      reads=["HSIG", "OML", "LOW"], writes=["HD"])
            S.act(ACT(HD[:], HD[:], AF.Ln), reads=["HD"], writes=["HD"])
            S.dve(TT(HKEY[:], HE[:], HSIG[:], MUL), reads=["HE", "HSIG"], writes=["HKEY"])
            S.dve(TT(HKEY[:], HKEY[:], bc(OML[:, l, :].unsqueeze(2), [128, 2, 128]), MUL), reads=["HKEY", "OML"], writes=["HKEY"])
            for blk in range(2):
                S.dve(lambda e, blk=blk: e.tensor_tensor_scan(out=HB[:, blk, :], data0=C["rst"][:, var, :], data1=HD[:, blk, :],
                                                              initial=0.0, op0=MUL, op1=ADD),
                      reads=["HD", "C_rst"], writes=["HB"])
            HB32 = HB[:].rearrange("p k (i e) -> p k i e", e=32)
            S.dve(TS(HREF[:, :, 1:4], HB32[:, :, 0:3, 31], -1.0, None, MUL), reads=["HB"], writes=["HREF"])
            S.dve(TT(HKV[:], bc(HB[:].unsqueeze(1), [128, 4, 2, 128]),
                     bc(HREF[:].rearrange("p k i -> p i k").unsqueeze(3), [128, 4, 2, 128]), ADD), reads=["HB", "HREF"], writes=["HKV"])
            S.dve(TS(HKV[:], HKV[:], -60.0, None, ALU.max), reads=["HKV"], writes=["HKV"])
            S.dve(TT(HKV4[:].rearrange("p k (b e) -> p k b e", e=e8), HBv, bc(HBv[:, :, :, e8 - 1:e8], [128, 2, nb, e8]), SUB),
                  reads=["HB"], writes=["HKV4"])
            S.act(ACT(HKV[:], HKV[:], AF.Exp, scale=-1.0), reads=["HKV"], writes=["HKV"])
            S.act(ACT(HKV4[:], HKV4[:], AF.Exp, scale=-1.0), reads=["HKV4"], writes=["HKV4"])
            S.dve(TT(HKVB[:], HKV[:], bc(HKEY[:].unsqueeze(1), [128, 4, 2, 128]), MUL), reads=["HKV", "HKEY"], writes=["HKVB"])
            S.dve(TT(HKV4[:], HKV4[:], HKEY[:], MUL), reads=["HKV4", "HKEY"], writes=["HKV4"])
            S.act(ACT(HEX[:], HB[:], AF.Exp), reads=["HB"], writes=["HEX"])
            S.dve(TT(HQD[:], HQ[:], HEX[:], MUL), reads=["HQ", "HEX"], writes=["HQD"])
            S.dve(TT(HQA[:].rearrange("p k (i e) -> p k i e", e=32), HB32, bc(HREF[:].unsqueeze(3), [128, 2, 4, 32]), ADD),
                  reads=["HB", "HREF"], writes=["HQA"])
            S.act(ACT(HQA[:], HQA[:], AF.Exp), reads=["HQA"], writes=["HQA"])
            S.dve(TT(HQA[:], HQA[:], HQ[:], MUL), reads=["HQA", "HQ"], writes=["HQA"])
            S.dve(TT(QH[:], bc(HQA[:].unsqueeze(2), [128, 2, 2, 128]), bc(C["hmask"][:].unsqueeze(1).unsqueeze(3), [128, 2, 2, 128]), MUL),
                  reads=["HQA", "C_hmask"], writes=["QH"])
            for h in range(4):
                blk, hl = h // 2, h % 2
                for iv in range(4):
                    S.pe(MM(PS[:, 5, h * 128 + 32 * iv:h * 128 + 32 * iv + 32], HKVB[:, iv, blk, :],
                            QH[:, blk, hl, 32 * iv:32 * iv + 32]), reads=["HKVB", "QH"], writes=[bk(5)])
            S.dve(TT(PTB[:], PS[:, 5, :].rearrange("p (c n) -> p c n", c=4), bc(C["causal"][:, var:var + 1, :], [128, 4, 128]), MUL),
                  reads=[bk(5), "C_causal"], writes=["PTB"])
            VOFF = 1024
            for h in range(4):
                S.pe(MM(PS[:, 6, h * 64:(h + 1) * 64], PTB[:, h, :], VB[:, 256 + h * 64:256 + (h + 1) * 64], start=(h == 0), stop=False),
                     reads=["PTB", "VB"], writes=[bk(6)])
            for blk in range(2):
                S.pe(TR(PS[:, 4, blk * 128:(blk + 1) * 128], HKV4[:, blk, :]), reads=["HKV4", "C_ident"], writes=[bk(4)])
            S.act(ACT(HKT[:], PS[:, 4, 0:256], AF.Copy), reads=[bk(4)], writes=["HKT"])
            S.act(ACT(EBL[:, :, 0:nb], HBv[:, :, :, e8 - 1], AF.Exp), reads=["HB"], writes=["EBL"])
            if var == 0:
                for blk in range(2):
                    S.pe(MM(PS[:, 6, blk * 128:(blk + 1) * 128], HQD[:, blk, :], SHG[:, l, blk, :], start=False, stop=(blk == 1)),
                         reads=["HQD", "SHG"], writes=[bk(6)])
                for blk in range(2):
                    S.pe(MM(PS[:, 7, blk * 128:(blk + 1) * 128], HKT[:, blk * 128:(blk + 1) * 128], TM[:, VOFF + blk * 128:VOFF + (blk + 1) * 128]),
                         reads=["HKT", "TM"], writes=[bk(7)])
                for blk in range(2):
                    for hl in range(2):
                        sl = slice(64 * hl, 64 * hl + 64)
                        S.dve(STT(SHG[sl, l, blk, sl], SHG[sl, l, blk, sl], EBL[sl, blk, 0:1],
                                  PS[sl, 7, blk * 128 + 64 * hl:blk * 128 + 64 * hl + 64], MUL, ADD),
                              reads=["SHG", "EBL", bk(7)], writes=["SHG"])
            else:
                for g in range(4):
                    load_sst("st_hg", l, g)
                    for blk in range(2):
                        S.dve(TT(QM[:], bc(HQD[:, blk, :].unsqueeze(1), [128, 4, 128]), C["colmask"][:, 4 * g:4 * g + 4, :], MUL),
                              reads=["HQD", "C_colmask"], writes=["QM"])
                        for bb in range(4):
                            last = (g == 3 and blk == 1 and bb == 3)
                            S.pe(MM(PS[:, 6, blk * 128:(blk + 1) * 128], QM[:, bb, :], SST[:, bb, blk, :], start=False, stop=last),
                                 reads=["QM", "SST"], writes=[bk(6)])
                    S.dve(TT(KDM[:], bc(HKT[:].unsqueeze(1), [128, 4, 256]),
                             bc(C["rowmask"][:, 4 * g:4 * g + 4].unsqueeze(2), [128, 4, 256]), MUL),
                          reads=["HKT", "C_rowmask"], writes=["KDM"])
                    for bb in range(4):
                        for blk in range(2):
                            S.pe(MM(PS[:, 2 + bb // 2, (bb % 2) * 256 + blk * 128:(bb % 2) * 256 + (blk + 1) * 128],
                                    KDM[:, bb, blk * 128:(blk + 1) * 128], TM[:, VOFF + blk * 128:VOFF + (blk + 1) * 128]),
                                 reads=["KDM", "TM"], writes=[bk(2 + bb // 2)])
                    psv = PS[:, 2:4, :].rearrange("p a (b blk n) -> p (a b) blk n", b=2, blk=2)
                    for blk in range(2):
                        for hl in range(2):
                            sl = slice(64 * hl, 64 * hl + 64)
                            S.dve(TT(SST[sl, :, blk, sl], SST[sl, :, blk, sl], bc(EBL[sl, blk, 4 * g:4 * g + 4].unsqueeze(2), [64, 4, 64]), MUL),
                                  reads=["SST", "EBL"], writes=["SST"])
                            S.dve(TT(SST[sl, :, blk, sl], SST[sl, :, blk, sl], psv[sl, :, blk, sl], ADD),
                                  reads=["SST", bk(2), bk(3)], writes=["SST"])
                    store_sst("s_hg", l, g)
            S.act(ACT(OB[:], PS[:, 6, 0:256], AF.Copy), reads=[bk(6)], writes=["OB"])
            S.pool(TT(OB2[:], OB[:], OB[:], MUL), reads=["OB"], writes=["OB2"])
            S.dve(lambda e: e.tensor_reduce(out=STA[:, 8:12], in_=OB2[:].rearrange("p (h v) -> p h v", h=4), axis=AX.X, op=ADD),
                  reads=["OB2"], writes=["STA"])
            S.act(ACT(STB[:, 0:4], STA[:, 8:12], AF.Ln, scale=1.0 / 64, bias=EPS), reads=["STA"], writes=["STB"])
            S.act(ACT(STB[:, 0:4], STB[:, 0:4], AF.Exp, scale=-0.5), reads=["STB"], writes=["STB"])
            ov = OB[:].rearrange("p (h v) -> p h v", h=4)
            S.dve(TT(ov, ov, bc(STB[:, 0:4].unsqueeze(2), [128, 4, 64]), MUL), reads=["OB", "STB"], writes=["OB"])
            S.dve(TT(MIX[:, 256:512], OB[:], SGH[:], MUL), reads=["OB", "SGH"], writes=["MIX"])
            if var == 0 and i == LASTP:
                dst = dout["p_hg"][l].rearrange("(blk hl) k v -> hl k blk v", hl=2)
                for hl in range(2):
                    sl = slice(64 * hl, 64 * hl + 64)
                    dma_out("sp", dst[hl], SHG[sl, l, :, sl], "SHG")

        def conv_phase(l, i, var):
            if var == 0:
                S.pool(CP(XPW[:, :, 0:3], XPH[:, l]), reads=["XPH"], writes=["XPW"])
                S.pool(CP(XPW[:, :, 3:131], FM[:, 4:10, :]), reads=["FM"], writes=["XPW"])
                for j in range(4):
                    src = XPW[:, :, j:j + 128]
                    w = bc(CW[:, l, :, j:j + 1], [128, 6, 128])
                    if j == 0:
                        S.pool(TT(XC[:], src, w, MUL), reads=["XPW", "CW"], writes=["XC"])
                    else:
                        S.pool(TT(CTMPP[:], src, w, MUL), reads=["XPW", "CW"], writes=["CTMPP"])
                        S.pool(TT(XC[:], XC[:], CTMPP[:], ADD), reads=["XC", "CTMPP"], writes=["XC"])
                S.pool(CP(XPH[:, l], XPW[:, :, 128:131]), reads=["XPW"], writes=["XPH"])
                if i == LASTP:
                    for j in range(3):
                        S.dma("sp", lambda e, j=j: e.dma_start(out=dout["p_conv"][l, j].rearrange("(b p) -> p b", p=128),
                                                               in_=XPH[:, l, :, j], allow_slow_non_contiguous=True),
                              reads=["XPH"], dkey=("st", "XPH"))
            else:
                dma_in("sp", CVT[0:48, :], din["st_conv"][l], "CVT")
                for c6 in range(6):
                    S.pe(TR(PS[:, 4, c6 * 48:(c6 + 1) * 48], CVT[0:48, c6 * 128:(c6 + 1) * 128], n=48), reads=["CVT", "C_ident"], writes=[bk(4)])
                S.act(ACT(XPS[:, :, :, 0:3], PS[:, 4, 0:288].rearrange("p (c b j) -> p c b j", c=6, b=SB), AF.Copy),
                      reads=[bk(4)], writes=["XPS"])
                S.pool(CP(XPS[:, :, :, 3:11], FM[:, 4:10, :].rearrange("p c (b e) -> p c b e", e=SL)), reads=["FM"], writes=["XPS"])
                cav = CACC[:].rearrange("p c (b e) -> p c b e", e=SL)
                ctv = CTMPS[:].rearrange("p c (b e) -> p c b e", e=SL)
                for j in range(4):
                    src = XPS[:, :, :, j:j + SL]
                    w = bc(CW[:, l, :, j:j + 1].unsqueeze(3), [128, 6, SB, SL])
                    if j == 0:
                        S.pool(TT(cav, src, w, MUL), reads=["XPS", "CW"], writes=["CACC"])
                    else:
                        S.pool(TT(ctv, src, w, MUL), reads=["XPS", "CW"], writes=["CTMPS"])
                        S.pool(TT(CACC[:], CACC[:], CTMPS[:], ADD), reads=["CACC", "CTMPS"], writes=["CACC"])
                S.pool(CP(CTMPS[:, :, 0:48].rearrange("p c (b j) -> p c b j", j=3), XPS[:, :, :, 8:11]), reads=["XPS", "CACC"], writes=["CTMPS"])
                for c6 in range(6):
                    bnk, off = (5, c6 * 128) if c6 < 4 else (6, (c6 - 4) * 128)
                    S.pe(TR(PS[0:48, bnk, off:off + 128], CTMPS[:, c6, 0:48]), reads=["CTMPS", "C_ident"], writes=[bk(bnk)])
                S.act(ACT(CVT[0:48, 0:512], PS[0:48, 5, :], AF.Copy), reads=[bk(5)], writes=["CVT"])
                S.act(ACT(CVT[0:48, 512:768], PS[0:48, 6, 0:256], AF.Copy), reads=[bk(6)], writes=["CVT"])
                dma_out("sp", dout["s_conv"][l], CVT[0:48, :], "CVT")

        def conv_silu(l, var):
            for c6 in range(6):
                if var == 0:
                    S.act(ACT(XC[:, c6, :], XC[:, c6, :], AF.Silu, bias=CB[:, l, c6:c6 + 1]), reads=["XC", "CB"], writes=["XC"])
                else:
                    S.act(ACT(XC[:, c6, :], CACC[:, c6, :], AF.Silu, bias=CB[:, l, c6:c6 + 1]), reads=["CACC", "CB"], writes=["XC"])

        def silu_phase(l):
            S.act(ACT(SGR[:], TM[:, 768:1024], AF.Silu), reads=["TM"], writes=["SGR"])
            S.act(ACT(SGH[:], TM[:, 1280:1536], AF.Silu), reads=["TM"], writes=["SGH"])
            S.act(ACT(SSZ[:], TM[:, 1536:1792], AF.Silu), reads=["TM"], writes=["SSZ"])
            S.act(ACT(HQ[:], FM[:, 0:2, :], AF.Silu), reads=["FM"], writes=["HQ"])

        def ssd(l, i, var):
            nb, e8 = (1, 128) if var == 0 else (SB, SL)
            S.dve(TT(DEL[:], TM[:, 1792:1796], DTB[:, l, :], ADD), reads=["TM", "DTB"], writes=["DEL"])
            S.act(ACT(DEL[:], DEL[:], AF.Exp), reads=["DEL"], writes=["DEL"])
            S.act(ACT(DEL[:], DEL[:], AF.Ln, bias=1.0), reads=["DEL"], writes=["DEL"])
            S.act(ACT(LND[:], DEL[:], AF.Ln), reads=["DEL"], writes=["LND"])
            S.dve(TT(LOGA[:], DEL[:], NEGA[:, l, :], MUL), reads=["DEL", "NEGA"], writes=["LOGA"])
            S.pe(MM(PS[:, 7, 0:4], C["causal"][:, var, :], LOGA[:]), reads=["C_causal", "LOGA"], writes=[bk(7)])
            S.pe(MM(PS[:, 7, 4:8], C["sameseq"][:, var, :], LOGA[:]), reads=["C_sameseq", "LOGA"], writes=[bk(7)])
            S.dve(TT(BCOL[:], PS[:, 7, 0:4], LND[:], SUB), reads=[bk(7), "LND"], writes=["BCOL"])
            S.dve(TT(BLAST[:], PS[:, 7, 4:8], BCOL[:], SUB), reads=[bk(7), "BCOL"], writes=["BLAST"])
            S.act(ACT(KSC[:], BLAST[:], AF.Exp), reads=["BLAST"], writes=["KSC"])
            S.pool(CP(LB[:], bc(LOGA[:].unsqueeze(2), [128, 4, 128])), reads=["LOGA"], writes=["LB"])
            for h in range(4):
                S.pe(MM(PS[:, 4, h * 128:(h + 1) * 128], LB[:, h, :], C["causal"][:, var, :]), reads=["LB", "C_causal"], writes=[bk(4)])
            p4 = PS[:, 4, :].rearrange("p (c n) -> p c n", c=4)
            S.dve(TT(D1[:], p4, bc(BCOL[:].unsqueeze(2), [128, 4, 128]), SUB), reads=[bk(4), "BCOL"], writes=["D1"])
            S.pool(TT(D1[:], D1[:], bc(C["negmask"][:, var:var + 1, :], [128, 4, 128]), ADD), reads=["D1", "C_negmask"], writes=["D1"])
            S.act(ACT(DEC[:], D1[:], AF.Exp), reads=["D1"], writes=["DEC"])
            S.act(ACT(EBT[:], p4, AF.Exp), reads=[bk(4)], writes=["EBT"])
            for g in range(2):
                S.pe(MM(PS[:, 5, g * 128:(g + 1) * 128], XC[:, 2 + g, :], XC[:, 4 + g, :]), reads=["XC"], writes=[bk(5)])
            for g in range(2):
                S.dve(TT(PT2B[:, 2 * g:2 * g + 2, :], bc(PS[:, 5, g * 128:(g + 1) * 128].unsqueeze(1), [128, 2, 128]), DEC[:, 2 * g:2 * g + 2, :], MUL),
                      reads=[bk(5), "DEC"], writes=["PT2B"])
                S.pool(TT(CMD[:, 2 * g:2 * g + 2, :], bc(XC[:, 4 + g, :].unsqueeze(1), [128, 2, 128]), EBT[:, 2 * g:2 * g + 2, :], MUL),
                       reads=["XC", "EBT"], writes=["CMD"])
            for b4 in range(4):
                S.pe(TR(PS[:, 6, b4 * 128:(b4 + 1) * 128], XC[:, b4, :]), reads=["XC", "C_ident"], writes=[bk(6)])
            S.act(ACT(XS[:], PS[:, 6, 0:256], AF.Copy), reads=[bk(6)], writes=["XS"])
            S.act(ACT(XSB[:], PS[:, 6, 0:256], AF.Copy), reads=[bk(6)], writes=["XSB"])
            S.act(ACT(BMT[:], PS[:, 6, 256:512].rearrange("p (g n) -> p g n", g=2), AF.Copy), reads=[bk(6)], writes=["BMT"])
            for g in range(2):
                S.pool(TT(BD[:, 2 * g:2 * g + 2, :], bc(BMT[:, g, :].unsqueeze(1), [128, 2, 128]),
                          bc(KSC[:, 2 * g:2 * g + 2].unsqueeze(2), [128, 2, 128]), MUL), reads=["BMT", "KSC"], writes=["BD"])
            for h in range(4):
                S.pe(MM(PS[:, 7, h * 64:(h + 1) * 64], PT2B[:, h, :], XSB[:, h * 64:(h + 1) * 64], start=(h == 0), stop=False),
                     reads=["PT2B", "XSB"], writes=[bk(7)])
            if var == 0:
                for h in range(4):
                    S.pe(MM(PS[:, 7, h * 64:(h + 1) * 64], CMD[:, h, :], SSSD[:, l, h, :], start=False, stop=(h == 3)),
                         reads=["CMD", "SSSD"], writes=[bk(7)])
                for h in range(4):
                    S.pe(MM(PS[:, 5, h * 64:(h + 1) * 64], BD[:, h, :], XS[:, h * 64:(h + 1) * 64]), reads=["BD", "XS"], writes=[bk(5)])
                S.dve(TT(SSSD[:, l], SSSD[:, l], bc(EBT[:, :, 127:128], [128, 4, 64]), MUL), reads=["SSSD", "EBT"], writes=["SSSD"])
                S.dve(TT(SSSD[:, l], SSSD[:, l], PS[:, 5, 0:256].rearrange("p (h v) -> p h v", h=4), ADD),
                      reads=["SSSD", bk(5)], writes=["SSSD"])
            else:
                EBTv = EBT[:].rearrange("p h (b e) -> p h b e", e=SL)
                for g in range(4):
                    dma_in("sp", SST2[:], din["st_ssd"][l, 4 * g:4 * g + 4].rearrange("b h n v -> n b h v"), "SST2")
                    for h in range(4):
                        S.dve(TT(QM[:], bc(CMD[:, h, :].unsqueeze(1), [128, 4, 128]), C["colmask"][:, 4 * g:4 * g + 4, :], MUL),
                              reads=["CMD", "C_colmask"], writes=["QM"])
                        for bb in range(4):
                            last = (g == 3 and h == 3 and bb == 3)
                            S.pe(MM(PS[:, 7, h * 64:(h + 1) * 64], QM[:, bb, :], SST2[:, bb, h, :], start=False, stop=last),
                                 reads=["QM", "SST2"], writes=[bk(7)])
                        S.dve(TT(BDM[:], bc(BD[:, h, :].unsqueeze(1), [128, 4, 128]),
                                 bc(C["rowmask"][:, 4 * g:4 * g + 4].unsqueeze(2), [128, 4, 128]), MUL),
                              reads=["BD", "C_rowmask"], writes=["BDM"])
                        for bb in range(4):
                            col = ((bb % 2) * 4 + h) * 64
                            S.pe(MM(PS[:, 2 + bb // 2, col:col + 64], BDM[:, bb, :], XS[:, h * 64:(h + 1) * 64]),
                                 reads=["BDM", "XS"], writes=[bk(2 + bb // 2)])
                    psv = PS[:, 2:4, :].rearrange("p a (b h v) -> p (a b) h v", b=2, h=4)
                    ach = EBTv[:, :, 4 * g:4 * g + 4, SL - 1].rearrange("p h b -> p b h")
                    S.dve(TT(SST2[:], SST2[:], bc(ach.unsqueeze(3), [128, 4, 4, 64]), MUL), reads=["SST2", "EBT"], writes=["SST2"])
                    S.dve(TT(SST2[:], SST2[:], psv, ADD), reads=["SST2", bk(2), bk(3)], writes=["SST2"])
                    dma_out("sp", dout["s_ssd"][l, 4 * g:4 * g + 4].rearrange("b h n v -> n b h v"), SST2[:], "SST2")
            S.act(ACT(OB[:], PS[:, 7, 0:256], AF.Copy), reads=[bk(7)], writes=["OB"])
            ov = OB[:].rearrange("p (h v) -> p h v", h=4)
            S.pool(TT(OB2[:].rearrange("p (h v) -> p h v", h=4), XS[:].rearrange("p (h v) -> p h v", h=4),
                      bc(SSDD[:, l, :].unsqueeze(2), [128, 4, 64]), MUL), reads=["XS", "SSDD"], writes=["OB2"])
            S.dve(TT(OB[:], OB[:], OB2[:], ADD), reads=["OB", "OB2"], writes=["OB"])
            S.dve(TT(OB[:], OB[:], SSZ[:], MUL), reads=["OB", "SSZ"], writes=["OB"])
            S.pool(TT(OB2[:], OB[:], OB[:], MUL), reads=["OB"], writes=["OB2"])
            S.dve(lambda e: e.tensor_reduce(out=STA[:, 8:10], in_=OB2[:].rearrange("p (g v) -> p g v", g=2), axis=AX.X, op=ADD),
                  reads=["OB2"], writes=["STA"])
            S.act(ACT(STB[:, 0:2], STA[:, 8:10], AF.Ln, scale=1.0 / 128, bias=EPS), reads=["STA"], writes=["STB"])
            S.act(ACT(STB[:, 0:2], STB[:, 0:2], AF.Exp, scale=-0.5), reads=["STB"], writes=["STB"])
            og = OB[:].rearrange("p (g v) -> p g v", g=2)
            S.dve(TT(og, og, bc(STB[:, 0:2].unsqueeze(2), [128, 2, 128]), MUL), reads=["OB", "STB"], writes=["OB"])
            S.dve(TT(MIX[:, 512:768], OB[:], SNORM[:], MUL), reads=["OB", "SNORM"], writes=["MIX"])
            if var == 0 and i == LASTP:
                dma_out("sp", dout["p_ssd"][l].rearrange("h n v -> n h v"), SSSD[:, l], "SSSD")

        def s5(l, i, var, hook=None):
            nb, e8 = (1, 128) if var == 0 else (SB, SL)
            if var == 1:
                for ri, nm in ((0, "st_s5re"), (1, "st_s5im")):
                    dma_in("sp", ZT[0:16, :], din[nm][l], "ZT")
                    for c in range(8):
                        S.pe(TR(PS[:, 4, c * 16:(c + 1) * 16], ZT[0:16, c * 128:(c + 1) * 128], n=16), reads=["ZT", "C_ident"], writes=[bk(4)])
                    S.act(ACT(Z0[:, ri], PS[:, 4, 0:128].rearrange("p (c b) -> p c b", b=SB), AF.Copy), reads=[bk(4)], writes=["Z0"])
            if var == 0:
                zr, zi, zk = ZS5[:, l, 0, :].unsqueeze(2), ZS5[:, l, 1, :].unsqueeze(2), "ZS5"
            else:
                zr, zi, zk = Z0[:, 0], Z0[:, 1], "Z0"
            ar = bc(ABAR[:, 0, :].unsqueeze(2), [128, 8, nb])
            ai = bc(ABAR[:, 1, :].unsqueeze(2), [128, 8, nb])
            S.dve(TT(INJ[:, 0, :, 0:nb], zr, ar, MUL), reads=[zk, "ABAR"], writes=["INJ"])
            S.dve(TT(INJ2[:, 0, :, 0:nb], zi, ai, MUL), reads=[zk, "ABAR"], writes=["INJ2"])
            S.dve(TT(INJ[:, 0, :, 0:nb], INJ[:, 0, :, 0:nb], INJ2[:, 0, :, 0:nb], SUB), reads=["INJ", "INJ2"], writes=["INJ"])
            S.dve(TT(INJ[:, 1, :, 0:nb], zi, ar, MUL), reads=[zk, "ABAR"], writes=["INJ"])
            S.dve(TT(INJ2[:, 1, :, 0:nb], zr, ai, MUL), reads=[zk, "ABAR"], writes=["INJ2"])
            S.dve(TT(INJ[:, 1, :, 0:nb], INJ[:, 1, :, 0:nb], INJ2[:, 1, :, 0:nb], ADD), reads=["INJ", "INJ2"], writes=["INJ"])
            S.dve(TT(PI[:], bc(RR[:].unsqueeze(2), [128, 8, 128]), bc(C["rst"][:, var:var + 1, :], [128, 8, 128]), MUL),
                  reads=["RR", "C_rst"], writes=["PI"])
            for ri in range(2):
                for c in range(8):
                    bnk = 4 + 2 * ri + c // 4
                    S.pe(MM(PS[:, bnk, (c % 4) * 128:(c % 4 + 1) * 128], BCF[:, ri, c, :], FM[:, 10 + c // 4, :]),
                         reads=["BCF", "FM"], writes=[bk(bnk)])
            if hook is not None:
                hook()
            pre = PS[:, 4:6, :].rearrange("p a (c n) -> p (a c) n", n=128)
            pim = PS[:, 6:8, :].rearrange("p a (c n) -> p (a c) n", n=128)
            S.dve(TT(T1[:], pre, COSR[:], MUL), reads=[bk(4), bk(5), "COSR"], writes=["T1"])
            S.dve(TT(T2[:], pim, SINR[:], MUL), reads=[bk(6), bk(7), "SINR"], writes=["T2"])
            S.pool(TT(XR[:], T1[:], T2[:], ADD), reads=["T1", "T2"], writes=["XR"])
            S.dve(TT(WR[:], pim, COSR[:], MUL), reads=[bk(6), bk(7), "COSR"], writes=["WR"])
            S.dve(TT(WI[:], pre, SINR[:], MUL), reads=[bk(4), bk(5), "SINR"], writes=["WI"])
            S.pool(TT(XI[:], WR[:], WI[:], SUB), reads=["WR", "WI"], writes=["XI"])
            xrv = XR[:].rearrange("p c (b e) -> p c b e", e=e8)
            xiv = XI[:].rearrange("p c (b e) -> p c b e", e=e8)
            S.dve(TT(xrv[:, :, :, 0], xrv[:, :, :, 0], INJ[:, 0, :, 0:nb], ADD), reads=["XR", "INJ"], writes=["XR"])
            S.dve(TT(xiv[:, :, :, 0], xiv[:, :, :, 0], INJ[:, 1, :, 0:nb], ADD), reads=["XI", "INJ"], writes=["XI"])
            fl = "p c t -> p (c t)"
            S.dve(lambda e: e.tensor_tensor_scan(out=WR[:].rearrange(fl), data0=PI[:].rearrange(fl), data1=XR[:].rearrange(fl),
                                                 initial=0.0, op0=MUL, op1=ADD), reads=["PI", "XR"], writes=["WR"])
            S.dve(lambda e: e.tensor_tensor_scan(out=WI[:].rearrange(fl), data0=PI[:].rearrange(fl), data1=XI[:].rearrange(fl),
                                                 initial=0.0, op0=MUL, op1=ADD), reads=["PI", "XI"], writes=["WI"])
            S.dve(TT(PR[:], WR[:], COSR[:], MUL), reads=["WR", "COSR"], writes=["PR"])
            S.dve(TT(PI[:], WI[:], SINR[:], MUL), reads=["WI", "SINR"], writes=["PI"])
            S.dve(TT(T2[:], WR[:], SINR[:], MUL), reads=["WR", "SINR"], writes=["T2"])
            S.dve(TT(T1[:], WI[:], COSR[:], MUL), reads=["WI", "COSR"], writes=["T1"])
            if var == 0:
                S.dve(TT(ZS5[:, l, 0, :], PR[:, :, 127], PI[:, :, 127], SUB), reads=["PR", "PI"], writes=["ZS5"])
                S.dve(TT(ZS5[:, l, 1, :], T1[:, :, 127], T2[:, :, 127], ADD), reads=["T1", "T2"], writes=["ZS5"])
            else:
                def lastv(t_):
                    return t_[:].rearrange("p c (b e) -> p c b e", e=SL)[:, :, :, SL - 1]
                S.dve(TT(ZL[:, 0], lastv(PR), lastv(PI), SUB), reads=["PR", "PI"], writes=["ZL"])
                S.dve(TT(ZL[:, 1], lastv(T1), lastv(T2), ADD), reads=["T1", "T2"], writes=["ZL"])
            S.dve(TT(XRB[:], PR[:], PI[:], SUB), reads=["PR", "PI"], writes=["XRB"])
            S.dve(TT(XIB[:], T1[:], T2[:], ADD), reads=["T1", "T2"], writes=["XIB"])
            for c in range(8):
                S.pe(MM(PS[:, 4, 32 * c:32 * c + 32], XRB[:, c, :], S5CB[:, c, :], start=(c == 0), stop=False), reads=["XRB", "S5CB"], writes=[bk(4)])
                S.pe(MM(PS[:, 4, 32 * c:32 * c + 32], XIB[:, c, :], S5CIB[:, c, :], start=False, stop=False), reads=["XIB", "S5CIB"], writes=[bk(4)])
            for ub in range(2):
                S.pe(MM(PS[:, 4, ub * 128:(ub + 1) * 128], FM[:, 10 + ub, :], DIAGD[:, ub, :], start=False, stop=(ub == 1)),
                     reads=["FM", "DIAGD"], writes=[bk(4)])
            S.act(ACT(YG[:], PS[:, 4, 0:256], AF.Copy), reads=[bk(4)], writes=["YG"])
            S.act(ACT(YU[:], PS[:, 4, 0:256], AF.Square), reads=[bk(4)], writes=["YU"])
            S.dve(TS(YU[:], YU[:], 0.044715, 1.0, MUL, ADD), reads=["YU"], writes=["YU"])
            S.dve(TT(YU[:], YU[:], YG[:], MUL), reads=["YU", "YG"], writes=["YU"])
            S.dve(TS(YU[:], YU[:], -26.0, None, ALU.max), reads=["YU"], writes=["YU"])
            S.act(ACT(YU[:], YU[:], AF.Exp, scale=-2.0 * math.sqrt(2.0 / math.pi)), reads=["YU"], writes=["YU"])
            S.act(ACT(YU[:], YU[:], AF.Ln, bias=1.0), reads=["YU"], writes=["YU"])
            S.act(ACT(YU[:], YU[:], AF.Exp, scale=-1.0), reads=["YU"], writes=["YU"])
            S.dve(TT(YE[:], YG[:], YU[:], MUL), reads=["YG", "YU"], writes=["YE"])
            for ub in range(2):
                S.pe(TR(PS[:, 5, ub * 128:(ub + 1) * 128], YE[:, ub * 128:(ub + 1) * 128]), reads=["YE", "C_ident"], writes=[bk(5)])
            S.act(ACT(YGT[:], PS[:, 5, 0:256].rearrange("p (k n) -> p k n", k=2), AF.Copy), reads=[bk(5)], writes=["YGT"])
            for k in range(2):
                S.pe(MM(PS[:, 6, 0:256], YGT[:, k, :], GLUW[:, k, :], start=(k == 0), stop=(k == 1)), reads=["YGT", "GLUW"], writes=[bk(6)])
            S.dve(STT(YU[:], PS[:, 6, 0:256], -40.0, GLUB[:], ALU.max, ADD), reads=[bk(6), "GLUB"], writes=["YU"])
            S.act(ACT(YU[:], YU[:], AF.Exp, scale=-1.0), reads=["YU"], writes=["YU"])
            S.act(ACT(YU[:], YU[:], AF.Ln, bias=1.0), reads=["YU"], writes=["YU"])
            S.act(ACT(YU[:], YU[:], AF.Exp, scale=-1.0), reads=["YU"], writes=["YU"])
            S.dve(TT(MIX[:, 768:1024], YE[:], YU[:], MUL), reads=["YE", "YU"], writes=["MIX"])
            if var == 0 and i == LASTP:
                for ri in range(2):
                    S.pe(TR(PS[0:8, 7, ri * 128:(ri + 1) * 128], ZS5[:, l, ri, :]), reads=["ZS5", "C_ident"], writes=[bk(7)])
                S.act(ACT(ZT[0:8, 0:256], PS[0:8, 7, 0:256], AF.Copy), reads=[bk(7)], writes=["ZT"])
                dma_out("sp", dout["p_s5re"][l], ZT[0:8, 0:128], "ZT")
                dma_out("sp", dout["p_s5im"][l], ZT[0:8, 128:256], "ZT")
            if var == 1:
                for ri, nm in ((0, "s_s5re"), (1, "s_s5im")):
                    for c in range(8):
                        S.pe(TR(PS[0:16, 6 + c // 4, (c % 4) * 128:(c % 4 + 1) * 128], ZL[:, ri, c, :]), reads=["ZL", "C_ident"],
                             writes=[bk(6 + c // 4)])
                    S.act(ACT(ZT[0:16, 0:512], PS[0:16, 6, :], AF.Copy), reads=[bk(6)], writes=["ZT"])
                    S.act(ACT(ZT[0:16, 512:1024], PS[0:16, 7, :], AF.Copy), reads=[bk(7)], writes=["ZT"])
                    dma_out("sp", dout[nm][l], ZT[0:16, :], "ZT")

        def phase1(l, j, i, hook=None):
            var = 0 if i < NPT else 1
            xk = ("X", j)
            if var == 1 and "ssd" in mixers:
                conv_phase(l, i, var)
                conv_silu(l, var)
            silu_phase(l)
            if var == 1:
                S.pool(lambda e: e.memset(SST[:], 0.0), writes=["SST"])
                if "s5" in mixers:
                    s5_sample_tables()
            if "ret" in mixers:
                retention(l, i, var)
            if var == 0 and "ssd" in mixers:
                conv_phase(l, i, var)
            if "hg" in mixers:
                hgrn(l, i, var)
            if "ssd" in mixers:
                if var == 0:
                    conv_silu(l, var)
                ssd(l, i, var)
            if "s5" in mixers:
                s5(l, i, var, hook)
            elif hook is not None:
                hook()
            for half in range(2):
                for cc in range(4):
                    c = half * 4 + cc
                    S.pe(TR(PS[:, half, cc * 128:(cc + 1) * 128], MIX[:, c * 128:(c + 1) * 128]), reads=["MIX", "C_ident"], writes=[bk(half)])
                src = PS[:, half, :].rearrange("p (c n) -> p c n", c=4)
                if half == 0:
                    S.act(ACT(MIXT[:, 0:4, :], src, AF.Copy), reads=[bk(0)], writes=["MIXT"])
                else:
                    S.dve(CP(MIXT[:, 4:8, :], src), reads=[bk(1)], writes=["MIXT"])
            for n in range(2):
                for k in range(8):
                    S.pe(MM(PS[:, 2 + n, :], MIXT[:, k, :], WOUT[:, k, n * 512:(n + 1) * 512], start=(k == 0), stop=(k == 7)),
                         reads=["MIXT", "WOUT"], writes=[bk(2 + n)])
            S.dve(TT(X[:, j, :], X[:, j, :], PS[:, 2:4, :].rearrange("p a n -> p (a n)"), ADD),
                  reads=[xk, bk(2), bk(3)], writes=[xk])

        def norm1(l, j, par):
            norm_T(X[:, j, :], ("X", j), NM[:, l, :], "NM", lambda h: HTS[par][:, 4 * h:4 * h + 4, :], "HT%d" % par)

        def norm2(l, j):
            if do_mlp:
                norm_T(X[:, j, :], ("X", j), NMLP[:, l, :], "NMLP", lambda h: H2T[:, j, 4 * h:4 * h + 4, :], ("H2T", j))

        def mlp_issue(l, ch):
            r = ch % 2
            S.dma("pool", lambda e: e.dma_start(out=WUP[r], in_=din["w_up"][l][:, ch * FCH:(ch + 1) * FCH].rearrange("(c p) n -> p c n", p=128)),
                  writes=["WUP%d" % r])
            S.dma("pool", lambda e: e.dma_start(out=WDN[r], in_=din["w_down"][l][ch * FCH:(ch + 1) * FCH, :].rearrange("(f p) n -> p f n", p=128)),
                  writes=["WDN%d" % r])

        def mlp(l, ntile):
            pairs = [(a, min(2, ntile - a)) for a in range(0, ntile, 2)]
            mlp_issue(l, 0)
            mlp_issue(l, 1)
            NF = FCH // 128
            dbank = [0]
            for ch in range(NFC):
                r = ch % 2
                for pi2, (j0, ln) in enumerate(pairs):
                    N = 128 * ln
                    hk = [("H2T", j0 + t) for t in range(ln)]
                    at = ACTT2[pi2 % 2]
                    ak = "ACTT%d" % (pi2 % 2)
                    for f in range(NF):
                        ub = f % 2
                        for k in range(8):
                            S.pe(MM(PS[:, ub, 0:N], WUP[r][:, k, f * 128:(f + 1) * 128], H2T[:, j0:j0 + ln, k, :], start=(k == 0), stop=(k == 7)),
                                 reads=hk + ["WUP%d" % r], writes=[bk(ub)])
                        S.act(ACT(RELU[ub][:, 0:N], PS[:, ub, 0:N], AF.Relu), reads=[bk(ub)], writes=["RELU%d" % ub])
                        S.pool(TT(at[:, f, 0:N], RELU[ub][:, 0:N], RELU[ub][:, 0:N], MUL), reads=["RELU%d" % ub], writes=[ak])
                for pi2, (j0, ln) in enumerate(pairs):
                    at = ACTT2[pi2 % 2]
                    ak = "ACTT%d" % (pi2 % 2)
                    for t in range(ln):
                        b0 = 2 + 2 * (dbank[0] % 3)
                        dbank[0] += 1
                        for n in range(2):
                            for f in range(NF):
                                S.pe(MM(PS[:, b0 + n, :], at[:, f, t * 128:(t + 1) * 128], WDN[r][:, f, n * 512:(n + 1) * 512],
                                        start=(f == 0), stop=(f == NF - 1)),
                                     reads=[ak, "WDN%d" % r], writes=[bk(b0 + n)])
                        xk = ("X", j0 + t)
                        S.dve(TT(X[:, j0 + t, :], X[:, j0 + t, :], PS[:, b0:b0 + 2, :].rearrange("p a n -> p (a n)"), ADD),
                              reads=[xk, bk(b0), bk(b0 + 1)], writes=[xk])
                if ch + 2 < NFC:
                    mlp_issue(l, ch + 2)

        def w_issue(l):
            S.dma("pool", lambda e: e.dma_start(out=WIN[:], in_=din["w_in"][l].rearrange("(c p) n -> p c n", p=128)), writes=["WIN"])
            S.dma("pool", lambda e: e.dma_start(out=WOUT[:], in_=din["w_out"][l].rearrange("(c p) n -> p c n", p=128)), writes=["WOUT"])

        def final_norm(tiles):
            dma_in("sp", WF[:], din["norm_final"].partition_broadcast(128), "WF")
            for j, i in enumerate(tiles):
                xk = ("X", j)
                S.act(ACT(JUNK, X[:, j, :], AF.Square, accum_out=STA[:, 0:1]), reads=[xk], writes=["JUNK", "STAN"])
                S.act(ACT(STA[:, 1:2], STA[:, 0:1], AF.Ln, scale=1.0 / D, bias=EPS), reads=["STAN"], writes=["STAN"])
                S.act(ACT(STA[:, 2:3], STA[:, 1:2], AF.Exp, scale=-0.5), reads=["STAN"], writes=["STAN"])
                S.act(ACT(YOUT[:], X[:, j, :], AF.Copy, scale=STA[:, 2:3]), reads=[xk, "STAN"], writes=["YOUT"])
                S.dve(TT(YOUT[:], YOUT[:], WF[:], MUL), reads=["YOUT", "WF"], writes=["YOUT"])
                if i < NPT:
                    dma_out("sp", dout["yp"][i * 128:(i + 1) * 128, :], YOUT[:], "YOUT")
                else:
                    dma_out("sp", dout["ys"], YOUT[:], "YOUT")

        plan = [(g, l) for g in range(len(sgs)) for l in range(n_layers)]
        w_issue(plan[0][1])
        for pi_, (g, l) in enumerate(plan):
            tiles = sgs[g]
            if l == 0:
                for j, i in enumerate(tiles):
                    src = din["xp"][i * 128:(i + 1) * 128, :] if i < NPT else din["xs"]
                    dma_in("sp", X[:, j, :], src, ("X", j))
            layer_params(l)
            if "s5" in mixers:
                s5_setup(l)
            norm1(l, 0, 0)
            proj_tm(0)
            for j, i in enumerate(tiles):
                proj_fm(j % 2)

                def hook(j=j, l=l, n=len(tiles)):
                    if j > 0:
                        norm2(l, j - 1)
                    if j + 1 < n:
                        norm1(l, j + 1, (j + 1) % 2)
                        proj_tm((j + 1) % 2)

                phase1(l, j, i, hook)
            norm2(l, len(tiles) - 1)
            if pi_ + 1 < len(plan):
                w_issue(plan[pi_ + 1][1])
            if do_mlp:
                mlp(l, len(tiles))
            if l == n_layers - 1:
                final_norm(tiles)
        S.emit()
    return nc, S


_CONSTS = None


def _host_params(inp):
    f = np.float32
    p = {}
    p["nm_fm"] = np.ascontiguousarray(inp["norm_mix"].reshape(2, 8, 128).transpose(0, 2, 1))
    p["nmlp_fm"] = np.ascontiguousarray(inp["norm_mlp"].reshape(2, 8, 128).transpose(0, 2, 1))
    p["norm_final"] = np.ascontiguousarray(inp["norm_final"])
    for k in ("w_in", "w_out", "w_up", "w_down"):
        p[k] = np.ascontiguousarray(inp[k])
    p["hg_lg"] = np.ascontiguousarray(inp["hg_lb_logits"].reshape(2, 2, 128).transpose(2, 1, 0))
    p["conv_w"] = np.ascontiguousarray(inp["ssd_conv_w"].reshape(2, 4, 6, 128).transpose(0, 3, 2, 1))
    p["conv_b"] = np.ascontiguousarray(inp["ssd_conv_b"].reshape(2, 6, 128).transpose(0, 2, 1))
    p["dt_bias"] = np.ascontiguousarray(inp["ssd_dt_bias"])
    p["a_log"] = np.ascontiguousarray(inp["ssd_a_log"])
    p["ssd_d"] = np.ascontiguousarray(inp["ssd_d"])
    p["ssd_norm"] = np.ascontiguousarray(inp["ssd_norm"])

    def fm8(a):
        return np.ascontiguousarray(a.reshape(2, 8, 2, 64).transpose(0, 2, 3, 1).reshape(2, 128, 8))

    p["s5_are"] = fm8(inp["s5_a_re"])
    p["s5_aim"] = fm8(inp["s5_a_im"])
    p["s5_ldt"] = fm8(np.broadcast_to(inp["s5_log_dt"][:, :, None], (2, 16, 64)))
    for nm, src in (("s5_bre", inp["s5_b_re"]), ("s5_bim", inp["s5_b_im"])):
        o = np.zeros((2, 128, 8, 2, 64), f)
        for c in range(8):
            for gl in range(2):
                g = 2 * c + gl
                r0 = 32 * (c % 4) + 16 * gl
                o[:, r0:r0 + 16, c, gl, :] = src[:, g].transpose(0, 2, 1)
        p[nm] = o.reshape(2, 128, 8, 128)
    for nm, src in (("s5_cre", inp["s5_c_re"]), ("s5_cim", inp["s5_c_im"])):
        o = np.zeros((2, 2, 64, 8, 2, 16), f)
        for c in range(8):
            for gl in range(2):
                o[:, gl, :, c, gl, :] = src[:, 2 * c + gl].transpose(0, 2, 1)
        p[nm] = o.reshape(2, 128, 8, 32)
    p["s5_dfm"] = np.ascontiguousarray(inp["s5_d"].reshape(2, 2, 128).transpose(0, 2, 1))
    p["glu_w"] = np.ascontiguousarray(inp["s5_glu_w"])
    p["glu_b"] = np.ascontiguousarray(inp["s5_glu_b"])
    return p


def _core_inputs(inp, shared, consts, c):
    m = dict(shared)
    for k, v in consts.items():
        m["c_" + k] = v
    b0 = c * SB
    m["xp"] = np.ascontiguousarray(inp["x_prompt"][c])
    m["xs"] = np.ascontiguousarray(inp["x_sample"][b0:b0 + SB].reshape(128, D))
    m["st_ret"] = np.ascontiguousarray(inp["state_ret"][:, b0:b0 + SB])
    m["st_hg"] = np.ascontiguousarray(inp["state_hgrn"][:, b0:b0 + SB])
    m["st_ssd"] = np.ascontiguousarray(inp["state_ssd"][:, b0:b0 + SB])
    m["st_conv"] = np.ascontiguousarray(inp["state_ssd_conv"][:, b0:b0 + SB].reshape(2, SB * 3, 768))
    m["st_s5re"] = np.ascontiguousarray(inp["state_s5_re"][:, b0:b0 + SB].reshape(2, SB, 1024))
    m["st_s5im"] = np.ascontiguousarray(inp["state_s5_im"][:, b0:b0 + SB].reshape(2, SB, 1024))
    return m


def _assemble(res):
    f = np.float32
    yp = np.stack([r["yp"] for r in res], 0).astype(f)
    ys = np.concatenate([r["ys"].reshape(SB, SL, D) for r in res], 0).astype(f)

    def pst(k, shp):
        return np.stack([r[k].reshape(shp) for r in res], 1).astype(f)

    def sst(k, shp):
        return np.concatenate([r[k].reshape(shp) for r in res], 1).astype(f)

    return (yp, ys,
            pst("p_ret", (2, 4, 64, 64)), pst("p_hg", (2, 4, 64, 64)), pst("p_ssd", (2, 4, 128, 64)),
            pst("p_conv", (2, 3, 768)), pst("p_s5re", (2, 16, 64)), pst("p_s5im", (2, 16, 64)),
            sst("s_ret", (2, SB, 4, 64, 64)), sst("s_hg", (2, SB, 4, 64, 64)), sst("s_ssd", (2, SB, 4, 128, 64)),
            sst("s_conv", (2, SB, 3, 768)), sst("s_s5re", (2, SB, 16, 64)), sst("s_s5im", (2, SB, 16, 64)))


def kernel(**inputs):
    global _CONSTS
    if _CONSTS is None:
        _CONSTS = _const_tables()
    inp = {k: np.asarray(v, dtype=np.float32) for k, v in inputs.items()}
    shared = _host_params(inp)
    nc, _ = build_program()
    in_maps = [_core_inputs(inp, shared, _CONSTS, c) for c in range(NCORE)]
    res = run_bass_kernel_spmd(nc, in_maps, core_ids=list(range(NCORE)))
    return _assemble(res.results)
```
